# Optimizing a Trainium2 kernel written in Bass

```python
import jax, jax.numpy as jnp
from jax import lax
import numpy as np

D_MODEL = 1024
BATCH = 2
SEQ = 8192
DEPTH = 1

PLE_DIM = 256
D_FF = 2816
HG_HEADS = 8
HG_DK = 128
HG_DV = 128
HG_WIDTH = HG_HEADS * HG_DK
HG_VWIDTH = HG_HEADS * HG_DV
CHUNK = 64
POOL_WINDOWS = (2, 4, 8, 16)
POOL_GROUPS = 4
POOL_CH = 128
POOL_WIDTH = POOL_GROUPS * POOL_CH
IN_SIZES = (HG_WIDTH, HG_WIDTH, HG_VWIDTH, HG_VWIDTH, POOL_WIDTH, D_MODEL, D_MODEL)
IN_COLS = HG_WIDTH * 2 + HG_VWIDTH * 2 + POOL_WIDTH + 2 * D_MODEL
EPS = 1e-6

kernel_name = "hybrid_hgrn2_pool_macaron_block"


def _rmsnorm(x, g):
    xf = x.astype(jnp.float32)
    y = xf * lax.rsqrt(jnp.mean(xf * xf, axis=-1, keepdims=True) + EPS)
    return (y * g.astype(jnp.float32)).astype(x.dtype)


def _swiglu(h, w1, w3, w2):
    return (jax.nn.silu(h @ w1) * (h @ w3)) @ w2


def _hgrn2_chunked(q, k, v, log_f):
    B, S, H, DK = q.shape
    DV = v.shape[-1]
    n_chunks = S // CHUNK

    def to_chunks(t):
        return t.reshape(B, n_chunks, CHUNK, H, t.shape[-1]).transpose(1, 0, 3, 2, 4)

    qc, kc, vc, gc = to_chunks(q), to_chunks(k), to_chunks(v), to_chunks(log_f)
    causal = jnp.tril(jnp.ones((CHUNK, CHUNK), dtype=bool))[:, :, None]

    def step(state, inp):
        qb, kb, vb, gb = inp
        G = jnp.cumsum(gb, axis=2)
        diff = G[:, :, :, None, :] - G[:, :, None, :, :]
        decay = jnp.exp(jnp.where(causal, diff, -jnp.inf))
        scores = jnp.einsum('bhtk,bhsk,bhtsk->bhts', qb, kb, decay)
        o_intra = jnp.einsum('bhts,bhsv->bhtv', scores, vb)
        o_inter = jnp.einsum('bhtk,bhkv->bhtv', qb * jnp.exp(G), state)
        G_last = G[:, :, -1:, :]
        k_dec = kb * jnp.exp(G_last - G)
        new_state = (jnp.exp(G_last[:, :, 0, :])[..., None] * state
                     + jnp.einsum('bhsk,bhsv->bhkv', k_dec, vb))
        return new_state, o_intra + o_inter

    state0 = jnp.zeros((B, H, DK, DV), jnp.float32)
    _, oc = lax.scan(step, state0, (qc, kc, vc, gc))
    return oc.transpose(1, 0, 3, 2, 4).reshape(B, S, H, DV)


def _causal_multiscale_pool(u):
    B, S, G, C = u.shape
    uf = u.astype(jnp.float32)
    cs = jnp.concatenate([jnp.zeros((B, 1, G, C), jnp.float32), jnp.cumsum(uf, axis=1)], axis=1)
    pos = jnp.arange(1, S + 1, dtype=jnp.float32)
    outs = []
    for g, w in enumerate(POOL_WINDOWS):
        csg = cs[:, :, g]
        upper = csg[:, 1:]
        lower = jnp.concatenate([jnp.zeros((B, w - 1, C), jnp.float32), csg[:, :S - w + 1]], axis=1)
        count = jnp.minimum(pos, float(w))[None, :, None]
        outs.append((upper - lower) / count - uf[:, :, g])
    return jnp.stack(outs, axis=2).astype(u.dtype)


def _normal(key, shape, fan_in):
    return jax.random.normal(key, shape, jnp.float32) * (fan_in ** -0.5)


def _gain(key, shape):
    return 1.0 + 0.02 * jax.random.normal(key, shape, jnp.float32)


def setup_inputs(seed: int = 0) -> dict:
    key = jax.random.key(seed)
    ks = jax.random.split(key, 26)
    L = DEPTH
    return {
        "x": jax.random.normal(ks[0], (BATCH, SEQ, D_MODEL), jnp.float32),
        "p": jax.random.normal(ks[1], (DEPTH, BATCH, SEQ, PLE_DIM), jnp.float32),
        "ffn1_norm": _gain(ks[2], (L, D_MODEL)),
        "ffn1_w1": _normal(ks[3], (L, D_MODEL, D_FF), D_MODEL),
        "ffn1_w3": _normal(ks[4], (L, D_MODEL, D_FF), D_MODEL),
        "ffn1_w2": _normal(ks[5], (L, D_FF, D_MODEL), D_FF),
        "mix_norm": _gain(ks[6], (L, D_MODEL)),
        "w_in": _normal(ks[7], (L, D_MODEL, IN_COLS), D_MODEL),
        "hgrn_lb": 0.1 * jax.random.normal(ks[8], (L + 1, HG_WIDTH), jnp.float32),
        "hgrn_onorm": _gain(ks[9], (L, HG_VWIDTH)),
        "w_branch_a": _normal(ks[10], (L, HG_VWIDTH, D_MODEL), HG_VWIDTH),
        "pool_w": _normal(ks[11], (L, POOL_GROUPS, POOL_CH, POOL_CH), POOL_CH),
        "pool_scale": _gain(ks[12], (L, POOL_WIDTH)),
        "w_branch_b": _normal(ks[13], (L, POOL_WIDTH, D_MODEL), POOL_WIDTH),
        "w_out": _normal(ks[14], (L, D_MODEL, D_MODEL), D_MODEL),
        "ffn2_norm": _gain(ks[15], (L, D_MODEL)),
        "ffn2_w1": _normal(ks[16], (L, D_MODEL, D_FF), D_MODEL),
        "ffn2_w3": _normal(ks[17], (L, D_MODEL, D_FF), D_MODEL),
        "ffn2_w2": _normal(ks[18], (L, D_FF, D_MODEL), D_FF),
        "ple_norm": _gain(ks[19], (L, D_MODEL)),
        "ple_w_gate": _normal(ks[20], (L, D_MODEL, D_MODEL), D_MODEL),
        "ple_w_proj": _normal(ks[21], (L, PLE_DIM, D_MODEL), PLE_DIM),
        "ple_post_norm": _gain(ks[22], (L, D_MODEL)),
        "final_norm": _gain(ks[23], (D_MODEL,)),
    }


def reference(x, p, ffn1_norm, ffn1_w1, ffn1_w3, ffn1_w2, mix_norm, w_in, hgrn_lb, hgrn_onorm,
              w_branch_a, pool_w, pool_scale, w_branch_b, w_out, ffn2_norm, ffn2_w1, ffn2_w3,
              ffn2_w2, ple_norm, ple_w_gate, ple_w_proj, ple_post_norm, final_norm):
    B, S, _ = x.shape
    split_points = [int(v) for v in np.cumsum(IN_SIZES)[:-1]]
    lb_all = jnp.cumsum(jax.nn.softmax(hgrn_lb.astype(jnp.float32), axis=0), axis=0)

    for i in range(DEPTH):
        h = _rmsnorm(x, ffn1_norm[i])
        x = x + 0.5 * _swiglu(h, ffn1_w1[i], ffn1_w3[i], ffn1_w2[i])

        h = _rmsnorm(x, mix_norm[i])
        proj = h @ w_in[i]
        q_r, f_r, i_r, og_r, pool_r, ga_r, gb_r = jnp.split(proj, split_points, axis=-1)

        lb = lb_all[i]
        f = lb + (1.0 - lb) * jax.nn.sigmoid(f_r.astype(jnp.float32))
        log_f = jnp.log(f).reshape(B, S, HG_HEADS, HG_DK)
        k = (1.0 - f).reshape(B, S, HG_HEADS, HG_DK)
        q = jax.nn.silu(q_r.astype(jnp.float32)).reshape(B, S, HG_HEADS, HG_DK)
        v = i_r.astype(jnp.float32).reshape(B, S, HG_HEADS, HG_DV)
        o = _hgrn2_chunked(q, k, v, log_f).astype(x.dtype)
        o = _rmsnorm(o, hgrn_onorm[i].reshape(HG_HEADS, HG_DV)) * jax.nn.silu(og_r.reshape(B, S, HG_HEADS, HG_DV))
        y_a = o.reshape(B, S, HG_VWIDTH) @ w_branch_a[i]

        u = pool_r.reshape(B, S, POOL_GROUPS, POOL_CH)
        pooled = _causal_multiscale_pool(u)
        mixed = jnp.einsum('bsgc,gcd->bsgd', pooled, pool_w[i]).reshape(B, S, POOL_WIDTH) * pool_scale[i]
        y_b = mixed @ w_branch_b[i]

        y = jax.nn.sigmoid(ga_r) * y_a + jax.nn.sigmoid(gb_r) * y_b
        x = x + y @ w_out[i]

        h = _rmsnorm(x, ffn2_norm[i])
        x = x + 0.5 * _swiglu(h, ffn2_w1[i], ffn2_w3[i], ffn2_w2[i])

        gate = jax.nn.sigmoid(_rmsnorm(x, ple_norm[i]) @ ple_w_gate[i])
        e = _rmsnorm(p[i] @ ple_w_proj[i], ple_post_norm[i])
        x = x + gate * e

    return _rmsnorm(x, final_norm)
```

```python
import numpy as np
from contextlib import ExitStack
import concourse.bass as bass
import concourse.mybir as mybir
from concourse.bass_utils import run_bass_kernel_spmd

F32 = mybir.dt.float32
BF16 = mybir.dt.bfloat16
AF = mybir.ActivationFunctionType
ALU = mybir.AluOpType

D = 1024
NT = 2048
HALO = 16
TT = NT + HALO
DFF = 2816
NFC = DFF // 128
NCORES = 8
EPS = 1e-6
NSLOT = 5
SLOTW = 2048
RW = 21504
CROSS_CORE = True
CC_VARIANT = "full"

V_FFN1, V_MIX, V_FFN2, V_PLE, V_POST, V_FINAL, V_LB0, V_LB1, V_ONORM, V_PSCALE = 0, 8, 16, 24, 32, 40, 48, 56, 64, 72
NV = 76

BLK = [(0, 512), (512, 512), (1024, 512), (1536, 512)]
BLKH = BLK + [(NT, HALO)]


class Sched:
    ENG = ("pe", "act", "dve", "pool", "sp")

    def __init__(self, nc, stack, n_dma_sems=12, n_eng_sems=40, epoch=12000, same_engine_sync=True):
        self.nc = nc
        self.same = same_engine_sync
        self.epoch = epoch
        self.lists = {k: [] for k in self.ENG}
        self.free_sems = [stack.enter_context(nc.semaphore(f"es{i}")) for i in range(n_eng_sems)]
        self.dsems = {"sp": [stack.enter_context(nc.semaphore(f"dsp{i}")) for i in range(8)],
                      "pool": [stack.enter_context(nc.semaphore(f"dpa{i}")) for i in range(n_dma_sems)]}
        self.dsems_b = {"sp": [stack.enter_context(nc.semaphore(f"dsq{i}")) for i in range(8)],
                        "pool": [stack.enter_context(nc.semaphore(f"dpb{i}")) for i in range(n_dma_sems)]}
        self.dval = {q: [0] * len(v) for q, v in self.dsems.items()}
        self.drr = {q: 0 for q in self.dsems}
        self.cur = {k: [self.free_sems.pop(), 0] for k in self.ENG}
        self.seen = {k: {} for k in self.ENG}
        self.last_w = {}
        self.rd = {}

    def _need(self, eng, tok, waits):
        sem, val, src = tok
        if src == eng and not (self.same and eng != "pe"):
            return
        key = id(sem)
        if self.seen[eng].get(key, 0) >= val:
            return
        self.seen[eng][key] = val
        waits.append((sem, val))

    def _deps(self, eng, reads, writes):
        waits = []
        for k in reads:
            t = self.last_w.get(k)
            if t is not None:
                self._need(eng, t, waits)
        for k in writes:
            t = self.last_w.get(k)
            if t is not None:
                self._need(eng, t, waits)
            for t in self.rd.get(k, ()):
                self._need(eng, t, waits)
        return waits

    def _commit(self, tok, reads, writes):
        for k in reads:
            lst = self.rd.setdefault(k, [])
            lst.append(tok)
            if len(lst) > 16:
                best = {}
                for t in lst:
                    kk = id(t[0])
                    if kk not in best or best[kk][1] < t[1]:
                        best[kk] = t
                self.rd[k] = list(best.values())
        for k in writes:
            self.last_w[k] = tok
            self.rd[k] = []

    def op(self, eng, fn, reads=(), writes=()):
        waits = self._deps(eng, reads, writes)
        cur = self.cur[eng]
        if cur[1] >= self.epoch:
            cur = self.cur[eng] = [self.free_sems.pop(), 0]
        cur[1] += 1
        sem, val = cur[0], cur[1]
        lst = self.lists[eng]
        for (s, v) in waits:
            lst.append(lambda e, s=s, v=v: e.wait_ge(s, v))
        lst.append(lambda e, fn=fn, sem=sem: fn(e).then_inc(sem, 1))
        tok = (sem, val, eng)
        self._commit(tok, reads, writes)
        return tok

    def dma(self, q, out, in_, reads=(), writes=()):
        waits = self._deps(q, reads, writes)
        j = self.drr[q]
        self.drr[q] = (j + 1) % len(self.dsems[q])
        sem = self.dsems[q][j]
        if self.dval[q][j] > 0:
            self._need(q, (sem, self.dval[q][j], None), waits)
        self.dval[q][j] += 16
        val = self.dval[q][j]
        lst = self.lists[q]
        for (s, v) in waits:
            lst.append(lambda e, s=s, v=v: e.wait_ge(s, v))
        lst.append(lambda e, out=out, in_=in_, sem=sem: e.dma_start(out=out, in_=in_).then_inc(sem, 16))
        tok = (sem, val, None)
        self._commit(tok, reads, writes)
        return tok

    def collective_on_pool(self, issue_fn, cc_sem, markers, post_fn):
        all_toks = []
        for q in self.dsems:
            for j, sem in enumerate(self.dsems[q]):
                if self.dval[q][j] > 0:
                    all_toks.append((sem, self.dval[q][j], None))
        for eng in self.ENG:
            waits = []
            for t in all_toks:
                self._need(eng, t, waits)
            for (s, v) in waits:
                self.lists[eng].append(lambda e, s=s, v=v: e.wait_ge(s, v))
        mk = [self.op(eng, fn, writes=[("marker", eng)]) for eng, fn in markers.items()]
        waits = []
        for t in mk:
            self._need("pool", t, waits)
        lst = self.lists["pool"]
        for (s, v) in waits:
            lst.append(lambda e, s=s, v=v: e.wait_ge(s, v))
        lst.append(lambda e: issue_fn(e).then_inc(cc_sem))
        lst.append(lambda e: e.wait_ge(cc_sem, 1))
        after = self.op("pool", post_fn, writes=[("marker", "pool")])
        self.wait_tok("sp", after)
        return after

    def fence(self):
        toks = [(self.cur[d][0], self.cur[d][1], d) for d in ("pe", "act", "dve") if self.cur[d][1] > 0]
        for eng in self.ENG:
            waits = []
            for t in toks:
                if t[2] != eng:
                    self._need(eng, t, waits)
            for (s, v) in waits:
                self.lists[eng].append(lambda e, s=s, v=v: e.wait_ge(s, v))

    def wait_tok(self, eng, tok):
        waits = []
        self._need(eng, tok, waits)
        for (s, v) in waits:
            self.lists[eng].append(lambda e, s=s, v=v: e.wait_ge(s, v))

    def emit(self):
        nc = self.nc
        L = self.lists
        with nc.Block() as block:
            @block.tensor
            def _(e):
                for th in L["pe"]:
                    th(e)

            @block.scalar
            def _(e):
                for th in L["act"]:
                    th(e)

            @block.vector
            def _(e):
                for th in L["dve"]:
                    th(e)

            @block.gpsimd
            def _(e):
                for th in L["pool"]:
                    th(e)

            @block.sync
            def _(e):
                for th in L["sp"]:
                    th(e)


def build_program(dbg=None):
    nc = bass.Bass("TRN2", target_bir_lowering=False)

    def din(name, shape, dt=F32):
        return nc.dram_tensor(name, shape, dt, kind="ExternalInput").ap()

    xT_d = din("xT", [D, TT])
    pT_d = din("pT", [256, NT])
    vec_d = din("vecs", [128, NV])
    cmask_d = din("cmask", [64, 64])
    invc_d = din("invc", [128, 64])
    smask_d = din("smask", [128, 512])
    W = {
        "ffn1_w1": din("ffn1_w1", [D, DFF]), "ffn1_w3": din("ffn1_w3", [D, DFF]), "ffn1_w2": din("ffn1_w2", [DFF, D]),
        "ffn2_w1": din("ffn2_w1", [D, DFF]), "ffn2_w3": din("ffn2_w3", [D, DFF]), "ffn2_w2": din("ffn2_w2", [DFF, D]),
        "w_in": din("w_in", [D, 6656]), "w_branch_a": din("w_branch_a", [D, D]), "w_branch_b": din("w_branch_b", [512, D]),
        "pool_w": din("pool_w", [512, 128]), "w_out": din("w_out", [D, D]),
        "ple_w_gate": din("ple_w_gate", [D, D]), "ple_w_proj": din("ple_w_proj", [256, D]),
    }
    outT_d = nc.dram_tensor("outT", [D, NT], F32, kind="ExternalOutput").ap()
    if CROSS_CORE:
        st_loc = nc.dram_tensor("st_loc", [128, 8 * 128 + 8], F32, kind="Internal").ap()
        st_all = nc.dram_tensor("st_all", [4 * 128, 8 * 128 + 8], F32, kind="Internal").ap()
        sel_d = din("sel", [128, 4])
        NIT = 8 * (NT // 512)
        sp_kdT = nc.dram_tensor("sp_kdT", [NIT, 64, 8 * 128], BF16, kind="Internal").ap()
        sp_vt = nc.dram_tensor("sp_vt", [NIT, 64, 8 * 128], BF16, kind="Internal").ap()
        sp_kt = nc.dram_tensor("sp_kt", [NIT, 128, 512], BF16, kind="Internal").ap()
        sp_P = nc.dram_tensor("sp_P", [NIT, 128, 512], F32, kind="Internal").ap()
        sp_Pl = nc.dram_tensor("sp_Pl", [NIT, 128, 8], F32, kind="Internal").ap()

    with ExitStack() as st:
        def sb(name, shape, dt):
            return st.enter_context(nc.sbuf_tensor(name, shape, dt))

        xT = sb("xTs", [128, 8, TT], F32)
        hb = sb("hb", [128, 8, TT], BF16)
        wbuf = sb("wbuf", [128, NSLOT * SLOTW // 2], F32)
        R = sb("R", [128, RW], F32)
        vec = sb("vec", [128, NV], F32)
        lbv = sb("lbv", [128, 32], F32)
        ident = sb("identb", [128, 128], BF16)
        ones = sb("ones", [128, 128], BF16)
        cmask = sb("cmasks", [64, 64], F32)
        invc = sb("invcs", [128, 64], F32)
        Sst = sb("Sst", [128, 8, 128], F32)
        Sbf = sb("Sbf", [128, 4, 128], BF16)
        Ptot = sb("Ptot", [128, 8], F32)
        mark = sb("mark", [128, 2], F32)
        mark2 = sb("mark2", [128, 2], F32)
        pbank = [st.enter_context(nc.psum_tensor(f"pb{i}", [128, 512], F32)) for i in range(8)]

        S = Sched(nc, st)
        state = {"bank": 0, "slot": 0, "tmp": 0}

        bank_groups = {"A": [0, 1, 2, 3], "C": [4], "A0": [4]}
        grp_pos = {"A": 0, "C": 0, "A0": 0}
        cur_grp = [None]

        def bank():
            g = cur_grp[0]
            if g is None:
                i = state["bank"]
                state["bank"] = (i + 1) % 7
            else:
                lst = bank_groups[g]
                i = lst[grp_pos[g] % len(lst)]
                grp_pos[g] += 1
            return pbank[i], ("pb", i)

        def hold_bank():
            return pbank[7], ("pb", 7)

        class Arena:
            def __init__(self):
                self.off = 0

            def f32(self, words):
                a = self.off
                self.off += words
                assert self.off <= RW, self.off
                return R[:, a:a + words]

            def bf16(self, elems):
                words = (elems + 1) // 2
                a = self.off
                self.off += words
                assert self.off <= RW, self.off
                return R[:, a:a + words].bitcast(BF16)

        SLOT32 = SLOTW // 2

        def wload(dram_ap, kc, ncols):
            i = state["slot"]
            state["slot"] = (i + 1) % NSLOT
            n = kc * ncols
            assert n <= SLOT32, n
            raw = wbuf[:, i * SLOT32:i * SLOT32 + n]
            S.dma("sp", raw.rearrange("p (k n) -> p k n", k=kc), dram_ap, writes=[("w", i)])
            half = wbuf[:, i * SLOT32:i * SLOT32 + (n + 1) // 2].bitcast(BF16)[:, 0:n]
            S.op("pool", lambda e, half=half, raw=raw: e.tensor_copy(out=half, in_=raw), reads=[("w", i)], writes=[("w", i)])
            return half.rearrange("p (k n) -> p k n", k=kc), ("w", i)

        def wtile(name, r0, nk, c0, ncols):
            ap = W[name].rearrange("(k p) n -> p k n", p=128)[:, r0:r0 + nk, c0:c0 + ncols]
            return wload(ap, nk, ncols)

        for (off_, n_) in [(NT, HALO)] + BLK:
            b_ = 4 if off_ >= NT else off_ // 512
            S.dma("sp", xT[:, :, off_:off_ + n_], xT_d.rearrange("(c p) t -> p c t", p=128)[:, :, off_:off_ + n_],
                  writes=[("x", c, b_) for c in range(8)])
        S.dma("sp", vec[:], vec_d, writes=["vec"])
        S.dma("sp", cmask[:], cmask_d, writes=["cmask"])
        S.dma("sp", invc[:], invc_d, writes=["invc"])
        S.op("pool", lambda e: e.memset(ident[:], 0.0), writes=["ident"])
        S.op("pool", lambda e: e.affine_select(out=ident[:], in_=ident[:], pattern=[[-1, 128]], compare_op=ALU.not_equal, fill=1.0,
                                               base=0, channel_multiplier=1), reads=["ident"], writes=["ident"])
        S.op("dve", lambda e: e.memset(ones[:], 1.0), writes=["ones"])
        S.op("dve", lambda e: e.tensor_tensor(out=lbv[:, 0:8], in0=vec[:, V_LB0:V_LB0 + 8], in1=vec[:, V_LB1:V_LB1 + 8], op=ALU.subtract),
             reads=["vec"], writes=["lbv"])
        S.op("act", lambda e: e.activation(out=lbv[:, 0:8], in_=lbv[:, 0:8], func=AF.Sigmoid), reads=["lbv"], writes=["lbv"])
        S.op("dve", lambda e: e.tensor_scalar(out=lbv[:, 8:16], in0=lbv[:, 0:8], scalar1=-1.0, scalar2=1.0, op0=ALU.mult, op1=ALU.add),
             reads=["lbv"], writes=["lbv"])

        S.op("dve", lambda e: e.tensor_scalar(out=lbv[:, 16:24], in0=lbv[:, 8:16], scalar1=0.5, scalar2=None, op0=ALU.mult), reads=["lbv"], writes=["lbv"])
        S.op("dve", lambda e: e.tensor_tensor(out=lbv[:, 24:32], in0=lbv[:, 0:8], in1=lbv[:, 16:24], op=ALU.add), reads=["lbv"], writes=["lbv"])

        def blk_idx(off):
            return 4 if off >= NT else off // 512

        def rstd_from_psum(ps, psk, n, dst, dstk, inv_d):
            S.op("dve", lambda e: e.tensor_scalar(out=dst[:, 0:n], in0=ps[:, 0:n], scalar1=inv_d, scalar2=EPS, op0=ALU.mult, op1=ALU.add),
                 reads=[psk], writes=[dstk])
            S.op("act", lambda e: e.activation(out=dst[:, 0:n], in_=dst[:, 0:n], func=AF.Ln), reads=[dstk], writes=[dstk])
            S.op("act", lambda e: e.activation(out=dst[:, 0:n], in_=dst[:, 0:n], func=AF.Exp, scale=-0.5), reads=[dstk], writes=[dstk])

        def norm_to_hb(blocks, gcol, sq, rstd):
            for (off, n) in blocks:
                b = blk_idx(off)
                ps, psk = bank()
                nsq = sq.shape[1]
                for c in range(8):
                    S.op("act", lambda e, c=c, off=off, n=n: e.activation(out=sq[:, c % nsq, 0:n], in_=xT[:, c, off:off + n], func=AF.Square),
                         reads=[("x", c, b)], writes=[("sq", c % nsq)])
                for c in range(8):
                    S.op("pe", lambda e, c=c, n=n, ps=ps: e.matmul(ps[:, 0:n], lhsT=ones[:], rhs=sq[:, c % nsq, 0:n], start=(c == 0), stop=(c == 7)),
                         reads=["ones", ("sq", c % nsq)], writes=[psk])
                rstd_from_psum(ps, psk, n, rstd, "rstd", 1.0 / D)
                for c in range(8):
                    S.op("dve", lambda e, c=c, off=off, n=n: e.scalar_tensor_tensor(
                        out=hb[:, c, off:off + n], in0=xT[:, c, off:off + n], scalar=vec[:, gcol + c:gcol + c + 1], in1=rstd[:, 0:n],
                        op0=ALU.mult, op1=ALU.mult), reads=[("x", c, b), "vec", "rstd"], writes=[("hb", c, b)])

        def proj(wt, wk, nk, col0, rhs_fn, rhs_keys, n, M=128):
            ps, psk = bank()
            segs = list(zip(wt, wk, nk)) if isinstance(wt, list) else [(wt, wk, nk)]
            tot = sum(sg_[2] for sg_ in segs)

            def f(e):
                kk = 0
                for (t_, _k, n_) in segs:
                    for k in range(n_):
                        i = e.matmul(ps[0:M, 0:n], lhsT=t_[:, k, col0:col0 + M], rhs=rhs_fn(kk), start=(kk == 0), stop=(kk == tot - 1))
                        kk += 1
                return i
            S.op("pe", f, reads=[sg_[1] for sg_ in segs] + rhs_keys, writes=[psk])
            return ps, psk

        def hb_rhs(off, n):
            return lambda k: hb[:, k, off:off + n]

        def hb_keys(off):
            b = blk_idx(off)
            return [("hb", c, b) for c in range(8)]

        def ffn(w1n, w3n, w2n, gcol, blocks):
            ar = Arena()
            aT = ar.bf16(11 * TT).rearrange("p (j t) -> p j t", j=11)
            sq = ar.bf16(8 * 512).rearrange("p (j t) -> p j t", j=8)
            rstd = ar.f32(512)
            stmp = ar.f32(2 * 512).rearrange("p (j t) -> p j t", j=2)
            norm_to_hb(blocks, gcol, sq, rstd)
            for half in range(2):
                for j in range(11):
                    n = half * 11 + j
                    w1t, k1 = wtile(w1n, 0, 8, n * 128, 128)
                    w3t, k3 = wtile(w3n, 0, 8, n * 128, 128)
                    for (off, nn) in blocks:
                        b = blk_idx(off)
                        ps1, pk1 = proj(w1t, k1, 8, 0, hb_rhs(off, nn), hb_keys(off), nn)
                        ps3, pk3 = proj(w3t, k3, 8, 0, hb_rhs(off, nn), hb_keys(off), nn)
                        ti = state["tmp"] = (state["tmp"] + 1) % 2
                        S.op("act", lambda e, ps1=ps1, nn=nn, ti=ti: e.activation(out=stmp[:, ti, 0:nn], in_=ps1[:, 0:nn], func=AF.Silu),
                             reads=[pk1], writes=[("stmp", ti)])
                        S.op("dve", lambda e, ps3=ps3, nn=nn, ti=ti, j=j, off=off: e.tensor_tensor(
                            out=aT[:, j, off:off + nn], in0=ps3[:, 0:nn], in1=stmp[:, ti, 0:nn], op=ALU.mult),
                            reads=[pk3, ("stmp", ti)], writes=[("aT", j, b)])
                for m in range(8):
                    w2a, k2a = wtile(w2n, half * 11, 6, m * 128, 128)
                    w2b, k2b = wtile(w2n, half * 11 + 6, 5, m * 128, 128)
                    for (off, nn) in blocks:
                        b = blk_idx(off)
                        ps, pk = proj([w2a, w2b], [k2a, k2b], [6, 5], 0, lambda k, off=off, nn=nn: aT[:, k, off:off + nn], [("aT", j, b) for j in range(11)], nn)
                        S.op("dve", lambda e, ps=ps, nn=nn, m=m, off=off: e.scalar_tensor_tensor(
                            out=xT[:, m, off:off + nn], in0=ps[:, 0:nn], scalar=0.5, in1=xT[:, m, off:off + nn], op0=ALU.mult, op1=ALU.add),
                            reads=[pk, ("x", m, b)], writes=[("x", m, b)])

        def finish_with_x():
            toks = []
            for (off, nn) in BLK:
                b = off // 512
                toks.append(S.dma("sp", outT_d.rearrange("(c p) t -> p c t", p=128)[:, :, off:off + 512], xT[:, :, off:off + 512],
                                  reads=[("x", c, b) for c in range(8)]))
            for t in toks:
                S.wait_tok("sp", t)
            S.emit()
            return nc

        ffn("ffn1_w1", "ffn1_w3", "ffn1_w2", V_FFN1, BLKH)
        S.fence()
        if dbg == "ffn1":
            return finish_with_x()

        ar = Arena()
        ob = ar.bf16(8 * NT).rearrange("p (h t) -> p h t", h=8)
        mixed = ar.bf16(4 * NT).rearrange("p (g t) -> p g t", g=4)
        work0 = ar.off

        HT = 512
        NCH = HT // 64
        NB = HT // 512
        NPART = NT // HT

        def run_interleaved(gens):
            gens = [g for g in gens if g is not None and g[1] is not None]
            while gens:
                for g in list(gens):
                    cur_grp[0] = g[0]
                    try:
                        next(g[1])
                    except StopIteration:
                        gens.remove(g)
            cur_grp[0] = None

        def hg_alloc(state_only, ar0):
            ar.off = ar0
            B = {}
            B["fbuf"] = ar.f32(HT)
            B["Pbuf"] = ar.f32(HT)
            B["rP"] = ar.f32(HT)
            B["tmpA"] = ar.f32(512)
            B["kd"] = ar.bf16(HT)
            B["kt"] = ar.bf16(HT)
            B["sets"] = []
            for i in range(2):
                d = {"kdT": ar.bf16(NCH * 128).rearrange("p (c k) -> p c k", c=NCH),
                     "vt": ar.bf16(NCH * 128).rearrange("p (c k) -> p c k", c=NCH),
                     "Pl": ar.f32(NCH)}
                if not state_only:
                    d["qt"] = ar.bf16(HT)
                    d["At"] = ar.bf16(NCH * 64).rearrange("p (c k) -> p c k", c=NCH)
                B["sets"].append(d)
            if not state_only:
                B["oh"] = [ar.f32(HT) for _ in range(2)]
                B["gs"] = [ar.bf16(HT) for _ in range(3)]
                B["rst"] = ar.f32(512)
                B["sqh"] = ar.bf16(512)
            return B

        def hg_A2(h, hh, wts, B, si, gi, item):
            t0 = hh * HT
            wq, kq, wf, kf, wi, ki, wog, kog = wts
            Pbuf, tmpA, kt = B["Pbuf"], B["tmpA"], B["kt"]
            D_ = B["sets"][si]
            kdT, vt, Pl, qt, At, gs = D_["kdT"], D_["vt"], D_["Pl"], D_["qt"], D_["At"], B["gs"][gi]
            qs = tmpA
            off = t0
            S.dma("sp", Pbuf[:, :], sp_P[item], writes=["P"])
            S.dma("sp", kt[:, :], sp_kt[item], writes=["kt"])
            S.dma("sp", kdT[0:64, :, :], sp_kdT[item].rearrange("p (c k) -> p c k", c=NCH), writes=[("kdT", si)])
            S.dma("sp", vt[0:64, :, :], sp_vt[item].rearrange("p (c k) -> p c k", c=NCH), writes=[("vt", si)])
            S.dma("sp", Pl[:, :], sp_Pl[item], writes=[("Pl", si)])
            yield
            psq, pkq = proj(wq, kq, 8, 0, hb_rhs(off, 512), hb_keys(off), 512)
            yield
            psg, pkg = proj(wog, kog, 8, 0, hb_rhs(off, 512), hb_keys(off), 512)
            yield
            S.op("act", lambda e: e.activation(out=qs[:, :], in_=psq[:, :], func=AF.Silu), reads=[pkq], writes=["tmpA"])
            yield
            S.op("act", lambda e: e.activation(out=gs[:, :], in_=psg[:, :], func=AF.Silu), reads=[pkg], writes=[("gs", gi)])
            yield
            S.op("dve", lambda e: e.tensor_tensor(out=qt[:, :], in0=qs[:, :], in1=Pbuf[:, :], op=ALU.mult), reads=["tmpA", "P"], writes=[("qt", si)])
            yield
            ps, pk = bank()

            def fs(e, ps=ps):
                for j in range(8):
                    i = e.matmul(ps[0:64, j * 64:(j + 1) * 64], lhsT=kt[:, j * 64:(j + 1) * 64], rhs=qt[:, j * 64:(j + 1) * 64], start=True, stop=True)
                return i
            S.op("pe", fs, reads=["kt", ("qt", si)], writes=[pk])
            yield
            S.op("dve", lambda e, ps=ps: e.tensor_tensor(out=At[0:64, :, :], in0=ps[0:64, :].rearrange("p (c t) -> p c t", c=8),
                                                         in1=cmask[:, None, :].to_broadcast([64, 8, 64]), op=ALU.mult),
                 reads=[pk, "cmask"], writes=[("At", si)])
            yield

        def hg_A(h, hh, wts, state_only, B, si, gi=0, item=0):
            t0 = hh * HT
            wq, kq, wf, kf, wi, ki, wog, kog = wts
            fbuf, Pbuf, rP, tmpA, kd = B["fbuf"], B["Pbuf"], B["rP"], B["tmpA"], B["kd"]
            gbuf = rP
            qs = tmpA
            D_ = B["sets"][si]
            kdT, vt, Pl = D_["kdT"], D_["vt"], D_["Pl"]
            off = t0
            psf, pkf = proj(wf, kf, 8, 0, hb_rhs(off, 512), hb_keys(off), 512)
            yield
            if not state_only:
                psq, pkq = proj(wq, kq, 8, 0, hb_rhs(off, 512), hb_keys(off), 512)
                yield
                psg, pkg = proj(wog, kog, 8, 0, hb_rhs(off, 512), hb_keys(off), 512)
                yield
            S.op("act", lambda e: e.activation(out=tmpA[:, :], in_=psf[:, :], func=AF.Tanh, scale=0.5), reads=[pkf], writes=["tmpA"])
            yield
            S.op("dve", lambda e: e.tensor_scalar(out=fbuf[:, :], in0=tmpA[:, :], scalar1=lbv[:, 16 + h:17 + h], scalar2=lbv[:, 24 + h:25 + h],
                                                  op0=ALU.mult, op1=ALU.add), reads=["tmpA", "lbv"], writes=["f"])
            yield
            if not state_only:
                qt, At, gs, kt = D_["qt"], D_["At"], B["gs"][gi], B["kt"]
                S.op("act", lambda e: e.activation(out=qs[:, :], in_=psq[:, :], func=AF.Silu), reads=[pkq, "f"], writes=["tmpA"])
                yield
                S.op("act", lambda e: e.activation(out=gs[:, :], in_=psg[:, :], func=AF.Silu), reads=[pkg], writes=[("gs", gi)])
                yield
            S.op("dve", lambda e: e.tensor_tensor(out=gbuf[:, :], in0=fbuf[:, :], in1=smask[:, 0:HT], op=ALU.mult), reads=["f", "smask"], writes=["rP"])
            yield
            S.op("dve", lambda e: e.tensor_tensor_scan(out=Pbuf[:, :], data0=fbuf[:, :], data1=gbuf[:, :], initial=1.0, op0=ALU.mult, op1=ALU.max),
                 reads=["f", "rP"], writes=["P"])
            yield
            S.op("act", lambda e: e.activation(out=rP[:, :], in_=Pbuf[:, :], func=AF.Ln), reads=["P"], writes=["rP"])
            yield
            S.op("act", lambda e: e.activation(out=rP[:, :], in_=rP[:, :], func=AF.Exp, scale=-1.0), reads=["rP"], writes=["rP"])
            yield
            S.op("dve", lambda e: e.tensor_copy(out=Pl[:, :], in_=Pbuf[:, :].rearrange("p (c t) -> p c t", c=NCH)[:, :, 63]), reads=["P"], writes=[("Pl", si)])
            yield
            if not state_only:
                S.op("dve", lambda e: e.tensor_tensor(out=qt[:, :], in0=qs[:, :], in1=Pbuf[:, :], op=ALU.mult), reads=["tmpA", "P"], writes=[("qt", si)])
                yield
            S.op("dve", lambda e: e.tensor_scalar(out=fbuf[:, :], in0=fbuf[:, :], scalar1=-1.0, scalar2=1.0, op0=ALU.mult, op1=ALU.add),
                 reads=["f", "P"], writes=["f"])
            yield
            S.op("dve", lambda e: e.tensor_tensor(out=rP[:, :], in0=rP[:, :], in1=fbuf[:, :], op=ALU.mult), reads=["rP", "f"], writes=["rP"])
            yield
            if not state_only or CROSS_CORE:
                kt = B["kt"]
                S.op("act", lambda e: e.activation(out=kt[:, :], in_=rP[:, :], func=AF.Copy), reads=["rP"], writes=["kt"])
                yield
            if state_only and CROSS_CORE:
                S.dma("sp", sp_kt[item], kt[:, :], reads=["kt"])
                S.dma("sp", sp_P[item], Pbuf[:, :], reads=["P"])
                S.dma("sp", sp_Pl[item], Pl[:, :], reads=[("Pl", si)])
                yield
            S.op("dve", lambda e: e.tensor_tensor(out=kd[:, :].rearrange("p (c t) -> p c t", c=NCH), in0=rP[:, :].rearrange("p (c t) -> p c t", c=NCH),
                                                   in1=Pbuf[:, :].rearrange("p (c t) -> p c t", c=NCH)[:, :, 63:64].to_broadcast([128, NCH, 64]), op=ALU.mult),
                 reads=["rP", "P"], writes=["kd"])
            yield
            for c4 in range(NCH // 4):
                ps, pk = bank()

                def fv(e, c4=c4, ps=ps):
                    for j in range(4):
                        c = c4 * 4 + j
                        o = t0 + c * 64
                        for k in range(8):
                            i = e.matmul(ps[0:64, j * 128:(j + 1) * 128], lhsT=hb[:, k, o:o + 64], rhs=wi[:, k, :], start=(k == 0), stop=(k == 7))
                    return i
                S.op("pe", fv, reads=[ki] + hb_keys(t0 + c4 * 256), writes=[pk])
                yield
                S.op("act", lambda e, c4=c4, ps=ps: e.activation(out=vt[0:64, c4 * 4:(c4 + 1) * 4, :], in_=ps[0:64, :].rearrange("p (c k) -> p c k", c=4), func=AF.Copy),
                     reads=[pk], writes=[("vt", si)])
                yield
            ps, pk = bank()
            psb = ps[:].bitcast(BF16)

            def ft(e, psb=psb):
                for j in range(8):
                    i = e.transpose(out=psb[0:64, j * 128:(j + 1) * 128], in_=kd[:, j * 64:(j + 1) * 64], identity=ident[:])
                return i
            S.op("pe", ft, reads=["kd", "ident"], writes=[pk])
            yield
            S.op("act", lambda e, psb=psb: e.activation(out=kdT[0:64, :, :], in_=psb[0:64, :].rearrange("p (c k) -> p c k", c=8), func=AF.Copy),
                 reads=[pk], writes=[("kdT", si)])
            yield
            if state_only and CROSS_CORE:
                S.dma("sp", sp_kdT[item].rearrange("p (c k) -> p c k", c=NCH), kdT[0:64, :, :], reads=[("kdT", si)])
                S.dma("sp", sp_vt[item].rearrange("p (c k) -> p c k", c=NCH), vt[0:64, :, :], reads=[("vt", si)])
                yield
            if not state_only:
                ps, pk = bank()

                def fs(e, ps=ps):
                    for j in range(8):
                        i = e.matmul(ps[0:64, j * 64:(j + 1) * 64], lhsT=kt[:, j * 64:(j + 1) * 64], rhs=qt[:, j * 64:(j + 1) * 64], start=True, stop=True)
                    return i
                S.op("pe", fs, reads=["kt", ("qt", si)], writes=[pk])
                yield
                S.op("dve", lambda e, ps=ps: e.tensor_tensor(out=At[0:64, :, :], in0=ps[0:64, :].rearrange("p (c t) -> p c t", c=8),
                                                             in1=cmask[:, None, :].to_broadcast([64, 8, 64]), op=ALU.mult),
                     reads=[pk, "cmask"], writes=[("At", si)])
                yield

        def hg_B(h, hh, state_only, B, si, oi=0, Pl_ap=None, Pl_key=None):
            t0 = hh * HT
            D_ = B["sets"][si]
            kdT, vt, Pl = D_["kdT"], D_["vt"], D_["Pl"]
            plk = ("Pl", si)
            if Pl_ap is not None:
                Pl, plk = Pl_ap, Pl_key
            if not state_only:
                qt, At = D_["qt"], D_["At"]
                oh = B["oh"][oi]
                if hh == 0:
                    S.op("act", lambda e: e.activation(out=Sbf[:, 0, :], in_=Sst[:, h, :], func=AF.Copy), reads=[("S", h)], writes=[("Sbf", 0)])
                    yield
            pso = None
            psS_of = []
            for c4 in range(NCH // 4):
                psS, pkS = pbank[5 + c4], ("pb", 5 + c4)

                def fS(e, c4=c4, psS=psS):
                    for j in range(4):
                        c = c4 * 4 + j
                        i = e.matmul(psS[:, j * 128:(j + 1) * 128], lhsT=kdT[0:64, c, :], rhs=vt[0:64, c, :], start=True, stop=True)
                    return i
                S.op("pe", fS, reads=[("kdT", si), ("vt", si)], writes=[pkS])
                yield
                for j in range(4):
                    psS_of.append((psS[:, j * 128:(j + 1) * 128], pkS))
            for c in range(NCH):
                psS_ap, pkS = psS_of[c]
                if not state_only:
                    if c % 8 == 0:
                        pso, pko = hold_bank()
                    sb_i = c % 4

                    def fo(e, c=c, pso=pso, sb_i=sb_i):
                        e.matmul(pso[:, (c % 8) * 64:(c % 8 + 1) * 64], lhsT=vt[0:64, c, :], rhs=At[0:64, c, :], start=True, stop=False)
                        return e.matmul(pso[:, (c % 8) * 64:(c % 8 + 1) * 64], lhsT=Sbf[:, sb_i, :], rhs=qt[:, c * 64:(c + 1) * 64], start=False, stop=True)
                    S.op("pe", fo, reads=[("vt", si), ("At", si), ("Sbf", sb_i), ("qt", si)], writes=[pko])
                    yield
                    if c % 8 == 7:
                        S.op("act", lambda e, pso=pso: e.activation(out=oh[:, :], in_=pso[:, :], func=AF.Copy), reads=[pko], writes=[("oh", oi)])
                        yield
                S.op("dve", lambda e, c=c, psS_ap=psS_ap: e.scalar_tensor_tensor(out=Sst[:, h, :], in0=Sst[:, h, :], scalar=Pl[:, c:c + 1], in1=psS_ap,
                                                                               op0=ALU.mult, op1=ALU.add), reads=[pkS, ("S", h), plk], writes=[("S", h)])
                yield
                if not state_only:
                    nb = (c + 1) % 4
                    S.op("pool", lambda e, nb=nb: e.tensor_copy(out=Sbf[:, nb, :], in_=Sst[:, h, :]), reads=[("S", h)], writes=[("Sbf", nb)])
                    yield
            if state_only:
                S.op("dve", lambda e: e.tensor_tensor_scan(out=ptmp[:, 0:NCH], data0=Pl[:, :], data1=zer64[:, 0:NCH],
                                                           initial=Ptot[:, h:h + 1], op0=ALU.mult, op1=ALU.add),
                     reads=[plk, "zer64", ("Ptot", h), "ptmp"], writes=["ptmp"])
                yield
                S.op("dve", lambda e: e.tensor_copy(out=Ptot[:, h:h + 1], in_=ptmp[:, NCH - 1:NCH]), reads=["ptmp"], writes=[("Ptot", h)])
                yield

        def hg_C(h, hh, B, oi, gi):
            if True:
                off = hh * HT
                oh, gs, rst, sqh = B["oh"][oi], B["gs"][gi], B["rst"], B["sqh"]
                S.op("act", lambda e: e.activation(out=sqh[:, :], in_=oh[:, :], func=AF.Square), reads=[("oh", oi)], writes=["sqh"])
                yield
                ps, pk = bank()
                S.op("pe", lambda e, ps=ps: e.matmul(ps[:, :], lhsT=ones[:], rhs=sqh[:, :], start=True, stop=True), reads=["ones", "sqh"], writes=[pk])
                yield
                S.op("dve", lambda e, ps=ps: e.tensor_scalar(out=rst[:, :], in0=ps[:, :], scalar1=1.0 / 128, scalar2=EPS, op0=ALU.mult, op1=ALU.add), reads=[pk], writes=["rst"])
                yield
                S.op("act", lambda e: e.activation(out=rst[:, :], in_=rst[:, :], func=AF.Ln), reads=["rst"], writes=["rst"])
                yield
                S.op("act", lambda e: e.activation(out=rst[:, :], in_=rst[:, :], func=AF.Exp, scale=-0.5), reads=["rst"], writes=["rst"])
                yield
                S.op("dve", lambda e: e.scalar_tensor_tensor(out=oh[:, :], in0=oh[:, :], scalar=vec[:, V_ONORM + h:V_ONORM + h + 1], in1=rst[:, :],
                                                             op0=ALU.mult, op1=ALU.mult), reads=[("oh", oi), "vec", "rst"], writes=[("oh", oi)])
                yield
                S.op("dve", lambda e: e.tensor_tensor(out=ob[:, h, off:off + 512], in0=oh[:, :], in1=gs[:, :], op=ALU.mult),
                     reads=[("oh", oi), ("gs", gi)], writes=[("ob", h, off // 512)])
                yield

        def hgrn_prepass(ar0):
            ar.off = ar0
            fb = [ar.f32(HT) for _ in range(2)]
            Pb = [ar.f32(HT) for _ in range(2)]
            gbuf = ar.f32(HT)
            rP = ar.f32(HT)
            tmpA = ar.f32(512)
            kd = ar.bf16(HT)
            kt = ar.bf16(HT)
            Pls = [ar.f32(NCH) for _ in range(3)]
            Bd = {"sets": [{"kdT": ar.bf16(NCH * 128).rearrange("p (c k) -> p c k", c=NCH),
                            "vt": ar.bf16(NCH * 128).rearrange("p (c k) -> p c k", c=NCH), "Pl": None} for _ in range(2)]}
            items = [(h, hh) for h in range(8) for hh in range(NPART)]
            wts_of = {}

            def A0(i):
                h, hh = items[i]
                if i == 0:
                    wts_of[0] = head_weights(0, True)
                if hh == 1 and h + 1 < 8:
                    wts_of[h + 1] = head_weights(h + 1, True)
                wf, kf = wts_of[h][2], wts_of[h][3]
                j = i % 2
                Pl = Pls[i % 3]
                off = hh * HT
                psf, pkf = proj(wf, kf, 8, 0, hb_rhs(off, 512), hb_keys(off), 512)
                yield
                S.op("act", lambda e: e.activation(out=tmpA[:, :], in_=psf[:, :], func=AF.Tanh, scale=0.5), reads=[pkf], writes=["tmpA"])
                yield
                S.op("dve", lambda e: e.tensor_scalar(out=fb[j][:, :], in0=tmpA[:, :], scalar1=lbv[:, 16 + h:17 + h], scalar2=lbv[:, 24 + h:25 + h],
                                                      op0=ALU.mult, op1=ALU.add), reads=["tmpA", "lbv"], writes=[("f", j)])
                yield
                S.op("dve", lambda e: e.tensor_tensor(out=gbuf[:, :], in0=fb[j][:, :], in1=smask[:, 0:HT], op=ALU.mult), reads=[("f", j), "smask"], writes=["g"])
                yield
                S.op("dve", lambda e: e.tensor_tensor_scan(out=Pb[j][:, :], data0=fb[j][:, :], data1=gbuf[:, :], initial=1.0, op0=ALU.mult, op1=ALU.max),
                     reads=[("f", j), "g"], writes=[("P", j)])
                yield
                S.op("dve", lambda e: e.tensor_copy(out=Pl[:, :], in_=Pb[j][:, :].rearrange("p (c t) -> p c t", c=NCH)[:, :, 63]), reads=[("P", j)], writes=[("Pl3", i % 3)])
                yield

            def A1(i):
                h, hh = items[i]
                wi, ki = wts_of[h][4], wts_of[h][5]
                j = i % 2
                si = i % 2
                Pl = Pls[i % 3]
                kdT, vt = Bd["sets"][si]["kdT"], Bd["sets"][si]["vt"]
                t0 = hh * HT
                S.op("act", lambda e: e.activation(out=rP[:, :], in_=Pb[j][:, :], func=AF.Ln), reads=[("P", j)], writes=["rP"])
                yield
                S.op("act", lambda e: e.activation(out=rP[:, :], in_=rP[:, :], func=AF.Exp, scale=-1.0), reads=["rP"], writes=["rP"])
                yield
                S.op("dve", lambda e: e.tensor_scalar(out=fb[j][:, :], in0=fb[j][:, :], scalar1=-1.0, scalar2=1.0, op0=ALU.mult, op1=ALU.add),
                     reads=[("f", j)], writes=[("f", j)])
                yield
                S.op("dve", lambda e: e.tensor_tensor(out=rP[:, :], in0=rP[:, :], in1=fb[j][:, :], op=ALU.mult), reads=["rP", ("f", j)], writes=["rP"])
                yield
                S.op("act", lambda e: e.activation(out=kt[:, :], in_=rP[:, :], func=AF.Copy), reads=["rP"], writes=["kt"])
                yield
                S.dma("sp", sp_kt[i], kt[:, :], reads=["kt"])
                S.dma("sp", sp_P[i], Pb[j][:, :], reads=[("P", j)])
                S.dma("sp", sp_Pl[i], Pl[:, :], reads=[("Pl3", i % 3)])
                yield
                S.op("dve", lambda e: e.tensor_tensor(out=kd[:, :].rearrange("p (c t) -> p c t", c=NCH), in0=rP[:, :].rearrange("p (c t) -> p c t", c=NCH),
                                                       in1=Pb[j][:, :].rearrange("p (c t) -> p c t", c=NCH)[:, :, 63:64].to_broadcast([128, NCH, 64]), op=ALU.mult),
                     reads=["rP", ("P", j)], writes=["kd"])
                yield
                for c4 in range(NCH // 4):
                    ps, pk = bank()

                    def fv(e, c4=c4, ps=ps):
                        for jj in range(4):
                            c = c4 * 4 + jj
                            o = t0 + c * 64
                            for k in range(8):
                                ins = e.matmul(ps[0:64, jj * 128:(jj + 1) * 128], lhsT=hb[:, k, o:o + 64], rhs=wi[:, k, :], start=(k == 0), stop=(k == 7))
                        return ins
                    S.op("pe", fv, reads=[ki] + hb_keys(t0 + c4 * 256), writes=[pk])
                    yield
                    S.op("act", lambda e, c4=c4, ps=ps: e.activation(out=vt[0:64, c4 * 4:(c4 + 1) * 4, :], in_=ps[0:64, :].rearrange("p (c k) -> p c k", c=4), func=AF.Copy),
                         reads=[pk], writes=[("vt", si)])
                    yield
                ps, pk = bank()
                psb = ps[:].bitcast(BF16)

                def ft(e, psb=psb):
                    for jj in range(8):
                        ins = e.transpose(out=psb[0:64, jj * 128:(jj + 1) * 128], in_=kd[:, jj * 64:(jj + 1) * 64], identity=ident[:])
                    return ins
                S.op("pe", ft, reads=["kd", "ident"], writes=[pk])
                yield
                S.op("act", lambda e, psb=psb: e.activation(out=kdT[0:64, :, :], in_=psb[0:64, :].rearrange("p (c k) -> p c k", c=8), func=AF.Copy),
                     reads=[pk], writes=[("kdT", si)])
                yield
                S.dma("sp", sp_kdT[i].rearrange("p (c k) -> p c k", c=NCH), kdT[0:64, :, :], reads=[("kdT", si)])
                S.dma("sp", sp_vt[i].rearrange("p (c k) -> p c k", c=NCH), vt[0:64, :, :], reads=[("vt", si)])
                yield

            n = len(items)
            run_interleaved([("A0", A0(0))])
            run_interleaved([("A", A1(0)), ("A0", A0(1))])
            for i in range(n):
                h, hh = items[i]
                run_interleaved([("B", hg_B(h, hh, True, Bd, i % 2, 0, Pls[i % 3], ("Pl3", i % 3))),
                                 ("A", A1(i + 1) if i + 1 < n else None), ("A0", A0(i + 2) if i + 2 < n else None)])

        def hgrn_pass(state_only, ar0):
            B = hg_alloc(state_only, ar0)
            items = [(h, hh) for h in range(8) for hh in range(NPART)]
            wts_of = {}

            def genA(i):
                h, hh = items[i]
                if i == 0:
                    wts_of[0] = head_weights(0, state_only)
                if hh == 1 and h + 1 < 8:
                    wts_of[h + 1] = head_weights(h + 1, state_only)
                if CROSS_CORE and not state_only:
                    return hg_A2(h, hh, wts_of[h], B, i % 2, i % 3, i)
                return hg_A(h, hh, wts_of[h], state_only, B, i % 2, i % 3, i)

            def genC(i):
                if state_only or i < 0:
                    return None
                h, hh = items[i]
                return hg_C(h, hh, B, i % 2, i % 3)

            run_interleaved([("A", genA(0))])
            for i in range(len(items)):
                h, hh = items[i]
                run_interleaved([("B", hg_B(h, hh, state_only, B, i % 2, i % 2)), ("C", genC(i - 1)), ("A", genA(i + 1) if i + 1 < len(items) else None)])
            run_interleaved([("C", genC(len(items) - 1))])

        zer64 = ar.f32(64)
        S.op("dve", lambda e: e.memset(zer64[:, :], 0.0), writes=["zer64"])
        smask = ar.f32(512)
        ptmp = ar.f32(8)
        S.dma("sp", smask, smask_d, writes=["smask"])
        work1 = ar.off
        sq = ar.bf16(8 * 512).rearrange("p (j t) -> p j t", j=8)
        rstd = ar.f32(512)
        norm_to_hb(BLKH, V_MIX, sq, rstd)
        S.fence()

        def head_weights(h, state_only):
            if CROSS_CORE and not state_only:
                wq, kq = wtile("w_in", 0, 8, h * 128, 128)
                wog, kog = wtile("w_in", 0, 8, 3072 + h * 128, 128)
                return (wq, kq, None, None, None, None, wog, kog)
            wf, kf = wtile("w_in", 0, 8, 1024 + h * 128, 128)
            wi, ki = wtile("w_in", 0, 8, 2048 + h * 128, 128)
            if state_only:
                return (None, None, wf, kf, wi, ki, None, None)
            wq, kq = wtile("w_in", 0, 8, h * 128, 128)
            wog, kog = wtile("w_in", 0, 8, 3072 + h * 128, 128)
            return (wq, kq, wf, kf, wi, ki, wog, kog)

        if CROSS_CORE:
            S.op("dve", lambda e: e.memset(Sst[:], 0.0), writes=[("S", h) for h in range(8)])
            S.op("dve", lambda e: e.memset(Ptot[:], 1.0), writes=[("Ptot", h) for h in range(8)])
            hgrn_prepass(work1)
            if dbg == "p1a":
                S.fence()
                return finish_with_x()
            t_a = S.dma("sp", st_loc[:, 0:1024], Sst[:].rearrange("p h k -> p (h k)"), reads=[("S", h) for h in range(8)])
            t_b = S.dma("sp", st_loc[:, 1024:1032], Ptot[:], reads=[("Ptot", h) for h in range(8)])
            S.wait_tok("pool", t_a)
            S.wait_tok("pool", t_b)

        else:
            S.op("dve", lambda e: e.memset(Sst[:], 0.0), writes=[("S", h) for h in range(8)])

        def pool_weights():
            tiles = [wload(W["pool_w"].rearrange("(g p) n -> p g n", p=128), 4, 128)]
            for g in range(4):
                tiles.append(wtile("w_in", 0, 8, 4096 + g * 128, 128))
            return tiles

        def pool_branch(ar0, tiles):
            ar.off = ar0
            pr = ar.f32(TT)
            tA = ar.f32(TT)
            tB = ar.f32(TT)
            pl = ar.bf16(NT)
            t16 = ar.f32(16)
            pw, kpw = tiles[0]
            for g in range(4):
                wsz = 2 ** (g + 1)
                wp, kp = tiles[1 + g]
                for (off, nn) in BLKH:
                    dst = 0 if off >= NT else HALO + off
                    ps, pk = proj(wp, kp, 8, 0, hb_rhs(off, nn), hb_keys(off), nn)
                    S.op("act", lambda e, ps=ps, nn=nn, dst=dst: e.activation(out=pr[:, dst:dst + nn], in_=ps[:, 0:nn], func=AF.Copy), reads=[pk], writes=["pr"])
                src = pr
                bufs = [tA, tB]
                sh = 1
                bi = 0
                while sh < wsz:
                    dstb = bufs[bi]
                    lo = 2 * sh - 1
                    S.op("dve", lambda e, src=src, dstb=dstb, sh=sh, lo=lo: e.tensor_tensor(out=dstb[:, lo:TT], in0=src[:, lo:TT], in1=src[:, lo - sh:TT - sh], op=ALU.add),
                         reads=["pr", "tA", "tB"], writes=["tA" if bi == 0 else "tB"])
                    src = dstb
                    sh *= 2
                    bi ^= 1
                S.op("dve", lambda e, src=src, wsz=wsz: e.scalar_tensor_tensor(out=pl[:, :], in0=src[:, HALO:TT], scalar=1.0 / wsz, in1=pr[:, HALO:TT], op0=ALU.mult, op1=ALU.subtract),
                     reads=["pr", "tA", "tB"], writes=["pl"])
                S.op("dve", lambda e, src=src, g=g: e.tensor_tensor(out=t16[:, :], in0=src[:, HALO:2 * HALO], in1=invc[:, g * 16:(g + 1) * 16], op=ALU.mult),
                     reads=["tA", "tB", "invc"], writes=["t16"])
                S.op("dve", lambda e: e.tensor_tensor(out=pl[:, 0:16], in0=t16[:, :], in1=pr[:, HALO:2 * HALO], op=ALU.subtract),
                     reads=["t16", "pr", "pl"], writes=["pl"])
                for (off, nn) in BLK:
                    ps, pk = bank()
                    S.op("pe", lambda e, ps=ps, g=g, off=off: e.matmul(ps[:, :], lhsT=pw[:, g, :], rhs=pl[:, off:off + 512], start=True, stop=True),
                         reads=[kpw, "pl"], writes=[pk])
                    S.op("act", lambda e, ps=ps, g=g, off=off: e.activation(out=mixed[:, g, off:off + 512], in_=ps[:, :], func=AF.Copy, scale=vec[:, V_PSCALE + g:V_PSCALE + g + 1]),
                         reads=[pk, "vec"], writes=[("mixed", g, off // 512)])

        S.fence()
        ptiles = pool_weights()
        if CROSS_CORE:
            cc_tok = S.collective_on_pool(lambda e: e.collective_compute(
                "AllGather", ALU.bypass, replica_groups=[[0, 1, 2, 3], [4, 5, 6, 7]], ins=[st_loc.opt()], outs=[st_all.opt()]), S.free_sems.pop(),
                {"pe": lambda e: e.matmul(pbank[7][0:1, 0:1], lhsT=ones[:, 0:1], rhs=ones[:, 0:1], start=True, stop=True),
                 "act": lambda e: e.activation(out=mark[:, 0:1], in_=ones[:, 0:1], func=AF.Copy),
                 "dve": lambda e: e.memset(mark[:, 1:2], 0.0)},
                lambda e: e.memset(mark2[:, :], 0.0))
        pool_branch(work1, ptiles)
        S.fence()

        if CROSS_CORE:
            ar.off = work1
            gat = ar.f32(4 * 1032).rearrange("p (r w) -> p r w", r=4)
            sel = ar.f32(4)
            acc = ar.f32(1024)
            S.dma("sp", sel[:, :], sel_d, writes=["sel"])
            S.wait_tok("sp", cc_tok)
            S.op("dve", lambda e: e.memset(Sst[:], 0.0), reads=[("S", h) for h in range(8)], writes=[("S", h) for h in range(8)] + ["Sin"])
            Sin = Sst
            for j in range(3):
                gj = gat[:, j % 4, :]
                S.dma("sp", gj, st_all[j * 128:(j + 1) * 128, :], writes=[("gat", j % 4)])
                for h in range(8):
                    S.op("dve", lambda e, gj=gj, h=h: e.scalar_tensor_tensor(out=acc[:, h * 128:(h + 1) * 128], in0=Sin[:, h, :], scalar=gj[:, 1024 + h:1025 + h],
                                                                             in1=gj[:, h * 128:(h + 1) * 128], op0=ALU.mult, op1=ALU.add),
                         reads=[("gat", j % 4), "Sin"], writes=["acc"])
                S.op("dve", lambda e: e.tensor_tensor(out=acc[:, :], in0=acc[:, :], in1=Sin[:].rearrange("p h k -> p (h k)"), op=ALU.subtract),
                     reads=["acc", "Sin"], writes=["acc"])
                S.op("dve", lambda e, j=j: e.scalar_tensor_tensor(out=Sin[:].rearrange("p h k -> p (h k)"), in0=acc[:, :], scalar=sel[:, j:j + 1],
                                                                  in1=Sin[:].rearrange("p h k -> p (h k)"), op0=ALU.mult, op1=ALU.add),
                     reads=["acc", "Sin", "sel"], writes=["Sin"] + ([("S", h) for h in range(8)] if j == 2 else []))

        S.fence()
        if dbg == "cc":
            return finish_with_x()
        hgrn_pass(False, work1)

        S.fence()
        ar.off = work0
        yb = ar.bf16(8 * NT).rearrange("p (m t) -> p m t", m=8)
        sga = ar.f32(512)
        sgb = ar.f32(512)
        for m in range(8):
            wga, kga = wtile("w_in", 0, 8, 4608 + m * 128, 128)
            wa, ka = wtile("w_branch_a", 0, 8, m * 128, 128)
            for (off, nn) in BLK:
                b = off // 512
                pga, kpga = proj(wga, kga, 8, 0, hb_rhs(off, 512), hb_keys(off), 512)
                S.op("act", lambda e, pga=pga: e.activation(out=sga[:, :], in_=pga[:, :], func=AF.Tanh, scale=0.5), reads=[kpga], writes=["sga"])
                pya, kpya = proj(wa, ka, 8, 0, lambda k, off=off: ob[:, k, off:off + 512], [("ob", hh_, b) for hh_ in range(8)], 512)
                S.op("dve", lambda e, pya=pya, m=m, off=off: e.scalar_tensor_tensor(out=yb[:, m, off:off + 512], in0=sga[:, :], scalar=1.0, in1=pya[:, :], op0=ALU.add, op1=ALU.mult),
                     reads=["sga", kpya], writes=[("yb", m, b)])
            wgb, kgb = wtile("w_in", 0, 8, 5632 + m * 128, 128)
            wbb, kbb = wtile("w_branch_b", 0, 4, m * 128, 128)
            for (off, nn) in BLK:
                b = off // 512
                pgb, kpgb = proj(wgb, kgb, 8, 0, hb_rhs(off, 512), hb_keys(off), 512)
                S.op("act", lambda e, pgb=pgb: e.activation(out=sgb[:, :], in_=pgb[:, :], func=AF.Tanh, scale=0.5), reads=[kpgb], writes=["sgb"])
                pyb, kpyb = proj(wbb, kbb, 4, 0, lambda k, off=off: mixed[:, k, off:off + 512], [("mixed", g, b) for g in range(4)], 512)
                S.op("dve", lambda e, pyb=pyb: e.scalar_tensor_tensor(out=sgb[:, :], in0=sgb[:, :], scalar=1.0, in1=pyb[:, :], op0=ALU.add, op1=ALU.mult), reads=["sgb", kpyb], writes=["sgb"])
                S.op("dve", lambda e, m=m, off=off: e.tensor_tensor(out=yb[:, m, off:off + 512], in0=yb[:, m, off:off + 512], in1=sgb[:, :], op=ALU.add),
                     reads=["sgb", ("yb", m, b)], writes=[("yb", m, b)])
        for m in range(8):
            wo, ko = wtile("w_out", 0, 8, m * 128, 128)
            for (off, nn) in BLK:
                b = off // 512
                ps, pk = proj(wo, ko, 8, 0, lambda k, off=off: yb[:, k, off:off + 512], [("yb", k, b) for k in range(8)], 512)
                S.op("dve", lambda e, ps=ps, m=m, off=off: e.scalar_tensor_tensor(out=xT[:, m, off:off + 512], in0=ps[:, :], scalar=0.5, in1=xT[:, m, off:off + 512],
                                                                              op0=ALU.mult, op1=ALU.add),
                     reads=[pk, ("x", m, b)], writes=[("x", m, b)])

        S.fence()
        if dbg == "mix":
            return finish_with_x()
        ffn("ffn2_w1", "ffn2_w3", "ffn2_w2", V_FFN2, BLK)
        S.fence()
        if dbg == "ffn2":
            return finish_with_x()

        ar = Arena()
        pbf = ar.bf16(2 * NT).rearrange("p (c t) -> p c t", c=2)
        eraw = [ar.f32(8 * 512).rearrange("p (m t) -> p m t", m=8) for _ in range(2)]
        sq = ar.bf16(2 * 512).rearrange("p (j t) -> p j t", j=2)
        rstd = ar.f32(512)
        rse = [ar.f32(512) for _ in range(2)]
        sq8 = ar.bf16(8 * 512).rearrange("p (j t) -> p j t", j=8)
        sg = ar.f32(2 * 512).rearrange("p (j t) -> p j t", j=2)
        tt_ = ar.f32(2 * 512).rearrange("p (j t) -> p j t", j=2)
        ostg = ar.f32(8 * 512).rearrange("p (m t) -> p m t", m=8)
        pstage = ostg[:, :, :].rearrange("p m t -> p (m t)").rearrange("p (c t) -> p c t", c=2)
        S.dma("sp", pstage, pT_d.rearrange("(c p) t -> p c t", p=128), writes=["ostg"])
        for c in range(2):
            S.op("pool", lambda e, c=c: e.tensor_copy(out=pbf[:, c, :], in_=pstage[:, c, :]), reads=["ostg"], writes=["pbf"])
        norm_to_hb(BLK, V_PLE, sq8, rstd)
        out_toks = []
        def ple_e(pair):
            wpj0, kpj0 = wtile("ple_w_proj", 0, 2, 0, 512)
            wpj1, kpj1 = wtile("ple_w_proj", 0, 2, 512, 512)
            for bi, b in enumerate(pair):
                off = 512 * b
                pss, pkss = hold_bank()
                for m in range(8):
                    ps, pk = proj(wpj0 if m < 4 else wpj1, kpj0 if m < 4 else kpj1, 2, (m % 4) * 128, lambda k, off=off: pbf[:, k, off:off + 512], ["pbf"], 512)
                    S.op("act", lambda e, ps=ps, m=m, bi=bi: e.activation(out=eraw[bi][:, m, :], in_=ps[:, :], func=AF.Copy), reads=[pk], writes=[("eraw", bi, m)])
                    S.op("act", lambda e, ps=ps, m=m: e.activation(out=sq[:, m % 2, :], in_=ps[:, :], func=AF.Square), reads=[pk], writes=[("sq", m % 2)])
                    S.op("pe", lambda e, m=m, pss=pss: e.matmul(pss[:, :], lhsT=ones[:], rhs=sq[:, m % 2, :], start=(m == 0), stop=(m == 7)),
                         reads=["ones", ("sq", m % 2)], writes=[pkss])
                rstd_from_psum(pss, pkss, 512, rse[bi], ("rse", bi), 1.0 / D)

        def ple_g(pair):
            for m in range(8):
                wg, kg = wtile("ple_w_gate", 0, 8, m * 128, 128)
                for bi, b in enumerate(pair):
                    off = 512 * b
                    ps, pk = proj(wg, kg, 8, 0, hb_rhs(off, 512), hb_keys(off), 512)
                    S.op("act", lambda e, ps=ps, bi=bi: e.activation(out=sg[:, bi, :], in_=ps[:, :], func=AF.Sigmoid), reads=[pk], writes=[("sg", bi)])
                    S.op("dve", lambda e, m=m, bi=bi: e.scalar_tensor_tensor(out=tt_[:, bi, :], in0=eraw[bi][:, m, :], scalar=vec[:, V_POST + m:V_POST + m + 1], in1=rse[bi][:, :],
                                                                             op0=ALU.mult, op1=ALU.mult), reads=[("eraw", bi, m), "vec", ("rse", bi)], writes=[("tt", bi)])
                    S.op("dve", lambda e, bi=bi: e.tensor_tensor(out=tt_[:, bi, :], in0=tt_[:, bi, :], in1=sg[:, bi, :], op=ALU.mult),
                         reads=[("tt", bi), ("sg", bi)], writes=[("tt", bi)])
                    S.op("dve", lambda e, m=m, off=off, bi=bi: e.tensor_tensor(out=xT[:, m, off:off + 512], in0=xT[:, m, off:off + 512], in1=tt_[:, bi, :], op=ALU.add),
                         reads=[("tt", bi), ("x", m, b)], writes=[("x", m, b)])

        def ple_f(pair):
            for bi, b in enumerate(pair):
                off = 512 * b
                ps, psk = bank()
                for c in range(8):
                    S.op("act", lambda e, c=c, off=off: e.activation(out=sq8[:, c, :], in_=xT[:, c, off:off + 512], func=AF.Square),
                         reads=[("x", c, b)], writes=[("sq8", c)])
                for c in range(8):
                    S.op("pe", lambda e, c=c, ps=ps: e.matmul(ps[:, :], lhsT=ones[:], rhs=sq8[:, c, :], start=(c == 0), stop=(c == 7)),
                         reads=["ones", ("sq8", c)], writes=[psk])
                rstd_from_psum(ps, psk, 512, rstd, "rstd", 1.0 / D)
                for c in range(8):
                    S.op("dve", lambda e, c=c, off=off: e.scalar_tensor_tensor(out=ostg[:, c, :], in0=xT[:, c, off:off + 512], scalar=vec[:, V_FINAL + c:V_FINAL + c + 1], in1=rstd[:, :],
                                                                               op0=ALU.mult, op1=ALU.mult), reads=[("x", c, b), "vec", "rstd"], writes=["ostg"])
                out_toks.append(S.dma("sp", outT_d.rearrange("(c p) t -> p c t", p=128)[:, :, off:off + 512], ostg[:, :, :], reads=["ostg"]))

        ple_e([0, 1])
        ple_g([0, 1])
        ple_e([2, 3])
        ple_f([0, 1])
        ple_g([2, 3])
        ple_f([2, 3])
        for t in out_toks:
            S.wait_tok("sp", t)
        S.emit()
    return nc


_DBG = None


def _pm(v, n):
    return np.ascontiguousarray(np.asarray(v, np.float32).reshape(n, 128).T)


def kernel(x, p, ffn1_norm, ffn1_w1, ffn1_w3, ffn1_w2, mix_norm, w_in, hgrn_lb, hgrn_onorm,
           w_branch_a, pool_w, pool_scale, w_branch_b, w_out, ffn2_norm, ffn2_w1, ffn2_w3,
           ffn2_w2, ple_norm, ple_w_gate, ple_w_proj, ple_post_norm, final_norm):
    f = lambda a: np.ascontiguousarray(np.asarray(a, np.float32))
    x = f(x)
    p = f(p)
    vecs = np.concatenate([
        _pm(ffn1_norm[0], 8), _pm(mix_norm[0], 8), _pm(ffn2_norm[0], 8), _pm(ple_norm[0], 8), _pm(ple_post_norm[0], 8),
        _pm(final_norm, 8), _pm(hgrn_lb[0], 8), _pm(hgrn_lb[1], 8), _pm(hgrn_onorm[0], 8), _pm(pool_scale[0], 4)], axis=1)
    vecs = np.ascontiguousarray(vecs)
    assert vecs.shape == (128, NV)
    shared = {
        "vecs": vecs,
        "cmask": np.triu(np.ones((64, 64), np.float32)),
        "smask": np.ascontiguousarray(np.tile((np.arange(512) % 64 == 0).astype(np.float32)[None, :], (128, 1))),
        "ffn1_w1": f(ffn1_w1[0]), "ffn1_w3": f(ffn1_w3[0]), "ffn1_w2": f(ffn1_w2[0]),
        "ffn2_w1": f(ffn2_w1[0]), "ffn2_w3": f(ffn2_w3[0]), "ffn2_w2": f(ffn2_w2[0]),
        "w_in": f(w_in[0]), "w_branch_a": f(w_branch_a[0]), "w_branch_b": f(w_branch_b[0]),
        "pool_w": f(np.asarray(pool_w[0]).reshape(512, 128)), "w_out": f(w_out[0]),
        "ple_w_gate": f(ple_w_gate[0]), "ple_w_proj": f(ple_w_proj[0]),
    }
    in_maps = []
    for c in range(NCORES):
        b, j = divmod(c, 4)
        t0 = j * NT
        xT = np.zeros((D, TT), np.float32)
        xT[:, :NT] = x[b, t0:t0 + NT, :].T
        if j > 0:
            xT[:, NT:] = x[b, t0 - HALO:t0, :].T
        invc = np.zeros((128, 64), np.float32)
        for g, w in enumerate((2, 4, 8, 16)):
            pos = t0 + np.arange(16) + 1
            invc[:, g * 16:(g + 1) * 16] = (1.0 / np.minimum(pos, w)).astype(np.float32)[None, :]
        m = dict(shared)
        m["xT"] = xT
        m["pT"] = np.ascontiguousarray(p[0, b, t0:t0 + NT, :].T)
        m["invc"] = invc
        if CROSS_CORE:
            sel = np.zeros((128, 4), np.float32)
            for jj in range(4):
                if jj < j:
                    sel[:, jj] = 1.0
            m["sel"] = sel
        in_maps.append(m)
    nc = build_program(_DBG)
    res = run_bass_kernel_spmd(nc, in_maps, core_ids=list(range(NCORES)))
    out = np.empty((2, 8192, D), np.float32)
    for c in range(NCORES):
        b, j = divmod(c, 4)
        out[b, j * NT:(j + 1) * NT, :] = np.asarray(res.results[c]["outT"]).T
    return out
```

```python
import numpy as np
from contextlib import ExitStack
import concourse.bass as bass
import concourse.mybir as mybir
from concourse.bass_utils import run_bass_kernel_spmd

F32 = mybir.dt.float32
BF16 = mybir.dt.bfloat16
AF = mybir.ActivationFunctionType
ALU = mybir.AluOpType

D = 1024
NT = 2048
HALO = 16
TT = NT + HALO
DFF = 2816
NFC = DFF // 128
NCORES = 8
EPS = 1e-6
NSLOT = 5
SLOTW = 2048
RW = 21504
CROSS_CORE = True
CC_VARIANT = "full"

V_FFN1, V_MIX, V_FFN2, V_PLE, V_POST, V_FINAL, V_LB0, V_LB1, V_ONORM, V_PSCALE = 0, 8, 16, 24, 32, 40, 48, 56, 64, 72
NV = 76

BLK = [(0, 512), (512, 512), (1024, 512), (1536, 512)]
BLKH = BLK + [(NT, HALO)]


class Sched:
    ENG = ("pe", "act", "dve", "pool", "sp")

    def __init__(self, nc, stack, n_dma_sems=12, n_eng_sems=40, epoch=12000, same_engine_sync=True):
        self.nc = nc
        self.same = same_engine_sync
        self.epoch = epoch
        self.lists = {k: [] for k in self.ENG}
        self.free_sems = [stack.enter_context(nc.semaphore(f"es{i}")) for i in range(n_eng_sems)]
        self.dsems = {"sp": [stack.enter_context(nc.semaphore(f"dsp{i}")) for i in range(8)],
                      "pool": [stack.enter_context(nc.semaphore(f"dpa{i}")) for i in range(n_dma_sems)]}
        self.dsems_b = {"sp": [stack.enter_context(nc.semaphore(f"dsq{i}")) for i in range(8)],
                        "pool": [stack.enter_context(nc.semaphore(f"dpb{i}")) for i in range(n_dma_sems)]}
        self.dval = {q: [0] * len(v) for q, v in self.dsems.items()}
        self.drr = {q: 0 for q in self.dsems}
        self.cur = {k: [self.free_sems.pop(), 0] for k in self.ENG}
        self.seen = {k: {} for k in self.ENG}
        self.last_w = {}
        self.rd = {}

    def _need(self, eng, tok, waits):
        sem, val, src = tok
        if src == eng and not (self.same and eng != "pe"):
            return
        key = id(sem)
        if self.seen[eng].get(key, 0) >= val:
            return
        self.seen[eng][key] = val
        waits.append((sem, val))

    def _deps(self, eng, reads, writes):
        waits = []
        for k in reads:
            t = self.last_w.get(k)
            if t is not None:
                self._need(eng, t, waits)
        for k in writes:
            t = self.last_w.get(k)
            if t is not None:
                self._need(eng, t, waits)
            for t in self.rd.get(k, ()):
                self._need(eng, t, waits)
        return waits

    def _commit(self, tok, reads, writes):
        for k in reads:
            lst = self.rd.setdefault(k, [])
            lst.append(tok)
            if len(lst) > 16:
                best = {}
                for t in lst:
                    kk = id(t[0])
                    if kk not in best or best[kk][1] < t[1]:
                        best[kk] = t
                self.rd[k] = list(best.values())
        for k in writes:
            self.last_w[k] = tok
            self.rd[k] = []

    def op(self, eng, fn, reads=(), writes=()):
        waits = self._deps(eng, reads, writes)
        cur = self.cur[eng]
        if cur[1] >= self.epoch:
            cur = self.cur[eng] = [self.free_sems.pop(), 0]
        cur[1] += 1
        sem, val = cur[0], cur[1]
        lst = self.lists[eng]
        for (s, v) in waits:
            lst.append(lambda e, s=s, v=v: e.wait_ge(s, v))
        lst.append(lambda e, fn=fn, sem=sem: fn(e).then_inc(sem, 1))
        tok = (sem, val, eng)
        self._commit(tok, reads, writes)
        return tok

    def dma(self, q, out, in_, reads=(), writes=()):
        waits = self._deps(q, reads, writes)
        j = self.drr[q]
        self.drr[q] = (j + 1) % len(self.dsems[q])
        sem = self.dsems[q][j]
        if self.dval[q][j] > 0:
            self._need(q, (sem, self.dval[q][j], None), waits)
        self.dval[q][j] += 16
        val = self.dval[q][j]
        lst = self.lists[q]
        for (s, v) in waits:
            lst.append(lambda e, s=s, v=v: e.wait_ge(s, v))
        lst.append(lambda e, out=out, in_=in_, sem=sem: e.dma_start(out=out, in_=in_).then_inc(sem, 16))
        tok = (sem, val, None)
        self._commit(tok, reads, writes)
        return tok

    def collective_on_pool(self, issue_fn, cc_sem, markers, post_fn):
        all_toks = []
        for q in self.dsems:
            for j, sem in enumerate(self.dsems[q]):
                if self.dval[q][j] > 0:
                    all_toks.append((sem, self.dval[q][j], None))
        for eng in self.ENG:
            waits = []
            for t in all_toks:
                self._need(eng, t, waits)
            for (s, v) in waits:
                self.lists[eng].append(lambda e, s=s, v=v: e.wait_ge(s, v))
        mk = [self.op(eng, fn, writes=[("marker", eng)]) for eng, fn in markers.items()]
        waits = []
        for t in mk:
            self._need("pool", t, waits)
        lst = self.lists["pool"]
        for (s, v) in waits:
            lst.append(lambda e, s=s, v=v: e.wait_ge(s, v))
        lst.append(lambda e: issue_fn(e).then_inc(cc_sem))
        lst.append(lambda e: e.wait_ge(cc_sem, 1))
        after = self.op("pool", post_fn, writes=[("marker", "pool")])
        self.wait_tok("sp", after)
        return after

    def fence(self):
        toks = [(self.cur[d][0], self.cur[d][1], d) for d in ("pe", "act", "dve") if self.cur[d][1] > 0]
        for eng in self.ENG:
            waits = []
            for t in toks:
                if t[2] != eng:
                    self._need(eng, t, waits)
            for (s, v) in waits:
                self.lists[eng].append(lambda e, s=s, v=v: e.wait_ge(s, v))

    def wait_tok(self, eng, tok):
        waits = []
        self._need(eng, tok, waits)
        for (s, v) in waits:
            self.lists[eng].append(lambda e, s=s, v=v: e.wait_ge(s, v))

    def emit(self):
        nc = self.nc
        L = self.lists
        with nc.Block() as block:
            @block.tensor
            def _(e):
                for th in L["pe"]:
                    th(e)

            @block.scalar
            def _(e):
                for th in L["act"]:
                    th(e)

            @block.vector
            def _(e):
                for th in L["dve"]:
                    th(e)

            @block.gpsimd
            def _(e):
                for th in L["pool"]:
                    th(e)

            @block.sync
            def _(e):
                for th in L["sp"]:
                    th(e)


def build_program(dbg=None):
    nc = bass.Bass("TRN2", target_bir_lowering=False)

    def din(name, shape, dt=F32):
        return nc.dram_tensor(name, shape, dt, kind="ExternalInput").ap()

    xT_d = din("xT", [D, TT])
    pT_d = din("pT", [256, NT])
    vec_d = din("vecs", [128, NV])
    cmask_d = din("cmask", [64, 64])
    invc_d = din("invc", [128, 64])
    smask_d = din("smask", [128, 512])
    W = {
        "ffn1_w1": din("ffn1_w1", [D, DFF]), "ffn1_w3": din("ffn1_w3", [D, DFF]), "ffn1_w2": din("ffn1_w2", [DFF, D]),
        "ffn2_w1": din("ffn2_w1", [D, DFF]), "ffn2_w3": din("ffn2_w3", [D, DFF]), "ffn2_w2": din("ffn2_w2", [DFF, D]),
        "w_in": din("w_in", [D, 6656]), "w_branch_a": din("w_branch_a", [D, D]), "w_branch_b": din("w_branch_b", [512, D]),
        "pool_w": din("pool_w", [512, 128]), "w_out": din("w_out", [D, D]),
        "ple_w_gate": din("ple_w_gate", [D, D]), "ple_w_proj": din("ple_w_proj", [256, D]),
    }
    outT_d = nc.dram_tensor("outT", [D, NT], F32, kind="ExternalOutput").ap()
    if CROSS_CORE:
        st_loc = nc.dram_tensor("st_loc", [128, 8 * 128 + 8], F32, kind="Internal").ap()
        st_all = nc.dram_tensor("st_all", [4 * 128, 8 * 128 + 8], F32, kind="Internal").ap()
        sel_d = din("sel", [128, 4])
        NIT = 8 * (NT // 512)
        sp_kdT = nc.dram_tensor("sp_kdT", [NIT, 64, 8 * 128], BF16, kind="Internal").ap()
        sp_vt = nc.dram_tensor("sp_vt", [NIT, 64, 8 * 128], BF16, kind="Internal").ap()
        sp_kt = nc.dram_tensor("sp_kt", [NIT, 128, 512], BF16, kind="Internal").ap()
        sp_P = nc.dram_tensor("sp_P", [NIT, 128, 512], F32, kind="Internal").ap()
        sp_Pl = nc.dram_tensor("sp_Pl", [NIT, 128, 8], F32, kind="Internal").ap()

    with ExitStack() as st:
        def sb(name, shape, dt):
            return st.enter_context(nc.sbuf_tensor(name, shape, dt))

        xT = sb("xTs", [128, 8, TT], F32)
        hb = sb("hb", [128, 8, TT], BF16)
        wbuf = sb("wbuf", [128, NSLOT * SLOTW // 2], F32)
        R = sb("R", [128, RW], F32)
        vec = sb("vec", [128, NV], F32)
        lbv = sb("lbv", [128, 32], F32)
        ident = sb("identb", [128, 128], BF16)
        ones = sb("ones", [128, 128], BF16)
        cmask = sb("cmasks", [64, 64], F32)
        invc = sb("invcs", [128, 64], F32)
        Sst = sb("Sst", [128, 8, 128], F32)
        Sbf = sb("Sbf", [128, 4, 128], BF16)
        Ptot = sb("Ptot", [128, 8], F32)
        mark = sb("mark", [128, 2], F32)
        mark2 = sb("mark2", [128, 2], F32)
        pbank = [st.enter_context(nc.psum_tensor(f"pb{i}", [128, 512], F32)) for i in range(8)]

        S = Sched(nc, st)
        state = {"bank": 0, "slot": 0, "tmp": 0}

        bank_groups = {"A": [0, 1, 2, 3], "C": [4], "A0": [4]}
        grp_pos = {"A": 0, "C": 0, "A0": 0}
        cur_grp = [None]

        def bank():
            g = cur_grp[0]
            if g is None:
                i = state["bank"]
                state["bank"] = (i + 1) % 7
            else:
                lst = bank_groups[g]
                i = lst[grp_pos[g] % len(lst)]
                grp_pos[g] += 1
            return pbank[i], ("pb", i)

        def hold_bank():
            return pbank[7], ("pb", 7)

        class Arena:
            def __init__(self):
                self.off = 0

            def f32(self, words):
                a = self.off
                self.off += words
                assert self.off <= RW, self.off
                return R[:, a:a + words]

            def bf16(self, elems):
                words = (elems + 1) // 2
                a = self.off
                self.off += words
                assert self.off <= RW, self.off
                return R[:, a:a + words].bitcast(BF16)

        SLOT32 = SLOTW // 2

        def wload(dram_ap, kc, ncols):
            i = state["slot"]
            state["slot"] = (i + 1) % NSLOT
            n = kc * ncols
            assert n <= SLOT32, n
            raw = wbuf[:, i * SLOT32:i * SLOT32 + n]
            S.dma("sp", raw.rearrange("p (k n) -> p k n", k=kc), dram_ap, writes=[("w", i)])
            half = wbuf[:, i * SLOT32:i * SLOT32 + (n + 1) // 2].bitcast(BF16)[:, 0:n]
            S.op("pool", lambda e, half=half, raw=raw: e.tensor_copy(out=half, in_=raw), reads=[("w", i)], writes=[("w", i)])
            return half.rearrange("p (k n) -> p k n", k=kc), ("w", i)

        def wtile(name, r0, nk, c0, ncols):
            ap = W[name].rearrange("(k p) n -> p k n", p=128)[:, r0:r0 + nk, c0:c0 + ncols]
            return wload(ap, nk, ncols)

        for (off_, n_) in [(NT, HALO)] + BLK:
            b_ = 4 if off_ >= NT else off_ // 512
            S.dma("sp", xT[:, :, off_:off_ + n_], xT_d.rearrange("(c p) t -> p c t", p=128)[:, :, off_:off_ + n_],
                  writes=[("x", c, b_) for c in range(8)])
        S.dma("sp", vec[:], vec_d, writes=["vec"])
        S.dma("sp", cmask[:], cmask_d, writes=["cmask"])
        S.dma("sp", invc[:], invc_d, writes=["invc"])
        S.op("pool", lambda e: e.memset(ident[:], 0.0), writes=["ident"])
        S.op("pool", lambda e: e.affine_select(out=ident[:], in_=ident[:], pattern=[[-1, 128]], compare_op=ALU.not_equal, fill=1.0,
                                               base=0, channel_multiplier=1), reads=["ident"], writes=["ident"])
        S.op("dve", lambda e: e.memset(ones[:], 1.0), writes=["ones"])
        S.op("dve", lambda e: e.tensor_tensor(out=lbv[:, 0:8], in0=vec[:, V_LB0:V_LB0 + 8], in1=vec[:, V_LB1:V_LB1 + 8], op=ALU.subtract),
             reads=["vec"], writes=["lbv"])
        S.op("act", lambda e: e.activation(out=lbv[:, 0:8], in_=lbv[:, 0:8], func=AF.Sigmoid), reads=["lbv"], writes=["lbv"])
        S.op("dve", lambda e: e.tensor_scalar(out=lbv[:, 8:16], in0=lbv[:, 0:8], scalar1=-1.0, scalar2=1.0, op0=ALU.mult, op1=ALU.add),
             reads=["lbv"], writes=["lbv"])

        S.op("dve", lambda e: e.tensor_scalar(out=lbv[:, 16:24], in0=lbv[:, 8:16], scalar1=0.5, scalar2=None, op0=ALU.mult), reads=["lbv"], writes=["lbv"])
        S.op("dve", lambda e: e.tensor_tensor(out=lbv[:, 24:32], in0=lbv[:, 0:8], in1=lbv[:, 16:24], op=ALU.add), reads=["lbv"], writes=["lbv"])

        def blk_idx(off):
            return 4 if off >= NT else off // 512

        def rstd_from_psum(ps, psk, n, dst, dstk, inv_d):
            S.op("dve", lambda e: e.tensor_scalar(out=dst[:, 0:n], in0=ps[:, 0:n], scalar1=inv_d, scalar2=EPS, op0=ALU.mult, op1=ALU.add),
                 reads=[psk], writes=[dstk])
            S.op("act", lambda e: e.activation(out=dst[:, 0:n], in_=dst[:, 0:n], func=AF.Ln), reads=[dstk], writes=[dstk])
            S.op("act", lambda e: e.activation(out=dst[:, 0:n], in_=dst[:, 0:n], func=AF.Exp, scale=-0.5), reads=[dstk], writes=[dstk])

        def norm_to_hb(blocks, gcol, sq, rstd):
            for (off, n) in blocks:
                b = blk_idx(off)
                ps, psk = bank()
                nsq = sq.shape[1]
                for c in range(8):
                    S.op("act", lambda e, c=c, off=off, n=n: e.activation(out=sq[:, c % nsq, 0:n], in_=xT[:, c, off:off + n], func=AF.Square),
                         reads=[("x", c, b)], writes=[("sq", c % nsq)])
                for c in range(8):
                    S.op("pe", lambda e, c=c, n=n, ps=ps: e.matmul(ps[:, 0:n], lhsT=ones[:], rhs=sq[:, c % nsq, 0:n], start=(c == 0), stop=(c == 7)),
                         reads=["ones", ("sq", c % nsq)], writes=[psk])
                rstd_from_psum(ps, psk, n, rstd, "rstd", 1.0 / D)
                for c in range(8):
                    S.op("dve", lambda e, c=c, off=off, n=n: e.scalar_tensor_tensor(
                        out=hb[:, c, off:off + n], in0=xT[:, c, off:off + n], scalar=vec[:, gcol + c:gcol + c + 1], in1=rstd[:, 0:n],
                        op0=ALU.mult, op1=ALU.mult), reads=[("x", c, b), "vec", "rstd"], writes=[("hb", c, b)])

        def proj(wt, wk, nk, col0, rhs_fn, rhs_keys, n, M=128):
            ps, psk = bank()
            segs = list(zip(wt, wk, nk)) if isinstance(wt, list) else [(wt, wk, nk)]
            tot = sum(sg_[2] for sg_ in segs)

            def f(e):
                kk = 0
                for (t_, _k, n_) in segs:
                    for k in range(n_):
                        i = e.matmul(ps[0:M, 0:n], lhsT=t_[:, k, col0:col0 + M], rhs=rhs_fn(kk), start=(kk == 0), stop=(kk == tot - 1))
                        kk += 1
                return i
            S.op("pe", f, reads=[sg_[1] for sg_ in segs] + rhs_keys, writes=[psk])
            return ps, psk

        def hb_rhs(off, n):
            return lambda k: hb[:, k, off:off + n]

        def hb_keys(off):
            b = blk_idx(off)
            return [("hb", c, b) for c in range(8)]

        def ffn(w1n, w3n, w2n, gcol, blocks):
            ar = Arena()
            aT = ar.bf16(11 * TT).rearrange("p (j t) -> p j t", j=11)
            sq = ar.bf16(8 * 512).rearrange("p (j t) -> p j t", j=8)
            rstd = ar.f32(512)
            stmp = ar.f32(2 * 512).rearrange("p (j t) -> p j t", j=2)
            norm_to_hb(blocks, gcol, sq, rstd)
            for half in range(2):
                for j in range(11):
                    n = half * 11 + j
                    w1t, k1 = wtile(w1n, 0, 8, n * 128, 128)
                    w3t, k3 = wtile(w3n, 0, 8, n * 128, 128)
                    for (off, nn) in blocks:
                        b = blk_idx(off)
                        ps1, pk1 = proj(w1t, k1, 8, 0, hb_rhs(off, nn), hb_keys(off), nn)
                        ps3, pk3 = proj(w3t, k3, 8, 0, hb_rhs(off, nn), hb_keys(off), nn)
                        ti = state["tmp"] = (state["tmp"] + 1) % 2
                        S.op("act", lambda e, ps1=ps1, nn=nn, ti=ti: e.activation(out=stmp[:, ti, 0:nn], in_=ps1[:, 0:nn], func=AF.Silu),
                             reads=[pk1], writes=[("stmp", ti)])
                        S.op("dve", lambda e, ps3=ps3, nn=nn, ti=ti, j=j, off=off: e.tensor_tensor(
                            out=aT[:, j, off:off + nn], in0=ps3[:, 0:nn], in1=stmp[:, ti, 0:nn], op=ALU.mult),
                            reads=[pk3, ("stmp", ti)], writes=[("aT", j, b)])
                for m in range(8):
                    w2a, k2a = wtile(w2n, half * 11, 6, m * 128, 128)
                    w2b, k2b = wtile(w2n, half * 11 + 6, 5, m * 128, 128)
                    for (off, nn) in blocks:
                        b = blk_idx(off)
                        ps, pk = proj([w2a, w2b], [k2a, k2b], [6, 5], 0, lambda k, off=off, nn=nn: aT[:, k, off:off + nn], [("aT", j, b) for j in range(11)], nn)
                        S.op("dve", lambda e, ps=ps, nn=nn, m=m, off=off: e.scalar_tensor_tensor(
                            out=xT[:, m, off:off + nn], in0=ps[:, 0:nn], scalar=0.5, in1=xT[:, m, off:off + nn], op0=ALU.mult, op1=ALU.add),
                            reads=[pk, ("x", m, b)], writes=[("x", m, b)])

        def finish_with_x():
            toks = []
            for (off, nn) in BLK:
                b = off // 512
                toks.append(S.dma("sp", outT_d.rearrange("(c p) t -> p c t", p=128)[:, :, off:off + 512], xT[:, :, off:off + 512],
                                  reads=[("x", c, b) for c in range(8)]))
            for t in toks:
                S.wait_tok("sp", t)
            S.emit()
            return nc

        ffn("ffn1_w1", "ffn1_w3", "ffn1_w2", V_FFN1, BLKH)
        S.fence()
        if dbg == "ffn1":
            return finish_with_x()

        ar = Arena()
        ob = ar.bf16(8 * NT).rearrange("p (h t) -> p h t", h=8)
        mixed = ar.bf16(4 * NT).rearrange("p (g t) -> p g t", g=4)
        work0 = ar.off

        HT = 512
        NCH = HT // 64
        NB = HT // 512
        NPART = NT // HT

        def run_interleaved(gens):
            gens = [g for g in gens if g is not None and g[1] is not None]
            while gens:
                for g in list(gens):
                    cur_grp[0] = g[0]
                    try:
                        next(g[1])
                    except StopIteration:
                        gens.remove(g)
            cur_grp[0] = None

        def hg_alloc(state_only, ar0):
            ar.off = ar0
            B = {}
            B["fbuf"] = ar.f32(HT)
            B["Pbuf"] = ar.f32(HT)
            B["rP"] = ar.f32(HT)
            B["tmpA"] = ar.f32(512)
            B["kd"] = ar.bf16(HT)
            B["kt"] = ar.bf16(HT)
            B["sets"] = []
            for i in range(2):
                d = {"kdT": ar.bf16(NCH * 128).rearrange("p (c k) -> p c k", c=NCH),
                     "vt": ar.bf16(NCH * 128).rearrange("p (c k) -> p c k", c=NCH),
                     "Pl": ar.f32(NCH)}
                if not state_only:
                    d["qt"] = ar.bf16(HT)
                    d["At"] = ar.bf16(NCH * 64).rearrange("p (c k) -> p c k", c=NCH)
                B["sets"].append(d)
            if not state_only:
                B["oh"] = [ar.f32(HT) for _ in range(2)]
                B["gs"] = [ar.bf16(HT) for _ in range(3)]
                B["rst"] = ar.f32(512)
                B["sqh"] = ar.bf16(512)
            return B

        def hg_A2(h, hh, wts, B, si, gi, item):
            t0 = hh * HT
            wq, kq, wf, kf, wi, ki, wog, kog = wts
            Pbuf, tmpA, kt = B["Pbuf"], B["tmpA"], B["kt"]
            D_ = B["sets"][si]
            kdT, vt, Pl, qt, At, gs = D_["kdT"], D_["vt"], D_["Pl"], D_["qt"], D_["At"], B["gs"][gi]
            qs = tmpA
            off = t0
            S.dma("sp", Pbuf[:, :], sp_P[item], writes=["P"])
            S.dma("sp", kt[:, :], sp_kt[item], writes=["kt"])
            S.dma("sp", kdT[0:64, :, :], sp_kdT[item].rearrange("p (c k) -> p c k", c=NCH), writes=[("kdT", si)])
            S.dma("sp", vt[0:64, :, :], sp_vt[item].rearrange("p (c k) -> p c k", c=NCH), writes=[("vt", si)])
            S.dma("sp", Pl[:, :], sp_Pl[item], writes=[("Pl", si)])
            yield
            psq, pkq = proj(wq, kq, 8, 0, hb_rhs(off, 512), hb_keys(off), 512)
            yield
            psg, pkg = proj(wog, kog, 8, 0, hb_rhs(off, 512), hb_keys(off), 512)
            yield
            S.op("act", lambda e: e.activation(out=qs[:, :], in_=psq[:, :], func=AF.Silu), reads=[pkq], writes=["tmpA"])
            yield
            S.op("act", lambda e: e.activation(out=gs[:, :], in_=psg[:, :], func=AF.Silu), reads=[pkg], writes=[("gs", gi)])
            yield
            S.op("dve", lambda e: e.tensor_tensor(out=qt[:, :], in0=qs[:, :], in1=Pbuf[:, :], op=ALU.mult), reads=["tmpA", "P"], writes=[("qt", si)])
            yield
            ps, pk = bank()

            def fs(e, ps=ps):
                for j in range(8):
                    i = e.matmul(ps[0:64, j * 64:(j + 1) * 64], lhsT=kt[:, j * 64:(j + 1) * 64], rhs=qt[:, j * 64:(j + 1) * 64], start=True, stop=True)
                return i
            S.op("pe", fs, reads=["kt", ("qt", si)], writes=[pk])
            yield
            S.op("dve", lambda e, ps=ps: e.tensor_tensor(out=At[0:64, :, :], in0=ps[0:64, :].rearrange("p (c t) -> p c t", c=8),
                                                         in1=cmask[:, None, :].to_broadcast([64, 8, 64]), op=ALU.mult),
                 reads=[pk, "cmask"], writes=[("At", si)])
            yield

        def hg_A(h, hh, wts, state_only, B, si, gi=0, item=0):
            t0 = hh * HT
            wq, kq, wf, kf, wi, ki, wog, kog = wts
            fbuf, Pbuf, rP, tmpA, kd = B["fbuf"], B["Pbuf"], B["rP"], B["tmpA"], B["kd"]
            gbuf = rP
            qs = tmpA
            D_ = B["sets"][si]
            kdT, vt, Pl = D_["kdT"], D_["vt"], D_["Pl"]
            off = t0
            psf, pkf = proj(wf, kf, 8, 0, hb_rhs(off, 512), hb_keys(off), 512)
            yield
            if not state_only:
                psq, pkq = proj(wq, kq, 8, 0, hb_rhs(off, 512), hb_keys(off), 512)
                yield
                psg, pkg = proj(wog, kog, 8, 0, hb_rhs(off, 512), hb_keys(off), 512)
                yield
            S.op("act", lambda e: e.activation(out=tmpA[:, :], in_=psf[:, :], func=AF.Tanh, scale=0.5), reads=[pkf], writes=["tmpA"])
            yield
            S.op("dve", lambda e: e.tensor_scalar(out=fbuf[:, :], in0=tmpA[:, :], scalar1=lbv[:, 16 + h:17 + h], scalar2=lbv[:, 24 + h:25 + h],
                                                  op0=ALU.mult, op1=ALU.add), reads=["tmpA", "lbv"], writes=["f"])
            yield
            if not state_only:
                qt, At, gs, kt = D_["qt"], D_["At"], B["gs"][gi], B["kt"]
                S.op("act", lambda e: e.activation(out=qs[:, :], in_=psq[:, :], func=AF.Silu), reads=[pkq, "f"], writes=["tmpA"])
                yield
                S.op("act", lambda e: e.activation(out=gs[:, :], in_=psg[:, :], func=AF.Silu), reads=[pkg], writes=[("gs", gi)])
                yield
            S.op("dve", lambda e: e.tensor_tensor(out=gbuf[:, :], in0=fbuf[:, :], in1=smask[:, 0:HT], op=ALU.mult), reads=["f", "smask"], writes=["rP"])
            yield
            S.op("dve", lambda e: e.tensor_tensor_scan(out=Pbuf[:, :], data0=fbuf[:, :], data1=gbuf[:, :], initial=1.0, op0=ALU.mult, op1=ALU.max),
                 reads=["f", "rP"], writes=["P"])
            yield
            S.op("act", lambda e: e.activation(out=rP[:, :], in_=Pbuf[:, :], func=AF.Ln), reads=["P"], writes=["rP"])
            yield
            S.op("act", lambda e: e.activation(out=rP[:, :], in_=rP[:, :], func=AF.Exp, scale=-1.0), reads=["rP"], writes=["rP"])
            yield
            S.op("dve", lambda e: e.tensor_copy(out=Pl[:, :], in_=Pbuf[:, :].rearrange("p (c t) -> p c t", c=NCH)[:, :, 63]), reads=["P"], writes=[("Pl", si)])
            yield
            if not state_only:
                S.op("dve", lambda e: e.tensor_tensor(out=qt[:, :], in0=qs[:, :], in1=Pbuf[:, :], op=ALU.mult), reads=["tmpA", "P"], writes=[("qt", si)])
                yield
            S.op("dve", lambda e: e.tensor_scalar(out=fbuf[:, :], in0=fbuf[:, :], scalar1=-1.0, scalar2=1.0, op0=ALU.mult, op1=ALU.add),
                 reads=["f", "P"], writes=["f"])
            yield
            S.op("dve", lambda e: e.tensor_tensor(out=rP[:, :], in0=rP[:, :], in1=fbuf[:, :], op=ALU.mult), reads=["rP", "f"], writes=["rP"])
            yield
            if not state_only or CROSS_CORE:
                kt = B["kt"]
                S.op("act", lambda e: e.activation(out=kt[:, :], in_=rP[:, :], func=AF.Copy), reads=["rP"], writes=["kt"])
                yield
            if state_only and CROSS_CORE:
                S.dma("sp", sp_kt[item], kt[:, :], reads=["kt"])
                S.dma("sp", sp_P[item], Pbuf[:, :], reads=["P"])
                S.dma("sp", sp_Pl[item], Pl[:, :], reads=[("Pl", si)])
                yield
            S.op("dve", lambda e: e.tensor_tensor(out=kd[:, :].rearrange("p (c t) -> p c t", c=NCH), in0=rP[:, :].rearrange("p (c t) -> p c t", c=NCH),
                                                   in1=Pbuf[:, :].rearrange("p (c t) -> p c t", c=NCH)[:, :, 63:64].to_broadcast([128, NCH, 64]), op=ALU.mult),
                 reads=["rP", "P"], writes=["kd"])
            yield
            for c4 in range(NCH // 4):
                ps, pk = bank()

                def fv(e, c4=c4, ps=ps):
                    for j in range(4):
                        c = c4 * 4 + j
                        o = t0 + c * 64
                        for k in range(8):
                            i = e.matmul(ps[0:64, j * 128:(j + 1) * 128], lhsT=hb[:, k, o:o + 64], rhs=wi[:, k, :], start=(k == 0), stop=(k == 7))
                    return i
                S.op("pe", fv, reads=[ki] + hb_keys(t0 + c4 * 256), writes=[pk])
                yield
                S.op("act", lambda e, c4=c4, ps=ps: e.activation(out=vt[0:64, c4 * 4:(c4 + 1) * 4, :], in_=ps[0:64, :].rearrange("p (c k) -> p c k", c=4), func=AF.Copy),
                     reads=[pk], writes=[("vt", si)])
                yield
            ps, pk = bank()
            psb = ps[:].bitcast(BF16)

            def ft(e, psb=psb):
                for j in range(8):
                    i = e.transpose(out=psb[0:64, j * 128:(j + 1) * 128], in_=kd[:, j * 64:(j + 1) * 64], identity=ident[:])
                return i
            S.op("pe", ft, reads=["kd", "ident"], writes=[pk])
            yield
            S.op("act", lambda e, psb=psb: e.activation(out=kdT[0:64, :, :], in_=psb[0:64, :].rearrange("p (c k) -> p c k", c=8), func=AF.Copy),
                 reads=[pk], writes=[("kdT", si)])
            yield
            if state_only and CROSS_CORE:
                S.dma("sp", sp_kdT[item].rearrange("p (c k) -> p c k", c=NCH), kdT[0:64, :, :], reads=[("kdT", si)])
                S.dma("sp", sp_vt[item].rearrange("p (c k) -> p c k", c=NCH), vt[0:64, :, :], reads=[("vt", si)])
                yield
            if not state_only:
                ps, pk = bank()

                def fs(e, ps=ps):
                    for j in range(8):
                        i = e.matmul(ps[0:64, j * 64:(j + 1) * 64], lhsT=kt[:, j * 64:(j + 1) * 64], rhs=qt[:, j * 64:(j + 1) * 64], start=True, stop=True)
                    return i
                S.op("pe", fs, reads=["kt", ("qt", si)], writes=[pk])
                yield
                S.op("dve", lambda e, ps=ps: e.tensor_tensor(out=At[0:64, :, :], in0=ps[0:64, :].rearrange("p (c t) -> p c t", c=8),
                                                             in1=cmask[:, None, :].to_broadcast([64, 8, 64]), op=ALU.mult),
                     reads=[pk, "cmask"], writes=[("At", si)])
                yield

        def hg_B(h, hh, state_only, B, si, oi=0, Pl_ap=None, Pl_key=None):
            t0 = hh * HT
            D_ = B["sets"][si]
            kdT, vt, Pl = D_["kdT"], D_["vt"], D_["Pl"]
            plk = ("Pl", si)
            if Pl_ap is not None:
                Pl, plk = Pl_ap, Pl_key
            if not state_only:
                qt, At = D_["qt"], D_["At"]
                oh = B["oh"][oi]
                if hh == 0:
                    S.op("act", lambda e: e.activation(out=Sbf[:, 0, :], in_=Sst[:, h, :], func=AF.Copy), reads=[("S", h)], writes=[("Sbf", 0)])
                    yield
            pso = None
            psS_of = []
            for c4 in range(NCH // 4):
                psS, pkS = pbank[5 + c4], ("pb", 5 + c4)

                def fS(e, c4=c4, psS=psS):
                    for j in range(4):
                        c = c4 * 4 + j
                        i = e.matmul(psS[:, j * 128:(j + 1) * 128], lhsT=kdT[0:64, c, :], rhs=vt[0:64, c, :], start=True, stop=True)
                    return i
                S.op("pe", fS, reads=[("kdT", si), ("vt", si)], writes=[pkS])
                yield
                for j in range(4):
                    psS_of.append((psS[:, j * 128:(j + 1) * 128], pkS))
            for c in range(NCH):
                psS_ap, pkS = psS_of[c]
                if not state_only:
                    if c % 8 == 0:
                        pso, pko = hold_bank()
                    sb_i = c % 4

                    def fo(e, c=c, pso=pso, sb_i=sb_i):
                        e.matmul(pso[:, (c % 8) * 64:(c % 8 + 1) * 64], lhsT=vt[0:64, c, :], rhs=At[0:64, c, :], start=True, stop=False)
                        return e.matmul(pso[:, (c % 8) * 64:(c % 8 + 1) * 64], lhsT=Sbf[:, sb_i, :], rhs=qt[:, c * 64:(c + 1) * 64], start=False, stop=True)
                    S.op("pe", fo, reads=[("vt", si), ("At", si), ("Sbf", sb_i), ("qt", si)], writes=[pko])
                    yield
                    if c % 8 == 7:
                        S.op("act", lambda e, pso=pso: e.activation(out=oh[:, :], in_=pso[:, :], func=AF.Copy), reads=[pko], writes=[("oh", oi)])
                        yield
                S.op("dve", lambda e, c=c, psS_ap=psS_ap: e.scalar_tensor_tensor(out=Sst[:, h, :], in0=Sst[:, h, :], scalar=Pl[:, c:c + 1], in1=psS_ap,
                                                                               op0=ALU.mult, op1=ALU.add), reads=[pkS, ("S", h), plk], writes=[("S", h)])
                yield
                if not state_only:
                    nb = (c + 1) % 4
                    S.op("dve", lambda e, nb=nb: e.tensor_copy(out=Sbf[:, nb, :], in_=Sst[:, h, :]), reads=[("S", h)], writes=[("Sbf", nb)])
                    yield
            if state_only:
                S.op("dve", lambda e: e.tensor_tensor_scan(out=ptmp[:, 0:NCH], data0=Pl[:, :], data1=zer64[:, 0:NCH],
                                                           initial=Ptot[:, h:h + 1], op0=ALU.mult, op1=ALU.add),
                     reads=[plk, "zer64", ("Ptot", h), "ptmp"], writes=["ptmp"])
                yield
                S.op("dve", lambda e: e.tensor_copy(out=Ptot[:, h:h + 1], in_=ptmp[:, NCH - 1:NCH]), reads=["ptmp"], writes=[("Ptot", h)])
                yield

        def hg_C(h, hh, B, oi, gi):
            if True:
                off = hh * HT
                oh, gs, rst, sqh = B["oh"][oi], B["gs"][gi], B["rst"], B["sqh"]
                S.op("act", lambda e: e.activation(out=sqh[:, :], in_=oh[:, :], func=AF.Square), reads=[("oh", oi)], writes=["sqh"])
                yield
                ps, pk = bank()
                S.op("pe", lambda e, ps=ps: e.matmul(ps[:, :], lhsT=ones[:], rhs=sqh[:, :], start=True, stop=True), reads=["ones", "sqh"], writes=[pk])
                yield
                S.op("dve", lambda e, ps=ps: e.tensor_scalar(out=rst[:, :], in0=ps[:, :], scalar1=1.0 / 128, scalar2=EPS, op0=ALU.mult, op1=ALU.add), reads=[pk], writes=["rst"])
                yield
                S.op("act", lambda e: e.activation(out=rst[:, :], in_=rst[:, :], func=AF.Ln), reads=["rst"], writes=["rst"])
                yield
                S.op("act", lambda e: e.activation(out=rst[:, :], in_=rst[:, :], func=AF.Exp, scale=-0.5), reads=["rst"], writes=["rst"])
                yield
                S.op("dve", lambda e: e.scalar_tensor_tensor(out=oh[:, :], in0=oh[:, :], scalar=vec[:, V_ONORM + h:V_ONORM + h + 1], in1=rst[:, :],
                                                             op0=ALU.mult, op1=ALU.mult), reads=[("oh", oi), "vec", "rst"], writes=[("oh", oi)])
                yield
                S.op("dve", lambda e: e.tensor_tensor(out=ob[:, h, off:off + 512], in0=oh[:, :], in1=gs[:, :], op=ALU.mult),
                     reads=[("oh", oi), ("gs", gi)], writes=[("ob", h, off // 512)])
                yield

        def hgrn_prepass(ar0):
            ar.off = ar0
            fb = [ar.f32(HT) for _ in range(2)]
            Pb = [ar.f32(HT) for _ in range(2)]
            gbuf = ar.f32(HT)
            rP = ar.f32(HT)
            tmpA = ar.f32(512)
            kd = ar.bf16(HT)
            kt = ar.bf16(HT)
            Pls = [ar.f32(NCH) for _ in range(3)]
            Bd = {"sets": [{"kdT": ar.bf16(NCH * 128).rearrange("p (c k) -> p c k", c=NCH),
                            "vt": ar.bf16(NCH * 128).rearrange("p (c k) -> p c k", c=NCH), "Pl": None} for _ in range(2)]}
            items = [(h, hh) for h in range(8) for hh in range(NPART)]
            wts_of = {}

            def A0(i):
                h, hh = items[i]
                if i == 0:
                    wts_of[0] = head_weights(0, True)
                if hh == 1 and h + 1 < 8:
                    wts_of[h + 1] = head_weights(h + 1, True)
                wf, kf = wts_of[h][2], wts_of[h][3]
                j = i % 2
                Pl = Pls[i % 3]
                off = hh * HT
                psf, pkf = proj(wf, kf, 8, 0, hb_rhs(off, 512), hb_keys(off), 512)
                yield
                S.op("act", lambda e: e.activation(out=tmpA[:, :], in_=psf[:, :], func=AF.Tanh, scale=0.5), reads=[pkf], writes=["tmpA"])
                yield
                S.op("dve", lambda e: e.tensor_scalar(out=fb[j][:, :], in0=tmpA[:, :], scalar1=lbv[:, 16 + h:17 + h], scalar2=lbv[:, 24 + h:25 + h],
                                                      op0=ALU.mult, op1=ALU.add), reads=["tmpA", "lbv"], writes=[("f", j)])
                yield
                S.op("dve", lambda e: e.tensor_tensor(out=gbuf[:, :], in0=fb[j][:, :], in1=smask[:, 0:HT], op=ALU.mult), reads=[("f", j), "smask"], writes=["g"])
                yield
                S.op("dve", lambda e: e.tensor_tensor_scan(out=Pb[j][:, :], data0=fb[j][:, :], data1=gbuf[:, :], initial=1.0, op0=ALU.mult, op1=ALU.max),
                     reads=[("f", j), "g"], writes=[("P", j)])
                yield
                S.op("dve", lambda e: e.tensor_copy(out=Pl[:, :], in_=Pb[j][:, :].rearrange("p (c t) -> p c t", c=NCH)[:, :, 63]), reads=[("P", j)], writes=[("Pl3", i % 3)])
                yield

            def A1(i):
                h, hh = items[i]
                wi, ki = wts_of[h][4], wts_of[h][5]
                j = i % 2
                si = i % 2
                Pl = Pls[i % 3]
                kdT, vt = Bd["sets"][si]["kdT"], Bd["sets"][si]["vt"]
                t0 = hh * HT
                S.op("dve", lambda e: e.reciprocal(out=rP[:, :], in_=Pb[j][:, :]), reads=[("P", j)], writes=["rP"])
                yield
                S.op("dve", lambda e: e.tensor_scalar(out=fb[j][:, :], in0=fb[j][:, :], scalar1=-1.0, scalar2=1.0, op0=ALU.mult, op1=ALU.add),
                     reads=[("f", j)], writes=[("f", j)])
                yield
                S.op("dve", lambda e: e.tensor_tensor(out=rP[:, :], in0=rP[:, :], in1=fb[j][:, :], op=ALU.mult), reads=["rP", ("f", j)], writes=["rP"])
                yield
                S.op("act", lambda e: e.activation(out=kt[:, :], in_=rP[:, :], func=AF.Copy), reads=["rP"], writes=["kt"])
                yield
                S.dma("sp", sp_kt[i], kt[:, :], reads=["kt"])
                S.dma("sp", sp_P[i], Pb[j][:, :], reads=[("P", j)])
                S.dma("sp", sp_Pl[i], Pl[:, :], reads=[("Pl3", i % 3)])
                yield
                S.op("dve", lambda e: e.tensor_tensor(out=kd[:, :].rearrange("p (c t) -> p c t", c=NCH), in0=rP[:, :].rearrange("p (c t) -> p c t", c=NCH),
                                                       in1=Pb[j][:, :].rearrange("p (c t) -> p c t", c=NCH)[:, :, 63:64].to_broadcast([128, NCH, 64]), op=ALU.mult),
                     reads=["rP", ("P", j)], writes=["kd"])
                yield
                for c4 in range(NCH // 4):
                    ps, pk = bank()

                    def fv(e, c4=c4, ps=ps):
                        for jj in range(4):
                            c = c4 * 4 + jj
                            o = t0 + c * 64
                            for k in range(8):
                                ins = e.matmul(ps[0:64, jj * 128:(jj + 1) * 128], lhsT=hb[:, k, o:o + 64], rhs=wi[:, k, :], start=(k == 0), stop=(k == 7))
                        return ins
                    S.op("pe", fv, reads=[ki] + hb_keys(t0 + c4 * 256), writes=[pk])
                    yield
                    S.op("act", lambda e, c4=c4, ps=ps: e.activation(out=vt[0:64, c4 * 4:(c4 + 1) * 4, :], in_=ps[0:64, :].rearrange("p (c k) -> p c k", c=4), func=AF.Copy),
                         reads=[pk], writes=[("vt", si)])
                    yield
                ps, pk = bank()
                psb = ps[:].bitcast(BF16)

                def ft(e, psb=psb):
                    for jj in range(8):
                        ins = e.transpose(out=psb[0:64, jj * 128:(jj + 1) * 128], in_=kd[:, jj * 64:(jj + 1) * 64], identity=ident[:])
                    return ins
                S.op("pe", ft, reads=["kd", "ident"], writes=[pk])
                yield
                S.op("act", lambda e, psb=psb: e.activation(out=kdT[0:64, :, :], in_=psb[0:64, :].rearrange("p (c k) -> p c k", c=8), func=AF.Copy),
                     reads=[pk], writes=[("kdT", si)])
                yield
                S.dma("sp", sp_kdT[i].rearrange("p (c k) -> p c k", c=NCH), kdT[0:64, :, :], reads=[("kdT", si)])
                S.dma("sp", sp_vt[i].rearrange("p (c k) -> p c k", c=NCH), vt[0:64, :, :], reads=[("vt", si)])
                yield

            n = len(items)
            run_interleaved([("A0", A0(0))])
            run_interleaved([("A", A1(0)), ("A0", A0(1))])
            for i in range(n):
                h, hh = items[i]
                run_interleaved([("B", hg_B(h, hh, True, Bd, i % 2, 0, Pls[i % 3], ("Pl3", i % 3))),
                                 ("A", A1(i + 1) if i + 1 < n else None), ("A0", A0(i + 2) if i + 2 < n else None)])

        def hgrn_pass(state_only, ar0):
            B = hg_alloc(state_only, ar0)
            items = [(h, hh) for h in range(8) for hh in range(NPART)]
            wts_of = {}

            def genA(i):
                h, hh = items[i]
                if i == 0:
                    wts_of[0] = head_weights(0, state_only)
                if hh == 1 and h + 1 < 8:
                    wts_of[h + 1] = head_weights(h + 1, state_only)
                if CROSS_CORE and not state_only:
                    return hg_A2(h, hh, wts_of[h], B, i % 2, i % 3, i)
                return hg_A(h, hh, wts_of[h], state_only, B, i % 2, i % 3, i)

            def genC(i):
                if state_only or i < 0:
                    return None
                h, hh = items[i]
                return hg_C(h, hh, B, i % 2, i % 3)

            run_interleaved([("A", genA(0))])
            for i in range(len(items)):
                h, hh = items[i]
                run_interleaved([("B", hg_B(h, hh, state_only, B, i % 2, i % 2)), ("C", genC(i - 1)), ("A", genA(i + 1) if i + 1 < len(items) else None)])
            run_interleaved([("C", genC(len(items) - 1))])

        zer64 = ar.f32(64)
        S.op("dve", lambda e: e.memset(zer64[:, :], 0.0), writes=["zer64"])
        smask = ar.f32(512)
        ptmp = ar.f32(8)
        S.dma("sp", smask, smask_d, writes=["smask"])
        work1 = ar.off
        sq = ar.bf16(8 * 512).rearrange("p (j t) -> p j t", j=8)
        rstd = ar.f32(512)
        norm_to_hb(BLKH, V_MIX, sq, rstd)
        S.fence()

        def head_weights(h, state_only):
            if CROSS_CORE and not state_only:
                wq, kq = wtile("w_in", 0, 8, h * 128, 128)
                wog, kog = wtile("w_in", 0, 8, 3072 + h * 128, 128)
                return (wq, kq, None, None, None, None, wog, kog)
            wf, kf = wtile("w_in", 0, 8, 1024 + h * 128, 128)
            wi, ki = wtile("w_in", 0, 8, 2048 + h * 128, 128)
            if state_only:
                return (None, None, wf, kf, wi, ki, None, None)
            wq, kq = wtile("w_in", 0, 8, h * 128, 128)
            wog, kog = wtile("w_in", 0, 8, 3072 + h * 128, 128)
            return (wq, kq, wf, kf, wi, ki, wog, kog)

        if CROSS_CORE:
            S.op("dve", lambda e: e.memset(Sst[:], 0.0), writes=[("S", h) for h in range(8)])
            S.op("dve", lambda e: e.memset(Ptot[:], 1.0), writes=[("Ptot", h) for h in range(8)])
            hgrn_prepass(work1)
            if dbg == "p1a":
                S.fence()
                return finish_with_x()
            t_a = S.dma("sp", st_loc[:, 0:1024], Sst[:].rearrange("p h k -> p (h k)"), reads=[("S", h) for h in range(8)])
            t_b = S.dma("sp", st_loc[:, 1024:1032], Ptot[:], reads=[("Ptot", h) for h in range(8)])
            S.wait_tok("pool", t_a)
            S.wait_tok("pool", t_b)

        else:
            S.op("dve", lambda e: e.memset(Sst[:], 0.0), writes=[("S", h) for h in range(8)])

        def pool_weights():
            tiles = [wload(W["pool_w"].rearrange("(g p) n -> p g n", p=128), 4, 128)]
            for g in range(4):
                tiles.append(wtile("w_in", 0, 8, 4096 + g * 128, 128))
            return tiles

        def pool_branch(ar0, tiles):
            ar.off = ar0
            pr = ar.f32(TT)
            tA = ar.f32(TT)
            tB = ar.f32(TT)
            pl = ar.bf16(NT)
            t16 = ar.f32(16)
            pw, kpw = tiles[0]
            for g in range(4):
                wsz = 2 ** (g + 1)
                wp, kp = tiles[1 + g]
                for (off, nn) in BLKH:
                    dst = 0 if off >= NT else HALO + off
                    ps, pk = proj(wp, kp, 8, 0, hb_rhs(off, nn), hb_keys(off), nn)
                    S.op("act", lambda e, ps=ps, nn=nn, dst=dst: e.activation(out=pr[:, dst:dst + nn], in_=ps[:, 0:nn], func=AF.Copy), reads=[pk], writes=["pr"])
                src = pr
                bufs = [tA, tB]
                sh = 1
                bi = 0
                while sh < wsz:
                    dstb = bufs[bi]
                    lo = 2 * sh - 1
                    S.op("dve", lambda e, src=src, dstb=dstb, sh=sh, lo=lo: e.tensor_tensor(out=dstb[:, lo:TT], in0=src[:, lo:TT], in1=src[:, lo - sh:TT - sh], op=ALU.add),
                         reads=["pr", "tA", "tB"], writes=["tA" if bi == 0 else "tB"])
                    src = dstb
                    sh *= 2
                    bi ^= 1
                S.op("dve", lambda e, src=src, wsz=wsz: e.scalar_tensor_tensor(out=pl[:, :], in0=src[:, HALO:TT], scalar=1.0 / wsz, in1=pr[:, HALO:TT], op0=ALU.mult, op1=ALU.subtract),
                     reads=["pr", "tA", "tB"], writes=["pl"])
                S.op("dve", lambda e, src=src, g=g: e.tensor_tensor(out=t16[:, :], in0=src[:, HALO:2 * HALO], in1=invc[:, g * 16:(g + 1) * 16], op=ALU.mult),
                     reads=["tA", "tB", "invc"], writes=["t16"])
                S.op("dve", lambda e: e.tensor_tensor(out=pl[:, 0:16], in0=t16[:, :], in1=pr[:, HALO:2 * HALO], op=ALU.subtract),
                     reads=["t16", "pr", "pl"], writes=["pl"])
                for (off, nn) in BLK:
                    ps, pk = bank()
                    S.op("pe", lambda e, ps=ps, g=g, off=off: e.matmul(ps[:, :], lhsT=pw[:, g, :], rhs=pl[:, off:off + 512], start=True, stop=True),
                         reads=[kpw, "pl"], writes=[pk])
                    S.op("act", lambda e, ps=ps, g=g, off=off: e.activation(out=mixed[:, g, off:off + 512], in_=ps[:, :], func=AF.Copy, scale=vec[:, V_PSCALE + g:V_PSCALE + g + 1]),
                         reads=[pk, "vec"], writes=[("mixed", g, off // 512)])

        S.fence()
        ptiles = pool_weights()
        if CROSS_CORE:
            cc_tok = S.collective_on_pool(lambda e: e.collective_compute(
                "AllGather", ALU.bypass, replica_groups=[[0, 1, 2, 3], [4, 5, 6, 7]], ins=[st_loc.opt()], outs=[st_all.opt()]), S.free_sems.pop(),
                {"pe": lambda e: e.matmul(pbank[7][0:1, 0:1], lhsT=ones[:, 0:1], rhs=ones[:, 0:1], start=True, stop=True),
                 "act": lambda e: e.activation(out=mark[:, 0:1], in_=ones[:, 0:1], func=AF.Copy),
                 "dve": lambda e: e.memset(mark[:, 1:2], 0.0)},
                lambda e: e.memset(mark2[:, :], 0.0))
        pool_branch(work1, ptiles)
        S.fence()

        if CROSS_CORE:
            ar.off = work1
            gat = ar.f32(4 * 1032).rearrange("p (r w) -> p r w", r=4)
            sel = ar.f32(4)
            acc = ar.f32(1024)
            S.dma("sp", sel[:, :], sel_d, writes=["sel"])
            S.wait_tok("sp", cc_tok)
            S.op("dve", lambda e: e.memset(Sst[:], 0.0), reads=[("S", h) for h in range(8)], writes=[("S", h) for h in range(8)] + ["Sin"])
            Sin = Sst
            for j in range(3):
                gj = gat[:, j % 4, :]
                S.dma("sp", gj, st_all[j * 128:(j + 1) * 128, :], writes=[("gat", j % 4)])
                for h in range(8):
                    S.op("dve", lambda e, gj=gj, h=h: e.scalar_tensor_tensor(out=acc[:, h * 128:(h + 1) * 128], in0=Sin[:, h, :], scalar=gj[:, 1024 + h:1025 + h],
                                                                             in1=gj[:, h * 128:(h + 1) * 128], op0=ALU.mult, op1=ALU.add),
                         reads=[("gat", j % 4), "Sin"], writes=["acc"])
                S.op("dve", lambda e: e.tensor_tensor(out=acc[:, :], in0=acc[:, :], in1=Sin[:].rearrange("p h k -> p (h k)"), op=ALU.subtract),
                     reads=["acc", "Sin"], writes=["acc"])
                S.op("dve", lambda e, j=j: e.scalar_tensor_tensor(out=Sin[:].rearrange("p h k -> p (h k)"), in0=acc[:, :], scalar=sel[:, j:j + 1],
                                                                  in1=Sin[:].rearrange("p h k -> p (h k)"), op0=ALU.mult, op1=ALU.add),
                     reads=["acc", "Sin", "sel"], writes=["Sin"] + ([("S", h) for h in range(8)] if j == 2 else []))

        S.fence()
        if dbg == "cc":
            return finish_with_x()
        hgrn_pass(False, work1)

        S.fence()
        ar.off = work0
        yb = ar.bf16(8 * NT).rearrange("p (m t) -> p m t", m=8)
        sga = ar.f32(512)
        sgb = ar.f32(512)
        for m in range(8):
            wga, kga = wtile("w_in", 0, 8, 4608 + m * 128, 128)
            wa, ka = wtile("w_branch_a", 0, 8, m * 128, 128)
            for (off, nn) in BLK:
                b = off // 512
                pga, kpga = proj(wga, kga, 8, 0, hb_rhs(off, 512), hb_keys(off), 512)
                S.op("act", lambda e, pga=pga: e.activation(out=sga[:, :], in_=pga[:, :], func=AF.Tanh, scale=0.5), reads=[kpga], writes=["sga"])
                pya, kpya = proj(wa, ka, 8, 0, lambda k, off=off: ob[:, k, off:off + 512], [("ob", hh_, b) for hh_ in range(8)], 512)
                S.op("dve", lambda e, pya=pya, m=m, off=off: e.scalar_tensor_tensor(out=yb[:, m, off:off + 512], in0=sga[:, :], scalar=1.0, in1=pya[:, :], op0=ALU.add, op1=ALU.mult),
                     reads=["sga", kpya], writes=[("yb", m, b)])
            wgb, kgb = wtile("w_in", 0, 8, 5632 + m * 128, 128)
            wbb, kbb = wtile("w_branch_b", 0, 4, m * 128, 128)
            for (off, nn) in BLK:
                b = off // 512
                pgb, kpgb = proj(wgb, kgb, 8, 0, hb_rhs(off, 512), hb_keys(off), 512)
                S.op("act", lambda e, pgb=pgb: e.activation(out=sgb[:, :], in_=pgb[:, :], func=AF.Tanh, scale=0.5), reads=[kpgb], writes=["sgb"])
                pyb, kpyb = proj(wbb, kbb, 4, 0, lambda k, off=off: mixed[:, k, off:off + 512], [("mixed", g, b) for g in range(4)], 512)
                S.op("dve", lambda e, pyb=pyb: e.scalar_tensor_tensor(out=sgb[:, :], in0=sgb[:, :], scalar=1.0, in1=pyb[:, :], op0=ALU.add, op1=ALU.mult), reads=["sgb", kpyb], writes=["sgb"])
                S.op("dve", lambda e, m=m, off=off: e.tensor_tensor(out=yb[:, m, off:off + 512], in0=yb[:, m, off:off + 512], in1=sgb[:, :], op=ALU.add),
                     reads=["sgb", ("yb", m, b)], writes=[("yb", m, b)])
        for m in range(8):
            wo, ko = wtile("w_out", 0, 8, m * 128, 128)
            for (off, nn) in BLK:
                b = off // 512
                ps, pk = proj(wo, ko, 8, 0, lambda k, off=off: yb[:, k, off:off + 512], [("yb", k, b) for k in range(8)], 512)
                S.op("dve", lambda e, ps=ps, m=m, off=off: e.scalar_tensor_tensor(out=xT[:, m, off:off + 512], in0=ps[:, :], scalar=0.5, in1=xT[:, m, off:off + 512],
                                                                              op0=ALU.mult, op1=ALU.add),
                     reads=[pk, ("x", m, b)], writes=[("x", m, b)])

        S.fence()
        if dbg == "mix":
            return finish_with_x()
        ffn("ffn2_w1", "ffn2_w3", "ffn2_w2", V_FFN2, BLK)
        S.fence()
        if dbg == "ffn2":
            return finish_with_x()

        ar = Arena()
        pbf = ar.bf16(2 * NT).rearrange("p (c t) -> p c t", c=2)
        eraw = [ar.f32(8 * 512).rearrange("p (m t) -> p m t", m=8) for _ in range(2)]
        sq = ar.bf16(2 * 512).rearrange("p (j t) -> p j t", j=2)
        rstd = ar.f32(512)
        rse = [ar.f32(512) for _ in range(2)]
        sq8 = ar.bf16(8 * 512).rearrange("p (j t) -> p j t", j=8)
        sg = ar.f32(2 * 512).rearrange("p (j t) -> p j t", j=2)
        tt_ = ar.f32(2 * 512).rearrange("p (j t) -> p j t", j=2)
        ostg = ar.f32(8 * 512).rearrange("p (m t) -> p m t", m=8)
        pstage = ostg[:, :, :].rearrange("p m t -> p (m t)").rearrange("p (c t) -> p c t", c=2)
        S.dma("sp", pstage, pT_d.rearrange("(c p) t -> p c t", p=128), writes=["ostg"])
        for c in range(2):
            S.op("pool", lambda e, c=c: e.tensor_copy(out=pbf[:, c, :], in_=pstage[:, c, :]), reads=["ostg"], writes=["pbf"])
        norm_to_hb(BLK, V_PLE, sq8, rstd)
        out_toks = []
        def ple_e(pair):
            wpj0, kpj0 = wtile("ple_w_proj", 0, 2, 0, 512)
            wpj1, kpj1 = wtile("ple_w_proj", 0, 2, 512, 512)
            for bi, b in enumerate(pair):
                off = 512 * b
                pss, pkss = hold_bank()
                for m in range(8):
                    ps, pk = proj(wpj0 if m < 4 else wpj1, kpj0 if m < 4 else kpj1, 2, (m % 4) * 128, lambda k, off=off: pbf[:, k, off:off + 512], ["pbf"], 512)
                    S.op("act", lambda e, ps=ps, m=m, bi=bi: e.activation(out=eraw[bi][:, m, :], in_=ps[:, :], func=AF.Copy), reads=[pk], writes=[("eraw", bi, m)])
                    S.op("act", lambda e, ps=ps, m=m: e.activation(out=sq[:, m % 2, :], in_=ps[:, :], func=AF.Square), reads=[pk], writes=[("sq", m % 2)])
                    S.op("pe", lambda e, m=m, pss=pss: e.matmul(pss[:, :], lhsT=ones[:], rhs=sq[:, m % 2, :], start=(m == 0), stop=(m == 7)),
                         reads=["ones", ("sq", m % 2)], writes=[pkss])
                rstd_from_psum(pss, pkss, 512, rse[bi], ("rse", bi), 1.0 / D)

        def ple_g(pair):
            for m in range(8):
                wg, kg = wtile("ple_w_gate", 0, 8, m * 128, 128)
                for bi, b in enumerate(pair):
                    off = 512 * b
                    ps, pk = proj(wg, kg, 8, 0, hb_rhs(off, 512), hb_keys(off), 512)
                    S.op("act", lambda e, ps=ps, bi=bi: e.activation(out=sg[:, bi, :], in_=ps[:, :], func=AF.Sigmoid), reads=[pk], writes=[("sg", bi)])
                    S.op("dve", lambda e, m=m, bi=bi: e.scalar_tensor_tensor(out=tt_[:, bi, :], in0=eraw[bi][:, m, :], scalar=vec[:, V_POST + m:V_POST + m + 1], in1=rse[bi][:, :],
                                                                             op0=ALU.mult, op1=ALU.mult), reads=[("eraw", bi, m), "vec", ("rse", bi)], writes=[("tt", bi)])
                    S.op("dve", lambda e, bi=bi: e.tensor_tensor(out=tt_[:, bi, :], in0=tt_[:, bi, :], in1=sg[:, bi, :], op=ALU.mult),
                         reads=[("tt", bi), ("sg", bi)], writes=[("tt", bi)])
                    S.op("dve", lambda e, m=m, off=off, bi=bi: e.tensor_tensor(out=xT[:, m, off:off + 512], in0=xT[:, m, off:off + 512], in1=tt_[:, bi, :], op=ALU.add),
                         reads=[("tt", bi), ("x", m, b)], writes=[("x", m, b)])

        def ple_f(pair):
            for bi, b in enumerate(pair):
                off = 512 * b
                ps, psk = bank()
                for c in range(8):
                    S.op("act", lambda e, c=c, off=off: e.activation(out=sq8[:, c, :], in_=xT[:, c, off:off + 512], func=AF.Square),
                         reads=[("x", c, b)], writes=[("sq8", c)])
                for c in range(8):
                    S.op("pe", lambda e, c=c, ps=ps: e.matmul(ps[:, :], lhsT=ones[:], rhs=sq8[:, c, :], start=(c == 0), stop=(c == 7)),
                         reads=["ones", ("sq8", c)], writes=[psk])
                rstd_from_psum(ps, psk, 512, rstd, "rstd", 1.0 / D)
                for c in range(8):
                    S.op("dve", lambda e, c=c, off=off: e.scalar_tensor_tensor(out=ostg[:, c, :], in0=xT[:, c, off:off + 512], scalar=vec[:, V_FINAL + c:V_FINAL + c + 1], in1=rstd[:, :],
                                                                               op0=ALU.mult, op1=ALU.mult), reads=[("x", c, b), "vec", "rstd"], writes=["ostg"])
                out_toks.append(S.dma("sp", outT_d.rearrange("(c p) t -> p c t", p=128)[:, :, off:off + 512], ostg[:, :, :], reads=["ostg"]))

        ple_e([0, 1])
        ple_g([0, 1])
        ple_e([2, 3])
        ple_f([0, 1])
        ple_g([2, 3])
        ple_f([2, 3])
        for t in out_toks:
            S.wait_tok("sp", t)
        S.emit()
    return nc


_DBG = None


def _pm(v, n):
    return np.ascontiguousarray(np.asarray(v, np.float32).reshape(n, 128).T)


def kernel(x, p, ffn1_norm, ffn1_w1, ffn1_w3, ffn1_w2, mix_norm, w_in, hgrn_lb, hgrn_onorm,
           w_branch_a, pool_w, pool_scale, w_branch_b, w_out, ffn2_norm, ffn2_w1, ffn2_w3,
           ffn2_w2, ple_norm, ple_w_gate, ple_w_proj, ple_post_norm, final_norm):
    f = lambda a: np.ascontiguousarray(np.asarray(a, np.float32))
    x = f(x)
    p = f(p)
    vecs = np.concatenate([
        _pm(ffn1_norm[0], 8), _pm(mix_norm[0], 8), _pm(ffn2_norm[0], 8), _pm(ple_norm[0], 8), _pm(ple_post_norm[0], 8),
        _pm(final_norm, 8), _pm(hgrn_lb[0], 8), _pm(hgrn_lb[1], 8), _pm(hgrn_onorm[0], 8), _pm(pool_scale[0], 4)], axis=1)
    vecs = np.ascontiguousarray(vecs)
    assert vecs.shape == (128, NV)
    shared = {
        "vecs": vecs,
        "cmask": np.triu(np.ones((64, 64), np.float32)),
        "smask": np.ascontiguousarray(np.tile((np.arange(512) % 64 == 0).astype(np.float32)[None, :], (128, 1))),
        "ffn1_w1": f(ffn1_w1[0]), "ffn1_w3": f(ffn1_w3[0]), "ffn1_w2": f(ffn1_w2[0]),
        "ffn2_w1": f(ffn2_w1[0]), "ffn2_w3": f(ffn2_w3[0]), "ffn2_w2": f(ffn2_w2[0]),
        "w_in": f(w_in[0]), "w_branch_a": f(w_branch_a[0]), "w_branch_b": f(w_branch_b[0]),
        "pool_w": f(np.asarray(pool_w[0]).reshape(512, 128)), "w_out": f(w_out[0]),
        "ple_w_gate": f(ple_w_gate[0]), "ple_w_proj": f(ple_w_proj[0]),
    }
    in_maps = []
    for c in range(NCORES):
        b, j = divmod(c, 4)
        t0 = j * NT
        xT = np.zeros((D, TT), np.float32)
        xT[:, :NT] = x[b, t0:t0 + NT, :].T
        if j > 0:
            xT[:, NT:] = x[b, t0 - HALO:t0, :].T
        invc = np.zeros((128, 64), np.float32)
        for g, w in enumerate((2, 4, 8, 16)):
            pos = t0 + np.arange(16) + 1
            invc[:, g * 16:(g + 1) * 16] = (1.0 / np.minimum(pos, w)).astype(np.float32)[None, :]
        m = dict(shared)
        m["xT"] = xT
        m["pT"] = np.ascontiguousarray(p[0, b, t0:t0 + NT, :].T)
        m["invc"] = invc
        if CROSS_CORE:
            sel = np.zeros((128, 4), np.float32)
            for jj in range(4):
                if jj < j:
                    sel[:, jj] = 1.0
            m["sel"] = sel
        in_maps.append(m)
    nc = build_program(_DBG)
    res = run_bass_kernel_spmd(nc, in_maps, core_ids=list(range(NCORES)))
    out = np.empty((2, 8192, D), np.float32)
    for c in range(NCORES):
        b, j = divmod(c, 4)
        out[b, j * NT:(j + 1) * NT, :] = np.asarray(res.results[c]["outT"]).T
    return out
```

```python
import numpy as np
from contextlib import ExitStack
import concourse.bass as bass
import concourse.mybir as mybir
from concourse.bass_utils import run_bass_kernel_spmd

F32 = mybir.dt.float32
BF16 = mybir.dt.bfloat16
AF = mybir.ActivationFunctionType
ALU = mybir.AluOpType

D = 1024
NT = 2048
HALO = 16
TT = NT + HALO
DFF = 2816
NFC = DFF // 128
NCORES = 8
EPS = 1e-6
NSLOT = 5
SLOTW = 2048
RW = 21504
CROSS_CORE = True
CC_VARIANT = "full"

V_FFN1, V_MIX, V_FFN2, V_PLE, V_POST, V_FINAL, V_LB0, V_LB1, V_ONORM, V_PSCALE = 0, 8, 16, 24, 32, 40, 48, 56, 64, 72
NV = 76

BLK = [(0, 512), (512, 512), (1024, 512), (1536, 512)]
BLKH = BLK + [(NT, HALO)]


class Sched:
    ENG = ("pe", "act", "dve", "pool", "sp")

    def __init__(self, nc, stack, n_dma_sems=12, n_eng_sems=40, epoch=12000, same_engine_sync=True):
        self.nc = nc
        self.same = same_engine_sync
        self.epoch = epoch
        self.lists = {k: [] for k in self.ENG}
        self.free_sems = [stack.enter_context(nc.semaphore(f"es{i}")) for i in range(n_eng_sems)]
        self.dsems = {"sp": [stack.enter_context(nc.semaphore(f"dsp{i}")) for i in range(8)],
                      "pool": [stack.enter_context(nc.semaphore(f"dpa{i}")) for i in range(n_dma_sems)]}
        self.dsems_b = {"sp": [stack.enter_context(nc.semaphore(f"dsq{i}")) for i in range(8)],
                        "pool": [stack.enter_context(nc.semaphore(f"dpb{i}")) for i in range(n_dma_sems)]}
        self.dval = {q: [0] * len(v) for q, v in self.dsems.items()}
        self.drr = {q: 0 for q in self.dsems}
        self.cur = {k: [self.free_sems.pop(), 0] for k in self.ENG}
        self.seen = {k: {} for k in self.ENG}
        self.last_w = {}
        self.rd = {}

    def _need(self, eng, tok, waits):
        sem, val, src = tok
        if src == eng and not (self.same and eng != "pe"):
            return
        key = id(sem)
        if self.seen[eng].get(key, 0) >= val:
            return
        self.seen[eng][key] = val
        waits.append((sem, val))

    def _deps(self, eng, reads, writes):
        waits = []
        for k in reads:
            t = self.last_w.get(k)
            if t is not None:
                self._need(eng, t, waits)
        for k in writes:
            t = self.last_w.get(k)
            if t is not None:
                self._need(eng, t, waits)
            for t in self.rd.get(k, ()):
                self._need(eng, t, waits)
        return waits

    def _commit(self, tok, reads, writes):
        for k in reads:
            lst = self.rd.setdefault(k, [])
            lst.append(tok)
            if len(lst) > 16:
                best = {}
                for t in lst:
                    kk = id(t[0])
                    if kk not in best or best[kk][1] < t[1]:
                        best[kk] = t
                self.rd[k] = list(best.values())
        for k in writes:
            self.last_w[k] = tok
            self.rd[k] = []

    def op(self, eng, fn, reads=(), writes=()):
        waits = self._deps(eng, reads, writes)
        cur = self.cur[eng]
        if cur[1] >= self.epoch:
            cur = self.cur[eng] = [self.free_sems.pop(), 0]
        cur[1] += 1
        sem, val = cur[0], cur[1]
        lst = self.lists[eng]
        for (s, v) in waits:
            lst.append(lambda e, s=s, v=v: e.wait_ge(s, v))
        lst.append(lambda e, fn=fn, sem=sem: fn(e).then_inc(sem, 1))
        tok = (sem, val, eng)
        self._commit(tok, reads, writes)
        return tok

    def dma(self, q, out, in_, reads=(), writes=()):
        waits = self._deps(q, reads, writes)
        j = self.drr[q]
        self.drr[q] = (j + 1) % len(self.dsems[q])
        sem = self.dsems[q][j]
        if self.dval[q][j] > 0:
            self._need(q, (sem, self.dval[q][j], None), waits)
        self.dval[q][j] += 16
        val = self.dval[q][j]
        lst = self.lists[q]
        for (s, v) in waits:
            lst.append(lambda e, s=s, v=v: e.wait_ge(s, v))
        lst.append(lambda e, out=out, in_=in_, sem=sem: e.dma_start(out=out, in_=in_).then_inc(sem, 16))
        tok = (sem, val, None)
        self._commit(tok, reads, writes)
        return tok

    def collective_on_pool(self, issue_fn, cc_sem, markers, post_fn):
        all_toks = []
        for q in self.dsems:
            for j, sem in enumerate(self.dsems[q]):
                if self.dval[q][j] > 0:
                    all_toks.append((sem, self.dval[q][j], None))
        for eng in self.ENG:
            waits = []
            for t in all_toks:
                self._need(eng, t, waits)
            for (s, v) in waits:
                self.lists[eng].append(lambda e, s=s, v=v: e.wait_ge(s, v))
        mk = [self.op(eng, fn, writes=[("marker", eng)]) for eng, fn in markers.items()]
        waits = []
        for t in mk:
            self._need("pool", t, waits)
        lst = self.lists["pool"]
        for (s, v) in waits:
            lst.append(lambda e, s=s, v=v: e.wait_ge(s, v))
        lst.append(lambda e: issue_fn(e).then_inc(cc_sem))
        lst.append(lambda e: e.wait_ge(cc_sem, 1))
        after = self.op("pool", post_fn, writes=[("marker", "pool")])
        self.wait_tok("sp", after)
        return after

    def fence(self):
        toks = [(self.cur[d][0], self.cur[d][1], d) for d in ("pe", "act", "dve") if self.cur[d][1] > 0]
        for eng in self.ENG:
            waits = []
            for t in toks:
                if t[2] != eng:
                    self._need(eng, t, waits)
            for (s, v) in waits:
                self.lists[eng].append(lambda e, s=s, v=v: e.wait_ge(s, v))

    def wait_tok(self, eng, tok):
        waits = []
        self._need(eng, tok, waits)
        for (s, v) in waits:
            self.lists[eng].append(lambda e, s=s, v=v: e.wait_ge(s, v))

    def emit(self):
        nc = self.nc
        L = self.lists
        with nc.Block() as block:
            @block.tensor
            def _(e):
                for th in L["pe"]:
                    th(e)

            @block.scalar
            def _(e):
                for th in L["act"]:
                    th(e)

            @block.vector
            def _(e):
                for th in L["dve"]:
                    th(e)

            @block.gpsimd
            def _(e):
                for th in L["pool"]:
                    th(e)

            @block.sync
            def _(e):
                for th in L["sp"]:
                    th(e)


def build_program(dbg=None):
    nc = bass.Bass("TRN2", target_bir_lowering=False)

    def din(name, shape, dt=F32):
        return nc.dram_tensor(name, shape, dt, kind="ExternalInput").ap()

    xT_d = din("xT", [D, TT])
    pT_d = din("pT", [256, NT])
    vec_d = din("vecs", [128, NV])
    cmask_d = din("cmask", [64, 64])
    invc_d = din("invc", [128, 64])
    smask_d = din("smask", [128, 512])
    W = {
        "ffn1_w1": din("ffn1_w1", [D, DFF]), "ffn1_w3": din("ffn1_w3", [D, DFF]), "ffn1_w2": din("ffn1_w2", [DFF, D]),
        "ffn2_w1": din("ffn2_w1", [D, DFF]), "ffn2_w3": din("ffn2_w3", [D, DFF]), "ffn2_w2": din("ffn2_w2", [DFF, D]),
        "w_in": din("w_in", [D, 6656]), "w_branch_a": din("w_branch_a", [D, D]), "w_branch_b": din("w_branch_b", [512, D]),
        "pool_w": din("pool_w", [512, 128]), "w_out": din("w_out", [D, D]),
        "ple_w_gate": din("ple_w_gate", [D, D]), "ple_w_proj": din("ple_w_proj", [256, D]),
    }
    outT_d = nc.dram_tensor("outT", [D, NT], F32, kind="ExternalOutput").ap()
    if CROSS_CORE:
        st_loc = nc.dram_tensor("st_loc", [128, 8 * 128 + 8], F32, kind="Internal").ap()
        st_all = nc.dram_tensor("st_all", [4 * 128, 8 * 128 + 8], F32, kind="Internal").ap()
        sel_d = din("sel", [128, 4])
        NIT = 8 * (NT // 512)
        sp_kdT = nc.dram_tensor("sp_kdT", [NIT, 64, 8 * 128], BF16, kind="Internal").ap()
        sp_vt = nc.dram_tensor("sp_vt", [NIT, 64, 8 * 128], BF16, kind="Internal").ap()
        sp_kt = nc.dram_tensor("sp_kt", [NIT, 128, 512], BF16, kind="Internal").ap()
        sp_P = nc.dram_tensor("sp_P", [NIT, 128, 512], F32, kind="Internal").ap()
        sp_Pl = nc.dram_tensor("sp_Pl", [NIT, 128, 8], F32, kind="Internal").ap()

    with ExitStack() as st:
        def sb(name, shape, dt):
            return st.enter_context(nc.sbuf_tensor(name, shape, dt))

        xT = sb("xTs", [128, 8, TT], F32)
        hb = sb("hb", [128, 8, TT], BF16)
        wbuf = sb("wbuf", [128, NSLOT * SLOTW // 2], F32)
        R = sb("R", [128, RW], F32)
        vec = sb("vec", [128, NV], F32)
        lbv = sb("lbv", [128, 48], F32)
        ident = sb("identb", [128, 128], BF16)
        ones = sb("ones", [128, 128], BF16)
        cmask = sb("cmasks", [64, 64], F32)
        invc = sb("invcs", [128, 64], F32)
        Sst = sb("Sst", [128, 8, 128], F32)
        Sbf = sb("Sbf", [128, 4, 128], BF16)
        Ptot = sb("Ptot", [128, 8], F32)
        mark = sb("mark", [128, 2], F32)
        mark2 = sb("mark2", [128, 2], F32)
        pbank = [st.enter_context(nc.psum_tensor(f"pb{i}", [128, 512], F32)) for i in range(8)]

        S = Sched(nc, st)
        state = {"bank": 0, "slot": 0, "tmp": 0}

        bank_groups = {"A": [0, 1, 2, 3], "C": [4], "A0": [4]}
        grp_pos = {"A": 0, "C": 0, "A0": 0}
        cur_grp = [None]

        def bank():
            g = cur_grp[0]
            if g is None:
                i = state["bank"]
                state["bank"] = (i + 1) % 7
            else:
                lst = bank_groups[g]
                i = lst[grp_pos[g] % len(lst)]
                grp_pos[g] += 1
            return pbank[i], ("pb", i)

        def hold_bank():
            return pbank[7], ("pb", 7)

        class Arena:
            def __init__(self):
                self.off = 0

            def f32(self, words):
                a = self.off
                self.off += words
                assert self.off <= RW, self.off
                return R[:, a:a + words]

            def bf16(self, elems):
                words = (elems + 1) // 2
                a = self.off
                self.off += words
                assert self.off <= RW, self.off
                return R[:, a:a + words].bitcast(BF16)

        SLOT32 = SLOTW // 2

        def wload(dram_ap, kc, ncols):
            i = state["slot"]
            state["slot"] = (i + 1) % NSLOT
            n = kc * ncols
            assert n <= SLOT32, n
            raw = wbuf[:, i * SLOT32:i * SLOT32 + n]
            S.dma("sp", raw.rearrange("p (k n) -> p k n", k=kc), dram_ap, writes=[("w", i)])
            half = wbuf[:, i * SLOT32:i * SLOT32 + (n + 1) // 2].bitcast(BF16)[:, 0:n]
            S.op("pool", lambda e, half=half, raw=raw: e.tensor_copy(out=half, in_=raw), reads=[("w", i)], writes=[("w", i)])
            return half.rearrange("p (k n) -> p k n", k=kc), ("w", i)

        def wtile(name, r0, nk, c0, ncols):
            ap = W[name].rearrange("(k p) n -> p k n", p=128)[:, r0:r0 + nk, c0:c0 + ncols]
            return wload(ap, nk, ncols)

        for (off_, n_) in [(NT, HALO)] + BLK:
            b_ = 4 if off_ >= NT else off_ // 512
            S.dma("sp", xT[:, :, off_:off_ + n_], xT_d.rearrange("(c p) t -> p c t", p=128)[:, :, off_:off_ + n_],
                  writes=[("x", c, b_) for c in range(8)])
        S.dma("sp", vec[:], vec_d, writes=["vec"])
        S.dma("sp", cmask[:], cmask_d, writes=["cmask"])
        S.dma("sp", invc[:], invc_d, writes=["invc"])
        S.op("pool", lambda e: e.memset(ident[:], 0.0), writes=["ident"])
        S.op("pool", lambda e: e.affine_select(out=ident[:], in_=ident[:], pattern=[[-1, 128]], compare_op=ALU.not_equal, fill=1.0,
                                               base=0, channel_multiplier=1), reads=["ident"], writes=["ident"])
        S.op("dve", lambda e: e.memset(ones[:], 1.0), writes=["ones"])
        S.op("dve", lambda e: e.tensor_tensor(out=lbv[:, 0:8], in0=vec[:, V_LB0:V_LB0 + 8], in1=vec[:, V_LB1:V_LB1 + 8], op=ALU.subtract),
             reads=["vec"], writes=["lbv"])
        S.op("act", lambda e: e.activation(out=lbv[:, 0:8], in_=lbv[:, 0:8], func=AF.Sigmoid), reads=["lbv"], writes=["lbv"])
        S.op("dve", lambda e: e.tensor_scalar(out=lbv[:, 8:16], in0=lbv[:, 0:8], scalar1=-1.0, scalar2=1.0, op0=ALU.mult, op1=ALU.add),
             reads=["lbv"], writes=["lbv"])

        S.op("dve", lambda e: e.tensor_scalar(out=lbv[:, 16:24], in0=lbv[:, 8:16], scalar1=0.5, scalar2=None, op0=ALU.mult), reads=["lbv"], writes=["lbv"])
        S.op("dve", lambda e: e.tensor_tensor(out=lbv[:, 24:32], in0=lbv[:, 0:8], in1=lbv[:, 16:24], op=ALU.add), reads=["lbv"], writes=["lbv"])
        S.op("dve", lambda e: e.tensor_scalar(out=lbv[:, 32:40], in0=lbv[:, 16:24], scalar1=-1.0, scalar2=None, op0=ALU.mult), reads=["lbv"], writes=["lbv"])
        S.op("dve", lambda e: e.tensor_scalar(out=lbv[:, 40:48], in0=lbv[:, 24:32], scalar1=-1.0, scalar2=1.0, op0=ALU.mult, op1=ALU.add), reads=["lbv"], writes=["lbv"])

        def blk_idx(off):
            return 4 if off >= NT else off // 512

        def rstd_from_psum(ps, psk, n, dst, dstk, inv_d):
            S.op("dve", lambda e: e.tensor_scalar(out=dst[:, 0:n], in0=ps[:, 0:n], scalar1=inv_d, scalar2=EPS, op0=ALU.mult, op1=ALU.add),
                 reads=[psk], writes=[dstk])
            S.op("act", lambda e: e.activation(out=dst[:, 0:n], in_=dst[:, 0:n], func=AF.Ln), reads=[dstk], writes=[dstk])
            S.op("act", lambda e: e.activation(out=dst[:, 0:n], in_=dst[:, 0:n], func=AF.Exp, scale=-0.5), reads=[dstk], writes=[dstk])

        def norm_to_hb(blocks, gcol, sq, rstd):
            for (off, n) in blocks:
                b = blk_idx(off)
                ps, psk = bank()
                nsq = sq.shape[1]
                for c in range(8):
                    S.op("act", lambda e, c=c, off=off, n=n: e.activation(out=sq[:, c % nsq, 0:n], in_=xT[:, c, off:off + n], func=AF.Square),
                         reads=[("x", c, b)], writes=[("sq", c % nsq)])
                for c in range(8):
                    S.op("pe", lambda e, c=c, n=n, ps=ps: e.matmul(ps[:, 0:n], lhsT=ones[:], rhs=sq[:, c % nsq, 0:n], start=(c == 0), stop=(c == 7)),
                         reads=["ones", ("sq", c % nsq)], writes=[psk])
                rstd_from_psum(ps, psk, n, rstd, "rstd", 1.0 / D)
                for c in range(8):
                    S.op("dve", lambda e, c=c, off=off, n=n: e.scalar_tensor_tensor(
                        out=hb[:, c, off:off + n], in0=xT[:, c, off:off + n], scalar=vec[:, gcol + c:gcol + c + 1], in1=rstd[:, 0:n],
                        op0=ALU.mult, op1=ALU.mult), reads=[("x", c, b), "vec", "rstd"], writes=[("hb", c, b)])

        def proj(wt, wk, nk, col0, rhs_fn, rhs_keys, n, M=128):
            ps, psk = bank()
            segs = list(zip(wt, wk, nk)) if isinstance(wt, list) else [(wt, wk, nk)]
            tot = sum(sg_[2] for sg_ in segs)

            def f(e):
                kk = 0
                for (t_, _k, n_) in segs:
                    for k in range(n_):
                        i = e.matmul(ps[0:M, 0:n], lhsT=t_[:, k, col0:col0 + M], rhs=rhs_fn(kk), start=(kk == 0), stop=(kk == tot - 1))
                        kk += 1
                return i
            S.op("pe", f, reads=[sg_[1] for sg_ in segs] + rhs_keys, writes=[psk])
            return ps, psk

        def hb_rhs(off, n):
            return lambda k: hb[:, k, off:off + n]

        def hb_keys(off):
            b = blk_idx(off)
            return [("hb", c, b) for c in range(8)]

        def ffn(w1n, w3n, w2n, gcol, blocks):
            ar = Arena()
            aT = ar.bf16(11 * TT).rearrange("p (j t) -> p j t", j=11)
            sq = ar.bf16(8 * 512).rearrange("p (j t) -> p j t", j=8)
            rstd = ar.f32(512)
            stmp = ar.f32(2 * 512).rearrange("p (j t) -> p j t", j=2)
            norm_to_hb(blocks, gcol, sq, rstd)
            for half in range(2):
                for j in range(11):
                    n = half * 11 + j
                    w1t, k1 = wtile(w1n, 0, 8, n * 128, 128)
                    w3t, k3 = wtile(w3n, 0, 8, n * 128, 128)
                    for (off, nn) in blocks:
                        b = blk_idx(off)
                        ps1, pk1 = proj(w1t, k1, 8, 0, hb_rhs(off, nn), hb_keys(off), nn)
                        ps3, pk3 = proj(w3t, k3, 8, 0, hb_rhs(off, nn), hb_keys(off), nn)
                        ti = state["tmp"] = (state["tmp"] + 1) % 2
                        S.op("act", lambda e, ps1=ps1, nn=nn, ti=ti: e.activation(out=stmp[:, ti, 0:nn], in_=ps1[:, 0:nn], func=AF.Silu),
                             reads=[pk1], writes=[("stmp", ti)])
                        S.op("dve", lambda e, ps3=ps3, nn=nn, ti=ti, j=j, off=off: e.tensor_tensor(
                            out=aT[:, j, off:off + nn], in0=ps3[:, 0:nn], in1=stmp[:, ti, 0:nn], op=ALU.mult),
                            reads=[pk3, ("stmp", ti)], writes=[("aT", j, b)])
                for m in range(8):
                    w2a, k2a = wtile(w2n, half * 11, 6, m * 128, 128)
                    w2b, k2b = wtile(w2n, half * 11 + 6, 5, m * 128, 128)
                    for (off, nn) in blocks:
                        b = blk_idx(off)
                        ps, pk = proj([w2a, w2b], [k2a, k2b], [6, 5], 0, lambda k, off=off, nn=nn: aT[:, k, off:off + nn], [("aT", j, b) for j in range(11)], nn)
                        S.op("dve", lambda e, ps=ps, nn=nn, m=m, off=off: e.scalar_tensor_tensor(
                            out=xT[:, m, off:off + nn], in0=ps[:, 0:nn], scalar=0.5, in1=xT[:, m, off:off + nn], op0=ALU.mult, op1=ALU.add),
                            reads=[pk, ("x", m, b)], writes=[("x", m, b)])

        def finish_with_x():
            toks = []
            for (off, nn) in BLK:
                b = off // 512
                toks.append(S.dma("sp", outT_d.rearrange("(c p) t -> p c t", p=128)[:, :, off:off + 512], xT[:, :, off:off + 512],
                                  reads=[("x", c, b) for c in range(8)]))
            for t in toks:
                S.wait_tok("sp", t)
            S.emit()
            return nc

        ffn("ffn1_w1", "ffn1_w3", "ffn1_w2", V_FFN1, BLKH)
        S.fence()
        if dbg == "ffn1":
            return finish_with_x()

        ar = Arena()
        ob = ar.bf16(8 * NT).rearrange("p (h t) -> p h t", h=8)
        mixed = ar.bf16(4 * NT).rearrange("p (g t) -> p g t", g=4)
        work0 = ar.off

        HT = 512
        NCH = HT // 64
        NB = HT // 512
        NPART = NT // HT

        def run_interleaved(gens):
            gens = [g for g in gens if g is not None and g[1] is not None]
            while gens:
                for g in list(gens):
                    cur_grp[0] = g[0]
                    try:
                        next(g[1])
                    except StopIteration:
                        gens.remove(g)
            cur_grp[0] = None

        def hg_alloc(state_only, ar0):
            ar.off = ar0
            B = {}
            B["fbuf"] = ar.f32(HT)
            B["Pbuf"] = ar.f32(HT)
            B["rP"] = ar.f32(HT)
            B["tmpA"] = ar.f32(512)
            B["kd"] = ar.bf16(HT)
            B["kt"] = ar.bf16(HT)
            B["sets"] = []
            for i in range(2):
                d = {"kdT": ar.bf16(NCH * 128).rearrange("p (c k) -> p c k", c=NCH),
                     "vt": ar.bf16(NCH * 128).rearrange("p (c k) -> p c k", c=NCH),
                     "Pl": ar.f32(NCH)}
                if not state_only:
                    d["qt"] = ar.bf16(HT)
                    d["At"] = ar.bf16(NCH * 64).rearrange("p (c k) -> p c k", c=NCH)
                B["sets"].append(d)
            if not state_only:
                B["oh"] = [ar.f32(HT) for _ in range(2)]
                B["gs"] = [ar.bf16(HT) for _ in range(3)]
                B["rst"] = ar.f32(512)
                B["sqh"] = ar.bf16(512)
            return B

        def hg_A2(h, hh, wts, B, si, gi, item):
            t0 = hh * HT
            wq, kq, wf, kf, wi, ki, wog, kog = wts
            Pbuf, tmpA, kt = B["Pbuf"], B["tmpA"], B["kt"]
            D_ = B["sets"][si]
            kdT, vt, Pl, qt, At, gs = D_["kdT"], D_["vt"], D_["Pl"], D_["qt"], D_["At"], B["gs"][gi]
            qs = tmpA
            off = t0
            S.dma("sp", Pbuf[:, :], sp_P[item], writes=["P"])
            S.dma("sp", kt[:, :], sp_kt[item], writes=["kt"])
            S.dma("sp", kdT[0:64, :, :], sp_kdT[item].rearrange("p (c k) -> p c k", c=NCH), writes=[("kdT", si)])
            S.dma("sp", vt[0:64, :, :], sp_vt[item].rearrange("p (c k) -> p c k", c=NCH), writes=[("vt", si)])
            S.dma("sp", Pl[:, :], sp_Pl[item], writes=[("Pl", si)])
            yield
            psq, pkq = proj(wq, kq, 8, 0, hb_rhs(off, 512), hb_keys(off), 512)
            yield
            psg, pkg = proj(wog, kog, 8, 0, hb_rhs(off, 512), hb_keys(off), 512)
            yield
            S.op("act", lambda e: e.activation(out=qs[:, :], in_=psq[:, :], func=AF.Silu), reads=[pkq], writes=["tmpA"])
            yield
            S.op("act", lambda e: e.activation(out=gs[:, :], in_=psg[:, :], func=AF.Silu), reads=[pkg], writes=[("gs", gi)])
            yield
            S.op("dve", lambda e: e.tensor_tensor(out=qt[:, :], in0=qs[:, :], in1=Pbuf[:, :], op=ALU.mult), reads=["tmpA", "P"], writes=[("qt", si)])
            yield
            ps, pk = bank()

            def fs(e, ps=ps):
                for j in range(8):
                    i = e.matmul(ps[0:64, j * 64:(j + 1) * 64], lhsT=kt[:, j * 64:(j + 1) * 64], rhs=qt[:, j * 64:(j + 1) * 64], start=True, stop=True)
                return i
            S.op("pe", fs, reads=["kt", ("qt", si)], writes=[pk])
            yield
            S.op("dve", lambda e, ps=ps: e.tensor_tensor(out=At[0:64, :, :], in0=ps[0:64, :].rearrange("p (c t) -> p c t", c=8),
                                                         in1=cmask[:, None, :].to_broadcast([64, 8, 64]), op=ALU.mult),
                 reads=[pk, "cmask"], writes=[("At", si)])
            yield

        def hg_A(h, hh, wts, state_only, B, si, gi=0, item=0):
            t0 = hh * HT
            wq, kq, wf, kf, wi, ki, wog, kog = wts
            fbuf, Pbuf, rP, tmpA, kd = B["fbuf"], B["Pbuf"], B["rP"], B["tmpA"], B["kd"]
            gbuf = rP
            qs = tmpA
            D_ = B["sets"][si]
            kdT, vt, Pl = D_["kdT"], D_["vt"], D_["Pl"]
            off = t0
            psf, pkf = proj(wf, kf, 8, 0, hb_rhs(off, 512), hb_keys(off), 512)
            yield
            if not state_only:
                psq, pkq = proj(wq, kq, 8, 0, hb_rhs(off, 512), hb_keys(off), 512)
                yield
                psg, pkg = proj(wog, kog, 8, 0, hb_rhs(off, 512), hb_keys(off), 512)
                yield
            S.op("act", lambda e: e.activation(out=tmpA[:, :], in_=psf[:, :], func=AF.Tanh, scale=0.5), reads=[pkf], writes=["tmpA"])
            yield
            S.op("dve", lambda e: e.tensor_scalar(out=fbuf[:, :], in0=tmpA[:, :], scalar1=lbv[:, 16 + h:17 + h], scalar2=lbv[:, 24 + h:25 + h],
                                                  op0=ALU.mult, op1=ALU.add), reads=["tmpA", "lbv"], writes=["f"])
            yield
            if not state_only:
                qt, At, gs, kt = D_["qt"], D_["At"], B["gs"][gi], B["kt"]
                S.op("act", lambda e: e.activation(out=qs[:, :], in_=psq[:, :], func=AF.Silu), reads=[pkq, "f"], writes=["tmpA"])
                yield
                S.op("act", lambda e: e.activation(out=gs[:, :], in_=psg[:, :], func=AF.Silu), reads=[pkg], writes=[("gs", gi)])
                yield
            S.op("dve", lambda e: e.tensor_tensor(out=gbuf[:, :], in0=fbuf[:, :], in1=smask[:, 0:HT], op=ALU.mult), reads=["f", "smask"], writes=["rP"])
            yield
            S.op("dve", lambda e: e.tensor_tensor_scan(out=Pbuf[:, :], data0=fbuf[:, :], data1=gbuf[:, :], initial=1.0, op0=ALU.mult, op1=ALU.max),
                 reads=["f", "rP"], writes=["P"])
            yield
            S.op("act", lambda e: e.activation(out=rP[:, :], in_=Pbuf[:, :], func=AF.Ln), reads=["P"], writes=["rP"])
            yield
            S.op("act", lambda e: e.activation(out=rP[:, :], in_=rP[:, :], func=AF.Exp, scale=-1.0), reads=["rP"], writes=["rP"])
            yield
            S.op("dve", lambda e: e.tensor_copy(out=Pl[:, :], in_=Pbuf[:, :].rearrange("p (c t) -> p c t", c=NCH)[:, :, 63]), reads=["P"], writes=[("Pl", si)])
            yield
            if not state_only:
                S.op("dve", lambda e: e.tensor_tensor(out=qt[:, :], in0=qs[:, :], in1=Pbuf[:, :], op=ALU.mult), reads=["tmpA", "P"], writes=[("qt", si)])
                yield
            S.op("dve", lambda e: e.tensor_scalar(out=fbuf[:, :], in0=fbuf[:, :], scalar1=-1.0, scalar2=1.0, op0=ALU.mult, op1=ALU.add),
                 reads=["f", "P"], writes=["f"])
            yield
            S.op("dve", lambda e: e.tensor_tensor(out=rP[:, :], in0=rP[:, :], in1=fbuf[:, :], op=ALU.mult), reads=["rP", "f"], writes=["rP"])
            yield
            if not state_only or CROSS_CORE:
                kt = B["kt"]
                S.op("act", lambda e: e.activation(out=kt[:, :], in_=rP[:, :], func=AF.Copy), reads=["rP"], writes=["kt"])
                yield
            if state_only and CROSS_CORE:
                S.dma("sp", sp_kt[item], kt[:, :], reads=["kt"])
                S.dma("sp", sp_P[item], Pbuf[:, :], reads=["P"])
                S.dma("sp", sp_Pl[item], Pl[:, :], reads=[("Pl", si)])
                yield
            S.op("dve", lambda e: e.tensor_tensor(out=kd[:, :].rearrange("p (c t) -> p c t", c=NCH), in0=rP[:, :].rearrange("p (c t) -> p c t", c=NCH),
                                                   in1=Pbuf[:, :].rearrange("p (c t) -> p c t", c=NCH)[:, :, 63:64].to_broadcast([128, NCH, 64]), op=ALU.mult),
                 reads=["rP", "P"], writes=["kd"])
            yield
            for c4 in range(NCH // 4):
                ps, pk = bank()

                def fv(e, c4=c4, ps=ps):
                    for j in range(4):
                        c = c4 * 4 + j
                        o = t0 + c * 64
                        for k in range(8):
                            i = e.matmul(ps[0:64, j * 128:(j + 1) * 128], lhsT=hb[:, k, o:o + 64], rhs=wi[:, k, :], start=(k == 0), stop=(k == 7))
                    return i
                S.op("pe", fv, reads=[ki] + hb_keys(t0 + c4 * 256), writes=[pk])
                yield
                S.op("act", lambda e, c4=c4, ps=ps: e.activation(out=vt[0:64, c4 * 4:(c4 + 1) * 4, :], in_=ps[0:64, :].rearrange("p (c k) -> p c k", c=4), func=AF.Copy),
                     reads=[pk], writes=[("vt", si)])
                yield
            ps, pk = bank()
            psb = ps[:].bitcast(BF16)

            def ft(e, psb=psb):
                for j in range(8):
                    i = e.transpose(out=psb[0:64, j * 128:(j + 1) * 128], in_=kd[:, j * 64:(j + 1) * 64], identity=ident[:])
                return i
            S.op("pe", ft, reads=["kd", "ident"], writes=[pk])
            yield
            S.op("act", lambda e, psb=psb: e.activation(out=kdT[0:64, :, :], in_=psb[0:64, :].rearrange("p (c k) -> p c k", c=8), func=AF.Copy),
                 reads=[pk], writes=[("kdT", si)])
            yield
            if state_only and CROSS_CORE:
                S.dma("sp", sp_kdT[item].rearrange("p (c k) -> p c k", c=NCH), kdT[0:64, :, :], reads=[("kdT", si)])
                S.dma("sp", sp_vt[item].rearrange("p (c k) -> p c k", c=NCH), vt[0:64, :, :], reads=[("vt", si)])
                yield
            if not state_only:
                ps, pk = bank()

                def fs(e, ps=ps):
                    for j in range(8):
                        i = e.matmul(ps[0:64, j * 64:(j + 1) * 64], lhsT=kt[:, j * 64:(j + 1) * 64], rhs=qt[:, j * 64:(j + 1) * 64], start=True, stop=True)
                    return i
                S.op("pe", fs, reads=["kt", ("qt", si)], writes=[pk])
                yield
                S.op("dve", lambda e, ps=ps: e.tensor_tensor(out=At[0:64, :, :], in0=ps[0:64, :].rearrange("p (c t) -> p c t", c=8),
                                                             in1=cmask[:, None, :].to_broadcast([64, 8, 64]), op=ALU.mult),
                     reads=[pk, "cmask"], writes=[("At", si)])
                yield

        def hg_B(h, hh, state_only, B, si, oi=0, Pl_ap=None, Pl_key=None):
            t0 = hh * HT
            D_ = B["sets"][si]
            kdT, vt, Pl = D_["kdT"], D_["vt"], D_["Pl"]
            plk = ("Pl", si)
            if Pl_ap is not None:
                Pl, plk = Pl_ap, Pl_key
            if not state_only:
                qt, At = D_["qt"], D_["At"]
                oh = B["oh"][oi]
                if hh == 0:
                    S.op("act", lambda e: e.activation(out=Sbf[:, 0, :], in_=Sst[:, h, :], func=AF.Copy), reads=[("S", h)], writes=[("Sbf", 0)])
                    yield
            pso = None
            psS_of = []
            for c4 in range(NCH // 4):
                psS, pkS = pbank[5 + c4], ("pb", 5 + c4)

                def fS(e, c4=c4, psS=psS):
                    for j in range(4):
                        c = c4 * 4 + j
                        i = e.matmul(psS[:, j * 128:(j + 1) * 128], lhsT=kdT[0:64, c, :], rhs=vt[0:64, c, :], start=True, stop=True)
                    return i
                S.op("pe", fS, reads=[("kdT", si), ("vt", si)], writes=[pkS])
                yield
                for j in range(4):
                    psS_of.append((psS[:, j * 128:(j + 1) * 128], pkS))
            for c in range(NCH):
                psS_ap, pkS = psS_of[c]
                if not state_only:
                    if c % 8 == 0:
                        pso, pko = hold_bank()
                    sb_i = c % 4

                    def fo(e, c=c, pso=pso, sb_i=sb_i):
                        e.matmul(pso[:, (c % 8) * 64:(c % 8 + 1) * 64], lhsT=vt[0:64, c, :], rhs=At[0:64, c, :], start=True, stop=False)
                        return e.matmul(pso[:, (c % 8) * 64:(c % 8 + 1) * 64], lhsT=Sbf[:, sb_i, :], rhs=qt[:, c * 64:(c + 1) * 64], start=False, stop=True)
                    S.op("pe", fo, reads=[("vt", si), ("At", si), ("Sbf", sb_i), ("qt", si)], writes=[pko])
                    yield
                    if c % 8 == 7:
                        S.op("act", lambda e, pso=pso: e.activation(out=oh[:, :], in_=pso[:, :], func=AF.Copy), reads=[pko], writes=[("oh", oi)])
                        yield
                S.op("dve", lambda e, c=c, psS_ap=psS_ap: e.scalar_tensor_tensor(out=Sst[:, h, :], in0=Sst[:, h, :], scalar=Pl[:, c:c + 1], in1=psS_ap,
                                                                               op0=ALU.mult, op1=ALU.add), reads=[pkS, ("S", h), plk], writes=[("S", h)])
                yield
                if not state_only:
                    nb = (c + 1) % 4
                    S.op("dve", lambda e, nb=nb: e.tensor_copy(out=Sbf[:, nb, :], in_=Sst[:, h, :]), reads=[("S", h)], writes=[("Sbf", nb)])
                    yield
            if state_only:
                S.op("dve", lambda e: e.tensor_tensor_scan(out=ptmp[:, 0:NCH], data0=Pl[:, :], data1=zer64[:, 0:NCH],
                                                           initial=Ptot[:, h:h + 1], op0=ALU.mult, op1=ALU.add),
                     reads=[plk, "zer64", ("Ptot", h), "ptmp"], writes=["ptmp"])
                yield
                S.op("dve", lambda e: e.tensor_copy(out=Ptot[:, h:h + 1], in_=ptmp[:, NCH - 1:NCH]), reads=["ptmp"], writes=[("Ptot", h)])
                yield

        def hg_C(h, hh, B, oi, gi):
            if True:
                off = hh * HT
                oh, gs, rst, sqh = B["oh"][oi], B["gs"][gi], B["rst"], B["sqh"]
                S.op("act", lambda e: e.activation(out=sqh[:, :], in_=oh[:, :], func=AF.Square), reads=[("oh", oi)], writes=["sqh"])
                yield
                ps, pk = bank()
                S.op("pe", lambda e, ps=ps: e.matmul(ps[:, :], lhsT=ones[:], rhs=sqh[:, :], start=True, stop=True), reads=["ones", "sqh"], writes=[pk])
                yield
                S.op("dve", lambda e, ps=ps: e.tensor_scalar(out=rst[:, :], in0=ps[:, :], scalar1=1.0 / 128, scalar2=EPS, op0=ALU.mult, op1=ALU.add), reads=[pk], writes=["rst"])
                yield
                S.op("act", lambda e: e.activation(out=rst[:, :], in_=rst[:, :], func=AF.Ln), reads=["rst"], writes=["rst"])
                yield
                S.op("act", lambda e: e.activation(out=rst[:, :], in_=rst[:, :], func=AF.Exp, scale=-0.5), reads=["rst"], writes=["rst"])
                yield
                S.op("dve", lambda e: e.scalar_tensor_tensor(out=oh[:, :], in0=oh[:, :], scalar=vec[:, V_ONORM + h:V_ONORM + h + 1], in1=rst[:, :],
                                                             op0=ALU.mult, op1=ALU.mult), reads=[("oh", oi), "vec", "rst"], writes=[("oh", oi)])
                yield
                S.op("dve", lambda e: e.tensor_tensor(out=ob[:, h, off:off + 512], in0=oh[:, :], in1=gs[:, :], op=ALU.mult),
                     reads=[("oh", oi), ("gs", gi)], writes=[("ob", h, off // 512)])
                yield

        def hgrn_prepass(ar0):
            ar.off = ar0
            fb = [ar.f32(HT) for _ in range(2)]
            omf = [ar.f32(HT) for _ in range(2)]
            Pb = [ar.f32(HT) for _ in range(2)]
            gbuf = ar.f32(HT)
            rP = ar.f32(HT)
            tmpA = ar.f32(512)
            kd = ar.bf16(HT)
            kt = ar.bf16(HT)
            Pls = [ar.f32(NCH) for _ in range(3)]
            Bd = {"sets": [{"kdT": ar.bf16(NCH * 128).rearrange("p (c k) -> p c k", c=NCH),
                            "vt": ar.bf16(NCH * 128).rearrange("p (c k) -> p c k", c=NCH), "Pl": None} for _ in range(2)]}
            items = [(h, hh) for h in range(8) for hh in range(NPART)]
            wts_of = {}

            def A0(i):
                h, hh = items[i]
                if i == 0:
                    wts_of[0] = head_weights(0, True)
                if hh == 1 and h + 1 < 8:
                    wts_of[h + 1] = head_weights(h + 1, True)
                wf, kf = wts_of[h][2], wts_of[h][3]
                j = i % 2
                Pl = Pls[i % 3]
                off = hh * HT
                psf, pkf = proj(wf, kf, 8, 0, hb_rhs(off, 512), hb_keys(off), 512)
                yield
                S.op("act", lambda e: e.activation(out=tmpA[:, :], in_=psf[:, :], func=AF.Tanh, scale=0.5), reads=[pkf], writes=["tmpA"])
                yield
                S.op("act", lambda e: e.activation(out=fb[j][:, :], in_=tmpA[:, :], func=AF.Identity, scale=lbv[:, 16 + h:17 + h], bias=lbv[:, 24 + h:25 + h]),
                     reads=["tmpA", "lbv"], writes=[("f", j)])
                yield
                S.op("act", lambda e: e.activation(out=omf[j][:, :], in_=tmpA[:, :], func=AF.Identity, scale=lbv[:, 32 + h:33 + h], bias=lbv[:, 40 + h:41 + h]),
                     reads=["tmpA", "lbv"], writes=[("omf", j)])
                yield
                S.op("dve", lambda e: e.tensor_tensor(out=gbuf[:, :], in0=fb[j][:, :], in1=smask[:, 0:HT], op=ALU.mult), reads=[("f", j), "smask"], writes=["g"])
                yield
                S.op("dve", lambda e: e.tensor_tensor_scan(out=Pb[j][:, :], data0=fb[j][:, :], data1=gbuf[:, :], initial=1.0, op0=ALU.mult, op1=ALU.max),
                     reads=[("f", j), "g"], writes=[("P", j)])
                yield
                S.op("dve", lambda e: e.tensor_copy(out=Pl[:, :], in_=Pb[j][:, :].rearrange("p (c t) -> p c t", c=NCH)[:, :, 63]), reads=[("P", j)], writes=[("Pl3", i % 3)])
                yield

            def A1(i):
                h, hh = items[i]
                wi, ki = wts_of[h][4], wts_of[h][5]
                j = i % 2
                si = i % 2
                Pl = Pls[i % 3]
                kdT, vt = Bd["sets"][si]["kdT"], Bd["sets"][si]["vt"]
                t0 = hh * HT
                S.op("dve", lambda e: e.reciprocal(out=rP[:, :], in_=Pb[j][:, :]), reads=[("P", j)], writes=["rP"])
                yield
                S.op("dve", lambda e: e.tensor_tensor(out=rP[:, :], in0=rP[:, :], in1=omf[j][:, :], op=ALU.mult), reads=["rP", ("omf", j)], writes=["rP"])
                yield
                S.op("act", lambda e: e.activation(out=kt[:, :], in_=rP[:, :], func=AF.Copy), reads=["rP"], writes=["kt"])
                yield
                S.dma("sp", sp_kt[i], kt[:, :], reads=["kt"])
                S.dma("sp", sp_P[i], Pb[j][:, :], reads=[("P", j)])
                S.dma("sp", sp_Pl[i], Pl[:, :], reads=[("Pl3", i % 3)])
                yield
                S.op("dve", lambda e: e.tensor_tensor(out=kd[:, :].rearrange("p (c t) -> p c t", c=NCH), in0=rP[:, :].rearrange("p (c t) -> p c t", c=NCH),
                                                       in1=Pb[j][:, :].rearrange("p (c t) -> p c t", c=NCH)[:, :, 63:64].to_broadcast([128, NCH, 64]), op=ALU.mult),
                     reads=["rP", ("P", j)], writes=["kd"])
                yield
                for c4 in range(NCH // 4):
                    ps, pk = bank()

                    def fv(e, c4=c4, ps=ps):
                        for jj in range(4):
                            c = c4 * 4 + jj
                            o = t0 + c * 64
                            for k in range(8):
                                ins = e.matmul(ps[0:64, jj * 128:(jj + 1) * 128], lhsT=hb[:, k, o:o + 64], rhs=wi[:, k, :], start=(k == 0), stop=(k == 7))
                        return ins
                    S.op("pe", fv, reads=[ki] + hb_keys(t0 + c4 * 256), writes=[pk])
                    yield
                    S.op("act", lambda e, c4=c4, ps=ps: e.activation(out=vt[0:64, c4 * 4:(c4 + 1) * 4, :], in_=ps[0:64, :].rearrange("p (c k) -> p c k", c=4), func=AF.Copy),
                         reads=[pk], writes=[("vt", si)])
                    yield
                ps, pk = bank()
                psb = ps[:].bitcast(BF16)

                def ft(e, psb=psb):
                    for jj in range(8):
                        ins = e.transpose(out=psb[0:64, jj * 128:(jj + 1) * 128], in_=kd[:, jj * 64:(jj + 1) * 64], identity=ident[:])
                    return ins
                S.op("pe", ft, reads=["kd", "ident"], writes=[pk])
                yield
                S.op("act", lambda e, psb=psb: e.activation(out=kdT[0:64, :, :], in_=psb[0:64, :].rearrange("p (c k) -> p c k", c=8), func=AF.Copy),
                     reads=[pk], writes=[("kdT", si)])
                yield
                S.dma("sp", sp_kdT[i].rearrange("p (c k) -> p c k", c=NCH), kdT[0:64, :, :], reads=[("kdT", si)])
                S.dma("sp", sp_vt[i].rearrange("p (c k) -> p c k", c=NCH), vt[0:64, :, :], reads=[("vt", si)])
                yield

            n = len(items)
            run_interleaved([("A0", A0(0))])
            run_interleaved([("A", A1(0)), ("A0", A0(1))])
            for i in range(n):
                h, hh = items[i]
                run_interleaved([("B", hg_B(h, hh, True, Bd, i % 2, 0, Pls[i % 3], ("Pl3", i % 3))),
                                 ("A", A1(i + 1) if i + 1 < n else None), ("A0", A0(i + 2) if i + 2 < n else None)])

        def hgrn_pass(state_only, ar0):
            B = hg_alloc(state_only, ar0)
            items = [(h, hh) for h in range(8) for hh in range(NPART)]
            wts_of = {}

            def genA(i):
                h, hh = items[i]
                if i == 0:
                    wts_of[0] = head_weights(0, state_only)
                if hh == 1 and h + 1 < 8:
                    wts_of[h + 1] = head_weights(h + 1, state_only)
                if CROSS_CORE and not state_only:
                    return hg_A2(h, hh, wts_of[h], B, i % 2, i % 3, i)
                return hg_A(h, hh, wts_of[h], state_only, B, i % 2, i % 3, i)

            def genC(i):
                if state_only or i < 0:
                    return None
                h, hh = items[i]
                return hg_C(h, hh, B, i % 2, i % 3)

            run_interleaved([("A", genA(0))])
            for i in range(len(items)):
                h, hh = items[i]
                run_interleaved([("B", hg_B(h, hh, state_only, B, i % 2, i % 2)), ("C", genC(i - 1)), ("A", genA(i + 1) if i + 1 < len(items) else None)])
            run_interleaved([("C", genC(len(items) - 1))])

        zer64 = ar.f32(64)
        S.op("dve", lambda e: e.memset(zer64[:, :], 0.0), writes=["zer64"])
        smask = ar.f32(512)
        ptmp = ar.f32(8)
        S.dma("sp", smask, smask_d, writes=["smask"])
        work1 = ar.off
        sq = ar.bf16(8 * 512).rearrange("p (j t) -> p j t", j=8)
        rstd = ar.f32(512)
        norm_to_hb(BLKH, V_MIX, sq, rstd)
        S.fence()

        def head_weights(h, state_only):
            if CROSS_CORE and not state_only:
                wq, kq = wtile("w_in", 0, 8, h * 128, 128)
                wog, kog = wtile("w_in", 0, 8, 3072 + h * 128, 128)
                return (wq, kq, None, None, None, None, wog, kog)
            wf, kf = wtile("w_in", 0, 8, 1024 + h * 128, 128)
            wi, ki = wtile("w_in", 0, 8, 2048 + h * 128, 128)
            if state_only:
                return (None, None, wf, kf, wi, ki, None, None)
            wq, kq = wtile("w_in", 0, 8, h * 128, 128)
            wog, kog = wtile("w_in", 0, 8, 3072 + h * 128, 128)
            return (wq, kq, wf, kf, wi, ki, wog, kog)

        if CROSS_CORE:
            S.op("dve", lambda e: e.memset(Sst[:], 0.0), writes=[("S", h) for h in range(8)])
            S.op("dve", lambda e: e.memset(Ptot[:], 1.0), writes=[("Ptot", h) for h in range(8)])
            hgrn_prepass(work1)
            if dbg == "p1a":
                S.fence()
                return finish_with_x()
            t_a = S.dma("sp", st_loc[:, 0:1024], Sst[:].rearrange("p h k -> p (h k)"), reads=[("S", h) for h in range(8)])
            t_b = S.dma("sp", st_loc[:, 1024:1032], Ptot[:], reads=[("Ptot", h) for h in range(8)])
            S.wait_tok("pool", t_a)
            S.wait_tok("pool", t_b)

        else:
            S.op("dve", lambda e: e.memset(Sst[:], 0.0), writes=[("S", h) for h in range(8)])

        def pool_weights():
            tiles = [wload(W["pool_w"].rearrange("(g p) n -> p g n", p=128), 4, 128)]
            for g in range(4):
                tiles.append(wtile("w_in", 0, 8, 4096 + g * 128, 128))
            return tiles

        def pool_branch(ar0, tiles):
            ar.off = ar0
            pr = ar.f32(TT)
            tA = ar.f32(TT)
            tB = ar.f32(TT)
            pl = ar.bf16(NT)
            t16 = ar.f32(16)
            pw, kpw = tiles[0]
            for g in range(4):
                wsz = 2 ** (g + 1)
                wp, kp = tiles[1 + g]
                for (off, nn) in BLKH:
                    dst = 0 if off >= NT else HALO + off
                    ps, pk = proj(wp, kp, 8, 0, hb_rhs(off, nn), hb_keys(off), nn)
                    S.op("act", lambda e, ps=ps, nn=nn, dst=dst: e.activation(out=pr[:, dst:dst + nn], in_=ps[:, 0:nn], func=AF.Copy), reads=[pk], writes=["pr"])
                src = pr
                bufs = [tA, tB]
                sh = 1
                bi = 0
                while sh < wsz:
                    dstb = bufs[bi]
                    lo = 2 * sh - 1
                    S.op("dve", lambda e, src=src, dstb=dstb, sh=sh, lo=lo: e.tensor_tensor(out=dstb[:, lo:TT], in0=src[:, lo:TT], in1=src[:, lo - sh:TT - sh], op=ALU.add),
                         reads=["pr", "tA", "tB"], writes=["tA" if bi == 0 else "tB"])
                    src = dstb
                    sh *= 2
                    bi ^= 1
                S.op("dve", lambda e, src=src, wsz=wsz: e.scalar_tensor_tensor(out=pl[:, :], in0=src[:, HALO:TT], scalar=1.0 / wsz, in1=pr[:, HALO:TT], op0=ALU.mult, op1=ALU.subtract),
                     reads=["pr", "tA", "tB"], writes=["pl"])
                S.op("dve", lambda e, src=src, g=g: e.tensor_tensor(out=t16[:, :], in0=src[:, HALO:2 * HALO], in1=invc[:, g * 16:(g + 1) * 16], op=ALU.mult),
                     reads=["tA", "tB", "invc"], writes=["t16"])
                S.op("dve", lambda e: e.tensor_tensor(out=pl[:, 0:16], in0=t16[:, :], in1=pr[:, HALO:2 * HALO], op=ALU.subtract),
                     reads=["t16", "pr", "pl"], writes=["pl"])
                for (off, nn) in BLK:
                    ps, pk = bank()
                    S.op("pe", lambda e, ps=ps, g=g, off=off: e.matmul(ps[:, :], lhsT=pw[:, g, :], rhs=pl[:, off:off + 512], start=True, stop=True),
                         reads=[kpw, "pl"], writes=[pk])
                    S.op("act", lambda e, ps=ps, g=g, off=off: e.activation(out=mixed[:, g, off:off + 512], in_=ps[:, :], func=AF.Copy, scale=vec[:, V_PSCALE + g:V_PSCALE + g + 1]),
                         reads=[pk, "vec"], writes=[("mixed", g, off // 512)])

        S.fence()
        ptiles = pool_weights()
        if CROSS_CORE:
            cc_tok = S.collective_on_pool(lambda e: e.collective_compute(
                "AllGather", ALU.bypass, replica_groups=[[0, 1, 2, 3], [4, 5, 6, 7]], ins=[st_loc.opt()], outs=[st_all.opt()]), S.free_sems.pop(),
                {"pe": lambda e: e.matmul(pbank[7][0:1, 0:1], lhsT=ones[:, 0:1], rhs=ones[:, 0:1], start=True, stop=True),
                 "act": lambda e: e.activation(out=mark[:, 0:1], in_=ones[:, 0:1], func=AF.Copy),
                 "dve": lambda e: e.memset(mark[:, 1:2], 0.0)},
                lambda e: e.memset(mark2[:, :], 0.0))
        pool_branch(work1, ptiles)
        S.fence()

        if CROSS_CORE:
            ar.off = work1
            gat = ar.f32(4 * 1032).rearrange("p (r w) -> p r w", r=4)
            sel = ar.f32(4)
            acc = ar.f32(1024)
            S.dma("sp", sel[:, :], sel_d, writes=["sel"])
            S.wait_tok("sp", cc_tok)
            S.op("dve", lambda e: e.memset(Sst[:], 0.0), reads=[("S", h) for h in range(8)], writes=[("S", h) for h in range(8)] + ["Sin"])
            Sin = Sst
            for j in range(3):
                gj = gat[:, j % 4, :]
                S.dma("sp", gj, st_all[j * 128:(j + 1) * 128, :], writes=[("gat", j % 4)])
                for h in range(8):
                    S.op("dve", lambda e, gj=gj, h=h: e.scalar_tensor_tensor(out=acc[:, h * 128:(h + 1) * 128], in0=Sin[:, h, :], scalar=gj[:, 1024 + h:1025 + h],
                                                                             in1=gj[:, h * 128:(h + 1) * 128], op0=ALU.mult, op1=ALU.add),
                         reads=[("gat", j % 4), "Sin"], writes=["acc"])
                S.op("dve", lambda e: e.tensor_tensor(out=acc[:, :], in0=acc[:, :], in1=Sin[:].rearrange("p h k -> p (h k)"), op=ALU.subtract),
                     reads=["acc", "Sin"], writes=["acc"])
                S.op("dve", lambda e, j=j: e.scalar_tensor_tensor(out=Sin[:].rearrange("p h k -> p (h k)"), in0=acc[:, :], scalar=sel[:, j:j + 1],
                                                                  in1=Sin[:].rearrange("p h k -> p (h k)"), op0=ALU.mult, op1=ALU.add),
                     reads=["acc", "Sin", "sel"], writes=["Sin"] + ([("S", h) for h in range(8)] if j == 2 else []))

        S.fence()
        if dbg == "cc":
            return finish_with_x()
        hgrn_pass(False, work1)

        S.fence()
        ar.off = work0
        yb = ar.bf16(8 * NT).rearrange("p (m t) -> p m t", m=8)
        sga = ar.f32(512)
        sgb = ar.f32(512)
        for m in range(8):
            wga, kga = wtile("w_in", 0, 8, 4608 + m * 128, 128)
            wa, ka = wtile("w_branch_a", 0, 8, m * 128, 128)
            for (off, nn) in BLK:
                b = off // 512
                pga, kpga = proj(wga, kga, 8, 0, hb_rhs(off, 512), hb_keys(off), 512)
                S.op("act", lambda e, pga=pga: e.activation(out=sga[:, :], in_=pga[:, :], func=AF.Tanh, scale=0.5), reads=[kpga], writes=["sga"])
                pya, kpya = proj(wa, ka, 8, 0, lambda k, off=off: ob[:, k, off:off + 512], [("ob", hh_, b) for hh_ in range(8)], 512)
                S.op("dve", lambda e, pya=pya, m=m, off=off: e.scalar_tensor_tensor(out=yb[:, m, off:off + 512], in0=sga[:, :], scalar=1.0, in1=pya[:, :], op0=ALU.add, op1=ALU.mult),
                     reads=["sga", kpya], writes=[("yb", m, b)])
            wgb, kgb = wtile("w_in", 0, 8, 5632 + m * 128, 128)
            wbb, kbb = wtile("w_branch_b", 0, 4, m * 128, 128)
            for (off, nn) in BLK:
                b = off // 512
                pgb, kpgb = proj(wgb, kgb, 8, 0, hb_rhs(off, 512), hb_keys(off), 512)
                S.op("act", lambda e, pgb=pgb: e.activation(out=sgb[:, :], in_=pgb[:, :], func=AF.Tanh, scale=0.5), reads=[kpgb], writes=["sgb"])
                pyb, kpyb = proj(wbb, kbb, 4, 0, lambda k, off=off: mixed[:, k, off:off + 512], [("mixed", g, b) for g in range(4)], 512)
                S.op("dve", lambda e, pyb=pyb: e.scalar_tensor_tensor(out=sgb[:, :], in0=sgb[:, :], scalar=1.0, in1=pyb[:, :], op0=ALU.add, op1=ALU.mult), reads=["sgb", kpyb], writes=["sgb"])
                S.op("dve", lambda e, m=m, off=off: e.tensor_tensor(out=yb[:, m, off:off + 512], in0=yb[:, m, off:off + 512], in1=sgb[:, :], op=ALU.add),
                     reads=["sgb", ("yb", m, b)], writes=[("yb", m, b)])
        for m in range(8):
            wo, ko = wtile("w_out", 0, 8, m * 128, 128)
            for (off, nn) in BLK:
                b = off // 512
                ps, pk = proj(wo, ko, 8, 0, lambda k, off=off: yb[:, k, off:off + 512], [("yb", k, b) for k in range(8)], 512)
                S.op("dve", lambda e, ps=ps, m=m, off=off: e.scalar_tensor_tensor(out=xT[:, m, off:off + 512], in0=ps[:, :], scalar=0.5, in1=xT[:, m, off:off + 512],
                                                                              op0=ALU.mult, op1=ALU.add),
                     reads=[pk, ("x", m, b)], writes=[("x", m, b)])

        S.fence()
        if dbg == "mix":
            return finish_with_x()
        ffn("ffn2_w1", "ffn2_w3", "ffn2_w2", V_FFN2, BLK)
        S.fence()
        if dbg == "ffn2":
            return finish_with_x()

        ar = Arena()
        pbf = ar.bf16(2 * NT).rearrange("p (c t) -> p c t", c=2)
        eraw = [ar.f32(8 * 512).rearrange("p (m t) -> p m t", m=8) for _ in range(2)]
        sq = ar.bf16(2 * 512).rearrange("p (j t) -> p j t", j=2)
        rstd = ar.f32(512)
        rse = [ar.f32(512) for _ in range(2)]
        sq8 = ar.bf16(8 * 512).rearrange("p (j t) -> p j t", j=8)
        sg = ar.f32(2 * 512).rearrange("p (j t) -> p j t", j=2)
        tt_ = ar.f32(2 * 512).rearrange("p (j t) -> p j t", j=2)
        ostg = ar.f32(8 * 512).rearrange("p (m t) -> p m t", m=8)
        pstage = ostg[:, :, :].rearrange("p m t -> p (m t)").rearrange("p (c t) -> p c t", c=2)
        S.dma("sp", pstage, pT_d.rearrange("(c p) t -> p c t", p=128), writes=["ostg"])
        for c in range(2):
            S.op("pool", lambda e, c=c: e.tensor_copy(out=pbf[:, c, :], in_=pstage[:, c, :]), reads=["ostg"], writes=["pbf"])
        norm_to_hb(BLK, V_PLE, sq8, rstd)
        out_toks = []
        def ple_e(pair):
            wpj0, kpj0 = wtile("ple_w_proj", 0, 2, 0, 512)
            wpj1, kpj1 = wtile("ple_w_proj", 0, 2, 512, 512)
            for bi, b in enumerate(pair):
                off = 512 * b
                pss, pkss = hold_bank()
                for m in range(8):
                    ps, pk = proj(wpj0 if m < 4 else wpj1, kpj0 if m < 4 else kpj1, 2, (m % 4) * 128, lambda k, off=off: pbf[:, k, off:off + 512], ["pbf"], 512)
                    S.op("act", lambda e, ps=ps, m=m, bi=bi: e.activation(out=eraw[bi][:, m, :], in_=ps[:, :], func=AF.Copy), reads=[pk], writes=[("eraw", bi, m)])
                    S.op("act", lambda e, ps=ps, m=m: e.activation(out=sq[:, m % 2, :], in_=ps[:, :], func=AF.Square), reads=[pk], writes=[("sq", m % 2)])
                    S.op("pe", lambda e, m=m, pss=pss: e.matmul(pss[:, :], lhsT=ones[:], rhs=sq[:, m % 2, :], start=(m == 0), stop=(m == 7)),
                         reads=["ones", ("sq", m % 2)], writes=[pkss])
                rstd_from_psum(pss, pkss, 512, rse[bi], ("rse", bi), 1.0 / D)

        def ple_g(pair):
            for m in range(8):
                wg, kg = wtile("ple_w_gate", 0, 8, m * 128, 128)
                for bi, b in enumerate(pair):
                    off = 512 * b
                    ps, pk = proj(wg, kg, 8, 0, hb_rhs(off, 512), hb_keys(off), 512)
                    S.op("act", lambda e, ps=ps, bi=bi: e.activation(out=sg[:, bi, :], in_=ps[:, :], func=AF.Sigmoid), reads=[pk], writes=[("sg", bi)])
                    S.op("dve", lambda e, m=m, bi=bi: e.scalar_tensor_tensor(out=tt_[:, bi, :], in0=eraw[bi][:, m, :], scalar=vec[:, V_POST + m:V_POST + m + 1], in1=rse[bi][:, :],
                                                                             op0=ALU.mult, op1=ALU.mult), reads=[("eraw", bi, m), "vec", ("rse", bi)], writes=[("tt", bi)])
                    S.op("dve", lambda e, bi=bi: e.tensor_tensor(out=tt_[:, bi, :], in0=tt_[:, bi, :], in1=sg[:, bi, :], op=ALU.mult),
                         reads=[("tt", bi), ("sg", bi)], writes=[("tt", bi)])
                    S.op("dve", lambda e, m=m, off=off, bi=bi: e.tensor_tensor(out=xT[:, m, off:off + 512], in0=xT[:, m, off:off + 512], in1=tt_[:, bi, :], op=ALU.add),
                         reads=[("tt", bi), ("x", m, b)], writes=[("x", m, b)])

        def ple_f(pair):
            for bi, b in enumerate(pair):
                off = 512 * b
                ps, psk = bank()
                for c in range(8):
                    S.op("act", lambda e, c=c, off=off: e.activation(out=sq8[:, c, :], in_=xT[:, c, off:off + 512], func=AF.Square),
                         reads=[("x", c, b)], writes=[("sq8", c)])
                for c in range(8):
                    S.op("pe", lambda e, c=c, ps=ps: e.matmul(ps[:, :], lhsT=ones[:], rhs=sq8[:, c, :], start=(c == 0), stop=(c == 7)),
                         reads=["ones", ("sq8", c)], writes=[psk])
                rstd_from_psum(ps, psk, 512, rstd, "rstd", 1.0 / D)
                for c in range(8):
                    S.op("dve", lambda e, c=c, off=off: e.scalar_tensor_tensor(out=ostg[:, c, :], in0=xT[:, c, off:off + 512], scalar=vec[:, V_FINAL + c:V_FINAL + c + 1], in1=rstd[:, :],
                                                                               op0=ALU.mult, op1=ALU.mult), reads=[("x", c, b), "vec", "rstd"], writes=["ostg"])
                out_toks.append(S.dma("sp", outT_d.rearrange("(c p) t -> p c t", p=128)[:, :, off:off + 512], ostg[:, :, :], reads=["ostg"]))

        ple_e([0, 1])
        ple_g([0, 1])
        ple_e([2, 3])
        ple_f([0, 1])
        ple_g([2, 3])
        ple_f([2, 3])
        for t in out_toks:
            S.wait_tok("sp", t)
        S.emit()
    return nc


_DBG = None


def _pm(v, n):
    return np.ascontiguousarray(np.asarray(v, np.float32).reshape(n, 128).T)


def kernel(x, p, ffn1_norm, ffn1_w1, ffn1_w3, ffn1_w2, mix_norm, w_in, hgrn_lb, hgrn_onorm,
           w_branch_a, pool_w, pool_scale, w_branch_b, w_out, ffn2_norm, ffn2_w1, ffn2_w3,
           ffn2_w2, ple_norm, ple_w_gate, ple_w_proj, ple_post_norm, final_norm):
    f = lambda a: np.ascontiguousarray(np.asarray(a, np.float32))
    x = f(x)
    p = f(p)
    vecs = np.concatenate([
        _pm(ffn1_norm[0], 8), _pm(mix_norm[0], 8), _pm(ffn2_norm[0], 8), _pm(ple_norm[0], 8), _pm(ple_post_norm[0], 8),
        _pm(final_norm, 8), _pm(hgrn_lb[0], 8), _pm(hgrn_lb[1], 8), _pm(hgrn_onorm[0], 8), _pm(pool_scale[0], 4)], axis=1)
    vecs = np.ascontiguousarray(vecs)
    assert vecs.shape == (128, NV)
    shared = {
        "vecs": vecs,
        "cmask": np.triu(np.ones((64, 64), np.float32)),
        "smask": np.ascontiguousarray(np.tile((np.arange(512) % 64 == 0).astype(np.float32)[None, :], (128, 1))),
        "ffn1_w1": f(ffn1_w1[0]), "ffn1_w3": f(ffn1_w3[0]), "ffn1_w2": f(ffn1_w2[0]),
        "ffn2_w1": f(ffn2_w1[0]), "ffn2_w3": f(ffn2_w3[0]), "ffn2_w2": f(ffn2_w2[0]),
        "w_in": f(w_in[0]), "w_branch_a": f(w_branch_a[0]), "w_branch_b": f(w_branch_b[0]),
        "pool_w": f(np.asarray(pool_w[0]).reshape(512, 128)), "w_out": f(w_out[0]),
        "ple_w_gate": f(ple_w_gate[0]), "ple_w_proj": f(ple_w_proj[0]),
    }
    in_maps = []
    for c in range(NCORES):
        b, j = divmod(c, 4)
        t0 = j * NT
        xT = np.zeros((D, TT), np.float32)
        xT[:, :NT] = x[b, t0:t0 + NT, :].T
        if j > 0:
            xT[:, NT:] = x[b, t0 - HALO:t0, :].T
        invc = np.zeros((128, 64), np.float32)
        for g, w in enumerate((2, 4, 8, 16)):
            pos = t0 + np.arange(16) + 1
            invc[:, g * 16:(g + 1) * 16] = (1.0 / np.minimum(pos, w)).astype(np.float32)[None, :]
        m = dict(shared)
        m["xT"] = xT
        m["pT"] = np.ascontiguousarray(p[0, b, t0:t0 + NT, :].T)
        m["invc"] = invc
        if CROSS_CORE:
            sel = np.zeros((128, 4), np.float32)
            for jj in range(4):
                if jj < j:
                    sel[:, jj] = 1.0
            m["sel"] = sel
        in_maps.append(m)
    nc = build_program(_DBG)
    res = run_bass_kernel_spmd(nc, in_maps, core_ids=list(range(NCORES)))
    out = np.empty((2, 8192, D), np.float32)
    for c in range(NCORES):
        b, j = divmod(c, 4)
        out[b, j * NT:(j + 1) * NT, :] = np.asarray(res.results[c]["outT"]).T
    return out
```
